# Optimizing a Trainium2 kernel written in Bass

```python
import jax, jax.numpy as jnp
from jax import lax
import numpy as np

D_MODEL = 1024
BATCH = 4
SEQ = 4096
DEPTH = 4

N_META = 16
BLOCK = 128
PAD = (-N_META) % BLOCK
EPS = 1e-6
NEG_INF = -1e30

MLA_HEADS = 4
MLA_NOPE = 64
MLA_ROPE = 32
MLA_V = 64
MLA_Q_RANK = 256
MLA_KV_RANK = 128
ROPE_THETA = 10000.0

RW_HEADS = 4
RW_HEAD = 64
RW_W = RW_HEADS * RW_HEAD
RW_DECAY_LORA = 64
RW_A_LORA = 64
RW_G_LORA = 128
RW_GN_EPS = 64e-5
RW_IN = 3 * RW_W + RW_DECAY_LORA + RW_A_LORA + RW_G_LORA
RW_SPLITS = [RW_W, 2 * RW_W, 3 * RW_W, 3 * RW_W + RW_DECAY_LORA, 3 * RW_W + RW_DECAY_LORA + RW_A_LORA]

SB_HEADS = 4
SB_HEAD = 64
SB_W = SB_HEADS * SB_HEAD

GLA_HEADS = 4
GLA_DK = 32
GLA_DV = 64
GLA_GATE_LORA = 16
GLA_TAU = 16.0
GLA_QK = GLA_HEADS * GLA_DK
GLA_W = GLA_HEADS * GLA_DV
GLA_IN = 2 * GLA_QK + GLA_W + GLA_GATE_LORA + GLA_W
GLA_SPLITS = [GLA_QK, 2 * GLA_QK, 2 * GLA_QK + GLA_W, 2 * GLA_QK + GLA_W + GLA_GATE_LORA]

N_BRANCH = 4
BRANCH_W = 256
IN_SIZES = [MLA_Q_RANK, MLA_KV_RANK, MLA_ROPE, RW_IN, 3 * SB_W, GLA_IN, N_BRANCH * D_MODEL]
IN_SPLITS = [int(s) for s in np.cumsum(IN_SIZES)[:-1]]
IN_TOTAL = int(sum(IN_SIZES))

D_FF = 2816
CONV_W = 3

kernel_name = "hybrid_gated_merge_mla_rwkv7_stickbreak_gla_convffn"


def rmsnorm(x, g):
    xf = x.astype(jnp.float32)
    y = xf * lax.rsqrt(jnp.mean(xf * xf, axis=-1, keepdims=True) + EPS)
    return (y * g).astype(x.dtype)


def shift(t):
    return jnp.pad(t[:, :-1], ((0, 0), (1, 0), (0, 0)))


def pad_front(t):
    return jnp.pad(t, ((0, 0), (PAD, 0)) + ((0, 0),) * (t.ndim - 2))


def rope(x, pos):
    half = x.shape[-1] // 2
    freqs = ROPE_THETA ** (-jnp.arange(half, dtype=jnp.float32) / half)
    ang = pos.astype(jnp.float32)[:, None] * freqs[None, :]
    cos = jnp.cos(ang)[None, :, None, :]
    sin = jnp.sin(ang)[None, :, None, :]
    x1, x2 = x[..., :half], x[..., half:]
    return jnp.concatenate([x1 * cos - x2 * sin, x1 * sin + x2 * cos], axis=-1).astype(x.dtype)


def sweep_blocks(weights_fn, q, k, v):
    B, Lp, H, dk = q.shape
    nb = Lp // BLOCK
    scale = dk ** -0.5
    k_pos = jnp.arange(Lp)[None, :]
    qb = q.reshape(B, nb, BLOCK, H, dk).swapaxes(0, 1)

    def one(args):
        q_blk, i = args
        z = jnp.einsum('bqhd,bkhd->bhqk', q_blk, k).astype(jnp.float32) * scale
        q_pos = (i * BLOCK + jnp.arange(BLOCK))[:, None]
        w = weights_fn(z, q_pos, k_pos)
        return jnp.einsum('bhqk,bkhd->bqhd', w.astype(v.dtype), v)

    out = lax.map(one, (qb, jnp.arange(nb)))
    return out.swapaxes(0, 1).reshape(B, Lp, H, v.shape[-1])


def softmax_weights(z, q_pos, k_pos):
    mask = (k_pos <= q_pos) & (k_pos >= PAD)
    return jax.nn.softmax(jnp.where(mask, z, NEG_INF), axis=-1)


def stick_breaking_weights(z, q_pos, k_pos):
    mask = (k_pos < q_pos) & (k_pos >= PAD)
    log_keep = jnp.where(mask, jax.nn.log_sigmoid(-z), 0.0)
    log_later = lax.cumsum(log_keep, axis=3, reverse=True) - log_keep
    return jnp.where(mask, jnp.exp(jax.nn.log_sigmoid(z) + log_later), 0.0)


def gla_chunked(q, k, v, log_a):
    B, Lp, H, dk = q.shape
    dv = v.shape[-1]
    nc = Lp // BLOCK

    def chunks(t):
        return t.astype(jnp.float32).reshape(B, nc, BLOCK, H, t.shape[-1]).transpose(1, 0, 3, 2, 4)

    causal = jnp.tril(jnp.ones((BLOCK, BLOCK), dtype=bool))[:, :, None]

    def step(S, inp):
        qc, kc, vc, gc = inp
        b = jnp.cumsum(gc, axis=2)
        inter = jnp.einsum('bhik,bhkv->bhiv', qc * jnp.exp(b), S)
        rel = jnp.exp(jnp.where(causal, b[:, :, :, None, :] - b[:, :, None, :, :], NEG_INF))
        scores = jnp.einsum('bhik,bhisk->bhis', qc, rel * kc[:, :, None, :, :])
        intra = jnp.einsum('bhis,bhsv->bhiv', scores, vc)
        b_end = b[:, :, -1:, :]
        S = S * jnp.exp(b_end[:, :, 0, :, None]) + jnp.einsum('bhsk,bhsv->bhkv', kc * jnp.exp(b_end - b), vc)
        return S, inter + intra

    S0 = jnp.zeros((B, H, dk, dv), jnp.float32)
    _, o = lax.scan(step, S0, (chunks(q), chunks(k), chunks(v), chunks(log_a)))
    return o.transpose(1, 0, 3, 2, 4).reshape(B, Lp, H, dv).astype(v.dtype)


def mla_branch(cq, ckv, kr, pos, q_norm, w_uq, kv_norm, w_ukv):
    B, L, _ = cq.shape
    q = (rmsnorm(cq, q_norm) @ w_uq).reshape(B, L, MLA_HEADS, MLA_NOPE + MLA_ROPE)
    kv = (rmsnorm(ckv, kv_norm) @ w_ukv).reshape(B, L, MLA_HEADS, MLA_NOPE + MLA_V)
    k_rope = jnp.broadcast_to(rope(kr[:, :, None, :], pos), (B, L, MLA_HEADS, MLA_ROPE))
    q = jnp.concatenate([q[..., :MLA_NOPE], rope(q[..., MLA_NOPE:], pos)], axis=-1)
    k = jnp.concatenate([kv[..., :MLA_NOPE], k_rope], axis=-1)
    v = kv[..., MLA_NOPE:]
    o = sweep_blocks(softmax_weights, pad_front(q), pad_front(k), pad_front(v))[:, PAD:]
    return o.reshape(B, L, MLA_HEADS * MLA_V)


def rwkv7_branch(z_rw, mu, w0, w2, a0, a2, g2, k_k, k_a, r_k, ln_w, ln_b):
    B, L, _ = z_rw.shape
    z = z_rw + (shift(z_rw) - z_rw) * mu
    r, k, v, wl, al, gl = jnp.split(z, RW_SPLITS, axis=-1)
    w = -jax.nn.softplus(-(w0 + jnp.tanh(wl) @ w2)) - 0.5
    decay = jnp.exp(-jnp.exp(w.astype(jnp.float32)))
    a = jax.nn.sigmoid(a0 + al @ a2)
    g = jax.nn.sigmoid(gl) @ g2

    def heads(t):
        return t.astype(jnp.float32).reshape(B, L, RW_HEADS, RW_HEAD)

    kk = heads(k * k_k)
    kk = kk / jnp.maximum(jnp.linalg.norm(kk, axis=-1, keepdims=True), 1e-12)
    a = heads(a)
    k = heads(k) * (1.0 + (a - 1.0) * k_a.astype(jnp.float32).reshape(RW_HEADS, RW_HEAD))
    r, v, decay = heads(r), heads(v), heads(decay)

    def step(S, inp):
        r_t, w_t, k_t, v_t, kk_t, a_t = inp
        s_kk = jnp.einsum('bhvk,bhk->bhv', S, kk_t)
        S = S * w_t[:, :, None, :] - s_kk[..., None] * (kk_t * a_t)[:, :, None, :] + v_t[..., None] * k_t[:, :, None, :]
        return S, jnp.einsum('bhvk,bhk->bhv', S, r_t)

    tm = lambda t: t.swapaxes(0, 1)
    S0 = jnp.zeros((B, RW_HEADS, RW_HEAD, RW_HEAD), jnp.float32)
    _, y = lax.scan(step, S0, (tm(r), tm(decay), tm(k), tm(v), tm(kk), tm(a)))
    y = tm(y)
    mean = jnp.mean(y, axis=-1, keepdims=True)
    var = jnp.var(y, axis=-1, keepdims=True)
    y = (y - mean) * lax.rsqrt(var + RW_GN_EPS)
    y = y * ln_w.reshape(RW_HEADS, RW_HEAD) + ln_b.reshape(RW_HEADS, RW_HEAD)
    y = y + jnp.sum(r * k * r_k, axis=-1, keepdims=True) * v
    return (y.reshape(B, L, RW_W) * g).astype(z_rw.dtype)


def stick_breaking_branch(z_sb):
    B, L, _ = z_sb.shape
    q, k, v = [t.reshape(B, L, SB_HEADS, SB_HEAD) for t in jnp.split(z_sb, 3, axis=-1)]
    o = sweep_blocks(stick_breaking_weights, pad_front(q), pad_front(k), pad_front(v))[:, PAD:]
    return o.reshape(B, L, SB_W)


def gla_branch(z_gla, a2, a_b, norm_g):
    B, L, _ = z_gla.shape
    q, k, v, al, r = jnp.split(z_gla, GLA_SPLITS, axis=-1)
    q = q.reshape(B, L, GLA_HEADS, GLA_DK) * GLA_DK ** -0.5
    k = k.reshape(B, L, GLA_HEADS, GLA_DK)
    v = v.reshape(B, L, GLA_HEADS, GLA_DV)
    log_a = (jax.nn.log_sigmoid((al @ a2 + a_b).astype(jnp.float32)) / GLA_TAU).reshape(B, L, GLA_HEADS, GLA_DK)
    o = gla_chunked(pad_front(q), pad_front(k), pad_front(v), pad_front(log_a))[:, PAD:]
    o = rmsnorm(o, norm_g.reshape(GLA_HEADS, GLA_DV))
    return o.reshape(B, L, GLA_W) * jax.nn.silu(r)


def token_mixing(n, pos, w_in, mla_q_norm, mla_w_uq, mla_kv_norm, mla_w_ukv,
                 rw_mu, rw_w0, rw_w2, rw_a0, rw_a2, rw_g2, rw_k_k, rw_k_a, rw_r_k, rw_ln_w, rw_ln_b,
                 gla_a2, gla_a_b, gla_norm, gate_b, w_branch, w_out):
    B, L, _ = n.shape
    cq, ckv, kr, z_rw, z_sb, z_gla, gate_logits = jnp.split(n @ w_in, IN_SPLITS, axis=-1)
    y_mla = mla_branch(cq, ckv, kr, pos, mla_q_norm, mla_w_uq, mla_kv_norm, mla_w_ukv)
    y_rw = rwkv7_branch(z_rw, rw_mu, rw_w0, rw_w2, rw_a0, rw_a2, rw_g2, rw_k_k, rw_k_a, rw_r_k, rw_ln_w, rw_ln_b)
    y_sb = stick_breaking_branch(z_sb)
    y_gla = gla_branch(z_gla, gla_a2, gla_a_b, gla_norm)
    y = jnp.stack([y_mla, y_rw, y_sb, y_gla], axis=2)
    branch = jnp.einsum('blnc,ncd->blnd', y, w_branch)
    gates = jax.nn.sigmoid(gate_logits.reshape(B, L, N_BRANCH, D_MODEL) + gate_b)
    return jnp.sum(gates * branch, axis=2) @ w_out


def conv_ffn(n, w_ffn_in, conv_w, conv_b, w_ffn_out):
    L = n.shape[1]
    a, u = jnp.split(n @ w_ffn_in, 2, axis=-1)
    a_pad = jnp.pad(a, ((0, 0), (CONV_W - 1, 0), (0, 0)))
    a = conv_b + sum(a_pad[:, j:j + L] * conv_w[j] for j in range(CONV_W))
    return (jax.nn.silu(a) * u) @ w_ffn_out


def setup_inputs(seed: int = 0) -> dict:
    key = jax.random.key(seed)
    ks = iter(jax.random.split(key, 40))

    def nrm(shape, scale):
        return scale * jax.random.normal(next(ks), shape, jnp.float32)

    def gain(shape):
        return 1.0 + nrm(shape, 0.02)

    return {
        "x": nrm((BATCH, SEQ, D_MODEL), 1.0),
        "meta_tokens": nrm((N_META, D_MODEL), 1.0),
        "norm_mix": gain((DEPTH, D_MODEL)),
        "w_in": nrm((DEPTH, D_MODEL, IN_TOTAL), D_MODEL ** -0.5),
        "mla_q_norm": gain((DEPTH, MLA_Q_RANK)),
        "mla_w_uq": nrm((DEPTH, MLA_Q_RANK, MLA_HEADS * (MLA_NOPE + MLA_ROPE)), MLA_Q_RANK ** -0.5),
        "mla_kv_norm": gain((DEPTH, MLA_KV_RANK)),
        "mla_w_ukv": nrm((DEPTH, MLA_KV_RANK, MLA_HEADS * (MLA_NOPE + MLA_V)), MLA_KV_RANK ** -0.5),
        "rw_mu": jax.random.uniform(next(ks), (DEPTH, RW_IN), jnp.float32),
        "rw_w0": nrm((DEPTH, RW_W), 0.5),
        "rw_w2": nrm((DEPTH, RW_DECAY_LORA, RW_W), RW_DECAY_LORA ** -0.5),
        "rw_a0": nrm((DEPTH, RW_W), 0.1),
        "rw_a2": nrm((DEPTH, RW_A_LORA, RW_W), RW_A_LORA ** -0.5),
        "rw_g2": nrm((DEPTH, RW_G_LORA, RW_W), RW_G_LORA ** -0.5),
        "rw_k_k": 0.85 + nrm((DEPTH, RW_W), 0.02),
        "rw_k_a": 1.0 + nrm((DEPTH, RW_W), 0.02),
        "rw_r_k": nrm((DEPTH, RW_HEADS, RW_HEAD), 0.1),
        "rw_ln_w": gain((DEPTH, RW_W)),
        "rw_ln_b": nrm((DEPTH, RW_W), 0.02),
        "gla_a2": nrm((DEPTH, GLA_GATE_LORA, GLA_QK), GLA_GATE_LORA ** -0.5),
        "gla_a_b": nrm((DEPTH, GLA_QK), 0.1),
        "gla_norm": gain((DEPTH, GLA_W)),
        "gate_b": nrm((DEPTH, N_BRANCH, D_MODEL), 0.02),
        "w_branch": nrm((DEPTH, N_BRANCH, BRANCH_W, D_MODEL), BRANCH_W ** -0.5),
        "w_out": nrm((DEPTH, D_MODEL, D_MODEL), D_MODEL ** -0.5),
        "norm_ffn": gain((DEPTH, D_MODEL)),
        "w_ffn_in": nrm((DEPTH, D_MODEL, 2 * D_FF), D_MODEL ** -0.5),
        "ffn_conv_w": nrm((DEPTH, CONV_W, D_FF), CONV_W ** -0.5),
        "ffn_conv_b": nrm((DEPTH, D_FF), 0.02),
        "w_ffn_out": nrm((DEPTH, D_FF, D_MODEL), D_FF ** -0.5),
        "norm_final": gain((D_MODEL,)),
    }


def reference(x, meta_tokens, norm_mix, w_in, mla_q_norm, mla_w_uq, mla_kv_norm, mla_w_ukv,
              rw_mu, rw_w0, rw_w2, rw_a0, rw_a2, rw_g2, rw_k_k, rw_k_a, rw_r_k, rw_ln_w, rw_ln_b,
              gla_a2, gla_a_b, gla_norm, gate_b, w_branch, w_out,
              norm_ffn, w_ffn_in, ffn_conv_w, ffn_conv_b, w_ffn_out, norm_final):
    B = x.shape[0]
    meta = jnp.broadcast_to(meta_tokens[None].astype(x.dtype), (B, N_META, D_MODEL))
    h = jnp.concatenate([meta, x], axis=1)
    pos = jnp.arange(h.shape[1])
    for i in range(DEPTH):
        h = h + token_mixing(rmsnorm(h, norm_mix[i]), pos, w_in[i],
                             mla_q_norm[i], mla_w_uq[i], mla_kv_norm[i], mla_w_ukv[i],
                             rw_mu[i], rw_w0[i], rw_w2[i], rw_a0[i], rw_a2[i], rw_g2[i],
                             rw_k_k[i], rw_k_a[i], rw_r_k[i], rw_ln_w[i], rw_ln_b[i],
                             gla_a2[i], gla_a_b[i], gla_norm[i], gate_b[i], w_branch[i], w_out[i])
        h = h + conv_ffn(rmsnorm(h, norm_ffn[i]), w_ffn_in[i], ffn_conv_w[i], ffn_conv_b[i], w_ffn_out[i])
    return rmsnorm(h, norm_final)[:, N_META:]
```

```python
import math
import numpy as np
from contextlib import ExitStack
import concourse.bass as bass
import concourse.mybir as mybir
from concourse.bass_utils import run_bass_kernel_spmd

F32 = mybir.dt.float32
BF16 = mybir.dt.bfloat16
ALU = mybir.AluOpType
AF = mybir.ActivationFunctionType
AX = mybir.AxisListType

NDMA = 24
EPS = 1e-6
PADC = 112
NMIX = 2992
OFF_CQ, OFF_CKV, OFF_KR, OFF_RW, OFF_SB, OFF_GLA, OFF_GATE = 0, 256, 384, 416, 1440, 2208, 2992
DFF = 2816
NV = 192
(C_NMIX, C_NFFN, C_GATEB, C_CONVW, C_CONVB, C_QN, C_KVN, C_MURKV, C_MUWL, C_MUAL, C_MUGL,
 C_W0, C_A0, C_KK, C_KA, C_RK, C_LNW, C_LNB, C_GAB, C_GNORM) = (
    0, 8, 16, 48, 114, 136, 138, 139, 151, 152, 153, 154, 158, 162, 166, 170, 174, 178, 182, 183)


class Res:
    __slots__ = ("w", "r", "excl")

    def __init__(self, excl=False):
        self.w = {}
        self.r = {}
        self.excl = excl


def _rl(v):
    r = v.res
    return r if isinstance(r, tuple) else (r,)


class V:
    __slots__ = ("ap", "res")

    def __init__(self, ap, res=None):
        self.ap = ap
        self.res = res if res is not None else Res()

    def __getitem__(self, idx):
        return V(self.ap[idx], self.res)

    def rr(self, pat, **kw):
        return V(self.ap.rearrange(pat, **kw), self.res)

    def fresh(self):
        return V(self.ap, Res())

    def withres(self, res):
        return V(self.ap, res)


class _Eng:
    def __init__(self, name, obj, sem):
        self.name, self.obj, self.sem = name, obj, sem
        self.n = 0
        self.waited = {}
        self.pending = False


class Rot:
    def __init__(self, items):
        self.items = items
        self.i = 0

    def next(self):
        x = self.items[self.i]
        self.i = (self.i + 1) % len(self.items)
        return x


class FW:
    def __init__(self, nc, es):
        self.nc = nc
        self.es = es
        self.engs = {}
        for name, obj in (("pe", nc.tensor), ("act", nc.scalar), ("dve", nc.vector),
                          ("pool", nc.gpsimd), ("sp", nc.sync)):
            sem = es.enter_context(nc.semaphore("s_" + name))
            self.engs[name] = _Eng(name, obj, sem)
        self.dma_sems = [es.enter_context(nc.semaphore("d%d" % i)) for i in range(NDMA)]
        self.dma_cnt = [0] * NDMA
        self.dma_next = 0
        self.nins = 0
        self._uid = 0
        self._ev = 0

    def sb(self, shape, dt=F32, es=None):
        self._uid += 1
        t = (es or self.es).enter_context(self.nc.sbuf_tensor("sb%d" % self._uid, list(shape), dt))
        return V(t[:], Res())

    def ps(self, shape, dt=F32, es=None):
        self._uid += 1
        t = (es or self.es).enter_context(self.nc.psum_tensor("ps%d" % self._uid, list(shape), dt))
        return V(t[:], Res(excl=True))

    def _wait(self, eng, tok):
        key, sem, val, src = tok
        if src == "pe" and eng.name == "pe":
            return
        if eng.waited.get(key, 0) >= val:
            return
        eng.obj.wait_ge(sem, val)
        eng.waited[key] = val
        self.nins += 1

    def _deps(self, reads, writes):
        toks = []
        for v in reads:
            for r in _rl(v):
                toks.extend(r.w.values())
                if r.excl:
                    toks.extend(t for t in r.r.values() if t[3] != self._cur)
        for v in writes:
            for r in _rl(v):
                toks.extend(r.w.values())
                toks.extend(r.r.values())
        return toks

    def _mark(self, key, tok, reads, writes):
        wres = []
        for v in writes:
            for r in _rl(v):
                r.w = {key: tok}
                r.r = {}
                wres.append(r)
        for v in reads:
            for r in _rl(v):
                if r not in wres:
                    r.r[key] = tok

    def op(self, engname, fn, reads, writes, inc=True):
        eng = self.engs[engname]
        self._cur = engname
        for t in self._deps(reads, writes):
            self._wait(eng, t)
        ins = fn(eng.obj)
        self.nins += 1
        if inc:
            eng.n += 1
            ins.then_inc(eng.sem, 1)
            tok = (engname, eng.sem, eng.n, engname)
            eng.pending = False
        else:
            tok = (engname, eng.sem, eng.n + 1, engname)
            eng.pending = True
        self._mark(engname, tok, reads, writes)
        return ins

    def dma(self, out, in_, q="sp"):
        eng = self.engs[q]
        self._cur = q
        for t in self._deps([in_], [out]):
            self._wait(eng, t)
        i = self.dma_next
        self.dma_next = (i + 1) % NDMA
        key = ("dma", i)
        if self.dma_cnt[i] > 0:
            self._wait(eng, (key, self.dma_sems[i], self.dma_cnt[i], None))
        self.dma_cnt[i] += 16
        eng.obj.dma_start(out=out.ap, in_=in_.ap).then_inc(self.dma_sems[i], 16)
        self.nins += 1
        tok = (key, self.dma_sems[i], self.dma_cnt[i], None)
        self._mark(key, tok, [in_], [out])

    def barrier(self, engines=("pe", "act", "dve", "pool", "sp")):
        for en in engines:
            eng = self.engs[en]
            for i in range(NDMA):
                if self.dma_cnt[i] > 0:
                    self._wait(eng, (("dma", i), self.dma_sems[i], self.dma_cnt[i], None))
            for name, e in self.engs.items():
                assert not e.pending, name
                if e.n > 0 and name != en:
                    self._wait(eng, (name, e.sem, e.n, None))

    def mm(self, out, lhsT, rhs, start=True, stop=True):
        return self.op("pe", lambda e: e.matmul(out.ap, lhsT.ap, rhs.ap, start=start, stop=stop),
                       [lhsT, rhs], [out], inc=stop)

    def transpose(self, out, in_, ident):
        return self.op("pe", lambda e: e.transpose(out.ap, in_.ap, ident.ap), [in_, ident], [out])

    def act(self, out, in_, func, bias=None, scale=None):
        reads = [in_]
        kw = {}
        if bias is not None:
            if isinstance(bias, V):
                reads.append(bias)
                kw["bias"] = bias.ap
            else:
                kw["bias"] = bias
        if scale is not None:
            if isinstance(scale, V):
                reads.append(scale)
                kw["scale"] = scale.ap
            else:
                kw["scale"] = scale
        return self.op("act", lambda e: e.activation(out.ap, in_.ap, func, **kw), reads, [out])

    def tt(self, out, in0, in1, op, eng="dve"):
        return self.op(eng, lambda e: e.tensor_tensor(out.ap, in0.ap, in1.ap, op), [in0, in1], [out])

    def ts(self, out, in0, s1, s2=None, op0=ALU.mult, op1=None, eng="dve"):
        reads = [in0]
        a1 = s1
        if isinstance(s1, V):
            reads.append(s1)
            a1 = s1.ap
        a2 = s2
        if isinstance(s2, V):
            reads.append(s2)
            a2 = s2.ap
        kw = {}
        if op1 is not None:
            kw["op1"] = op1
        return self.op(eng, lambda e: e.tensor_scalar(out.ap, in0.ap, a1, a2, op0, **kw), reads, [out])

    def stt(self, out, in0, scalar, in1, op0, op1, eng="dve"):
        reads = [in0, in1]
        a = scalar
        if isinstance(scalar, V):
            reads.append(scalar)
            a = scalar.ap
        return self.op(eng, lambda e: e.scalar_tensor_tensor(out.ap, in0.ap, a, in1.ap, op0, op1), reads, [out])

    def copy(self, out, in_, eng="dve"):
        if eng == "act":
            return self.act(out, in_, AF.Copy)
        return self.op(eng, lambda e: e.tensor_copy(out.ap, in_.ap), [in_], [out])

    def evac(self, out, in_):
        self._ev ^= 1
        return self.copy(out, in_, eng="act" if self._ev else "dve")

    def memset(self, out, val, eng="dve"):
        return self.op(eng, lambda e: e.memset(out.ap, val), [], [out])

    def scan(self, out, d0, d1, init, op0, op1):
        return self.op("dve", lambda e: e.tensor_tensor_scan(out.ap, d0.ap, d1.ap, init, op0, op1), [d0, d1], [out])

    def recip(self, out, in_):
        return self.op("dve", lambda e: e.reciprocal(out.ap, in_.ap), [in_], [out])

    def aselect(self, t, cmp, fill, base, pattern, cm):
        return self.op("pool", lambda g: g.affine_select(out=t.ap, in_=t.ap, compare_op=cmp, fill=fill, base=base,
                                                         pattern=pattern, channel_multiplier=cm), [t], [t])


def chunks(t0, t1, maxw=512):
    out = []
    t = t0
    while t < t1:
        w = min(maxw, t1 - t)
        out.append((t, w))
        t += w
    return out


class Builder:
    def __init__(self, NB, DEPTH, dbg=(), phases=None):
        self.phases = phases
        self.NB = NB
        self.Lp = NB * 128
        self.SEQ = self.Lp - 128
        self.DEPTH = DEPTH
        self.dbg = dbg
        self.nc = bass.Bass("TRN2", target_bir_lowering=False)

    def din(self, name, shape, dt=F32):
        return V(self.nc.dram_tensor(name, list(shape), dt, kind="ExternalInput").ap())

    def dscr(self, name, shape, dt=F32):
        kind = "ExternalOutput" if name in self.dbg else "Internal"
        return V(self.nc.dram_tensor(name, list(shape), dt, kind=kind).ap())

    def build(self):
        nc, Lp, DEPTH = self.nc, self.Lp, self.DEPTH
        self.hT0 = self.din("hT0", [1024, Lp])
        self.w_in = self.din("w_in", [DEPTH, 1024, 7088])
        self.w_uq = self.din("mla_w_uq", [DEPTH, 256, 384])
        self.w_ukv = self.din("mla_w_ukv", [DEPTH, 128, 512])
        self.rw_w2 = self.din("rw_w2", [DEPTH, 64, 256])
        self.rw_a2 = self.din("rw_a2", [DEPTH, 64, 256])
        self.rw_g2 = self.din("rw_g2", [DEPTH, 128, 256])
        self.gla_a2 = self.din("gla_a2", [DEPTH, 16, 128])
        self.w_branch = self.din("w_branch", [DEPTH, 4, 256, 1024])
        self.w_out = self.din("w_out", [DEPTH, 1024, 1024])
        self.w_ffn_in = self.din("w_ffn_in", [DEPTH, 1024, 5632])
        self.w_ffn_out = self.din("w_ffn_out", [DEPTH, 2816, 1024])
        self.vecs_d = self.din("vecs", [DEPTH, 128, NV])
        self.gvec_d = self.din("gvec", [128, 8])
        self.rope_d = self.din("rope", [64, Lp])
        self.outT = V(nc.dram_tensor("outT", [1024, self.SEQ], F32, kind="ExternalOutput").ap())
        self.hT = self.dscr("hT", [1024, Lp])
        self.nT = self.dscr("nT", [1024, Lp], BF16)
        self.zT = self.dscr("zT", [NMIX, Lp])
        self.yT = self.dscr("yT", [1024, Lp], BF16)
        nch = (Lp + 511) // 512
        self.hres = [[Res() for _ in range(nch)] for _ in range(8)]

        with ExitStack() as es:
            self.fw = fw = FW(nc, es)
            self.consts(es)
            for k in range(8):
                fw.dma(self.hT[k * 128:(k + 1) * 128, :].withres(tuple(self.hres[k])),
                       self.hT0[k * 128:(k + 1) * 128, :])
            fw.barrier()
            print("sbuf remaining after consts:", nc.sbuf_bytes_remaining)
            for l in range(DEPTH):
                self.layer_setup(l)
                for nm, fn in (("A", self.phase_A), ("mla", self.mla), ("sb", self.sbatt), ("gla", self.gla),
                               ("rwkv", self.rwkv), ("C1", self.phase_C1), ("C2", self.phase_C2)):
                    if self.phases is not None and nm not in self.phases:
                        continue
                    with ExitStack() as e2:
                        fn(l, e2)
                    fw.barrier()
            with ExitStack() as e2:
                self.phase_F(e2)
            fw.barrier()
            print("instructions:", fw.nins, {k: e.n for k, e in fw.engs.items()})
        return nc

    def hcell(self, k, t0, w):
        c0, c1 = t0 // 512, (t0 + w - 1) // 512
        res = tuple(self.hres[k][c] for c in range(c0, c1 + 1))
        return self.hT[k * 128:(k + 1) * 128, t0:t0 + w].withres(res if len(res) > 1 else res[0])

    def hall(self, t0, w):
        c0, c1 = t0 // 512, (t0 + w - 1) // 512
        res = tuple(self.hres[k][c] for k in range(8) for c in range(c0, c1 + 1))
        return self.hT.rr("(k p) t -> p k t", p=128)[:, :, t0:t0 + w].withres(res)

    def consts(self, es):
        fw = self.fw
        Lp = self.Lp
        self.ident = fw.sb([128, 128], F32, es)
        fw.memset(self.ident, 0.0, eng="pool")
        fw.aselect(self.ident, ALU.not_equal, 1.0, 0, [[-1, 128]], 1)
        self.ident_bf = fw.sb([128, 128], BF16, es)
        fw.copy(self.ident_bf, self.ident)
        self.ones_bf = fw.sb([128, 128], BF16, es)
        fw.memset(self.ones_bf, 1.0)
        self.zeros_bf = fw.sb([128, 128], BF16, es)
        fw.memset(self.zeros_bf, 0.0)
        padcol = fw.sb([128, 1], F32, es)
        fw.memset(padcol, 1.0, eng="pool")
        fw.aselect(padcol, ALU.is_ge, 0.0, -PADC, [[0, 1]], 1)
        self.ones_pad = fw.sb([128, 128], BF16, es)
        fw.ts(self.ones_pad, self.ones_bf, padcol[:, 0:1], None, op0=ALU.mult)

        def tri(cmp, base, pat, cm):
            t32 = fw.sb([128, 128], F32, es)
            fw.memset(t32, 1.0, eng="pool")
            fw.aselect(t32, cmp, 0.0, base, pat, cm)
            tb = fw.sb([128, 128], BF16, es)
            fw.copy(tb, t32)
            return t32, tb
        self.tri_incl32, self.tri_incl = tri(ALU.is_ge, 0, [[1, 128]], -1)
        self.tri_strict32, self.tri_strict = tri(ALU.is_gt, 0, [[1, 128]], -1)
        self.tri_sl32, self.tri_sl = tri(ALU.is_gt, 0, [[-1, 128]], 1)
        self.tri_ge32, self.uincl = tri(ALU.is_ge, 0, [[-1, 128]], 1)
        self.uincl_pad = fw.sb([128, 128], BF16, es)
        fw.ts(self.uincl_pad, self.uincl, padcol[:, 0:1], None, op0=ALU.mult)
        self.tri_incl4 = fw.sb([128, 4, 128], F32, es)
        for h in range(4):
            fw.copy(self.tri_incl4[:, h, :], self.tri_incl32)
        self.rm = fw.sb([128, 512], F32, es)
        fw.memset(self.rm, 1.0)
        for k in range(4):
            fw.memset(self.rm[:, k * 128:k * 128 + 1], 0.0)
        self.headmask = fw.sb([128, 4], F32, es)
        fw.memset(self.headmask, 1.0, eng="pool")
        fw.aselect(self.headmask, ALU.is_ge, 0.0, 0, [[-32, 4]], 1)
        fw.aselect(self.headmask, ALU.is_ge, 0.0, 31, [[32, 4]], -1)
        self.bdmask = fw.sb([128, 256], F32, es)
        fw.memset(self.bdmask, 1.0, eng="pool")
        bd3 = self.bdmask.rr("p (h c) -> p h c", h=4)
        fw.aselect(bd3, ALU.is_ge, 0.0, 0, [[-32, 4], [0, 64]], 1)
        fw.aselect(bd3, ALU.is_ge, 0.0, 31, [[32, 4], [0, 64]], -1)
        self.vecs = fw.sb([128, NV], F32, es)
        self.gvec = fw.sb([128, 8], F32, es)
        fw.dma(self.gvec, self.gvec_d)
        self.dcols = fw.sb([128, 32], F32, es)
        self._eps = {}
        for ev in (EPS, 1.0, 1e-24, 64e-5):
            t = fw.sb([128, 1], F32, es)
            fw.memset(t, ev)
            self._eps[ev] = t
        banks = [fw.ps([128, 512], F32, es) for _ in range(8)]
        self.psA = banks[0:3]
        self.psR = Rot(banks[3:8])
        self.wst = Rot([fw.sb([128, 1408], F32, es) for _ in range(2)])
        self.wbf = Rot([fw.sb([128, 2816], BF16, es) for _ in range(3)])
        self.t32 = Rot([fw.sb([128, 512], F32, es) for _ in range(6)])
        self.tbf = Rot([fw.sb([128, 512], BF16, es) for _ in range(6)])

    def vc(self, c, n=1, rows=128):
        return self.vecs[0:rows, c:c + n]

    def layer_setup(self, l):
        fw = self.fw
        fw.dma(self.vecs, self.vecs_d[l])
        d = self.dcols
        fw.ts(d[:, 0:15], self.vecs[:, C_MURKV:C_MURKV + 15], -1.0, 1.0, op0=ALU.mult, op1=ALU.add)
        fw.ts(d[:, 15:19], self.vecs[:, C_KA:C_KA + 4], -1.0, 1.0, op0=ALU.mult, op1=ALU.add)
        fw.ts(d[:, 19:20], self.vecs[:, C_GAB:C_GAB + 1], -1.0, None, op0=ALU.mult)

    def load_w(self, src2d, kc, ow, prows=128):
        fw = self.fw
        wb = self.wbf.next()[0:prows, 0:kc * ow].rr("p (k o) -> p k o", k=kc)
        srcv = src2d.rr("(k p) o -> p k o", p=prows)
        kmax = max(1, 1408 // ow)
        k0 = 0
        while k0 < kc:
            kn = min(kmax, kc - k0)
            st = self.wst.next()[0:prows, 0:kn * ow].rr("p (k o) -> p k o", k=kn)
            fw.dma(st, srcv[:, k0:k0 + kn, :])
            fw.copy(wb[:, k0:k0 + kn, :], st, eng="pool")
            k0 += kn
        return wb

    def rstd_from_ss(self, out, ss_ps, n, eps, rows=128):
        fw = self.fw
        fw.act(out, ss_ps, AF.Ln, bias=self.epscol(eps)[0:rows], scale=1.0 / n)
        fw.act(out, out, AF.Exp, scale=-0.5)

    def epscol(self, eps):
        return self._eps[eps]

    def rmsnorm_chunk(self, hc, w, gcol0, gsrc, out_fn):
        fw = self.fw
        sq = self.sq8.next()
        fw.act(sq[:, :, :w], hc[:, :, :w], AF.Square)
        ps = self.psR.next()
        for k in range(8):
            fw.mm(ps[:, :w], self.ones_bf, sq[:, k, :w], start=(k == 0), stop=(k == 7))
        rstd = self.t32.next()
        self.rstd_from_ss(rstd[:, :w], ps[:, :w], 1024.0, EPS)
        for k in range(8):
            fw.stt(out_fn(k), hc[:, k, :w], gsrc[:, gcol0 + k:gcol0 + k + 1], rstd[:, :w], ALU.mult, ALU.mult)

    def phase_A(self, l, es):
        fw, Lp = self.fw, self.Lp
        n = fw.sb([128, 8, Lp], BF16, es)
        self.h8 = Rot([fw.sb([128, 8, 512], F32, es) for _ in range(2)])
        self.sq8 = Rot([fw.sb([128, 8, 512], BF16, es) for _ in range(2)])
        nTv = self.nT.rr("(k p) t -> p k t", p=128)
        for (t0, w) in chunks(0, Lp):
            hc = self.h8.next()
            fw.dma(hc[:, :, :w], self.hall(t0, w))
            self.rmsnorm_chunk(hc, w, C_NMIX, self.vecs, lambda k: n[:, k, t0:t0 + w])
            fw.dma(nTv[:, :, t0:t0 + w].fresh(), n[:, :, t0:t0 + w], q="sp")
        ocs = [(o, min(128, NMIX - o)) for o in range(0, NMIX, 128)]
        wl = self.w_in[l]
        cur = self.load_w(wl[:, 0:ocs[0][1]], 8, ocs[0][1])
        for i, (o0, ow) in enumerate(ocs):
            nxt = None
            if i + 1 < len(ocs):
                o1, ow1 = ocs[i + 1]
                nxt = self.load_w(wl[:, o1:o1 + ow1], 8, ow1)
            for (t0, w) in chunks(0, Lp):
                ps = self.psR.next()
                for k in range(8):
                    fw.mm(ps[:ow, :w], cur[:, k, :], n[:, k, t0:t0 + w], start=(k == 0), stop=(k == 7))
                ot = self.t32.next()
                fw.evac(ot[:ow, :w], ps[:ow, :w])
                fw.dma(self.zT[o0:o0 + ow, t0:t0 + w].fresh(), ot[:ow, :w], q="sp")
            cur = nxt

    def mla(self, l, es):
        fw, Lp, NB = self.fw, self.Lp, self.NB
        scale = 96.0 ** -0.5
        Q = [fw.sb([96, Lp], BF16, es) for _ in range(4)]
        K = [fw.sb([96, Lp], BF16, es) for _ in range(4)]
        Vt = fw.sb([128, NB, 256], BF16, es)
        self.ropeC = fw.sb([96, Lp], F32, es)
        self.ropeS = fw.sb([96, Lp], F32, es)
        fw.dma(self.ropeC[64:96, :], self.rope_d[0:32, :])
        fw.dma(self.ropeS[64:96, :], self.rope_d[32:64, :])
        wuq_t = self.load_w(self.w_uq[l], 2, 384)
        wuq = fw.sb([128, 2, 384], BF16, es)
        fw.copy(wuq, wuq_t, eng="pool")
        wuq_s = fw.sb([128, 2, 384], BF16, es)
        fw.copy(wuq_s, wuq_t, eng="pool")
        src = self.w_uq[l].rr("(k p) (h c) -> p k h c", p=128, h=4)
        st2 = self.wst.next()[:, 0:256].rr("p (k c) -> p k c", k=2)
        for k in range(2):
            fw.dma(st2[:, k, 0:64].rr("p (h c) -> p h c", h=4), src[:, k, :, 80:96])
            fw.dma(st2[:, k, 64:128].rr("p (h c) -> p h c", h=4), src[:, k, :, 64:80])
        wv = wuq_s.rr("p k (h c) -> p k h c", h=4)
        for k in range(2):
            fw.copy(wv[:, k, :, 64:80], st2[:, k, 0:64].rr("p (h c) -> p h c", h=4), eng="pool")
            fw.copy(wv[:, k, :, 80:96], st2[:, k, 64:128].rr("p (h c) -> p h c", h=4), eng="pool")
        wukv = self.load_w(self.w_ukv[l], 1, 512)
        wkn = fw.sb([128, 4, 64], BF16, es)
        wvv = fw.sb([128, 256], BF16, es)
        wk4 = wukv[:, 0, :].rr("p (h c) -> p h c", h=4)
        fw.copy(wkn, wk4[:, :, 0:64], eng="pool")
        fw.copy(wvv.rr("p (h c) -> p h c", h=4), wk4[:, :, 64:128], eng="pool")
        cq2 = Rot([fw.sb([128, 2, 512], F32, es) for _ in range(2)])
        sq2 = Rot([fw.sb([128, 2, 512], BF16, es) for _ in range(2)])
        cqn = Rot([fw.sb([128, 2, 512], BF16, es) for _ in range(2)])
        kr2 = Rot([fw.sb([96, 2, 512], F32, es) for _ in range(2)])
        krr = Rot([fw.sb([96, 512], BF16, es) for _ in range(2)])
        zcq = self.zT[0:256, :].rr("(k p) t -> p k t", p=128)
        for (t0, w) in chunks(0, Lp):
            c = cq2.next()
            fw.dma(c[:, :, :w], zcq[:, 0:2, t0:t0 + w].fresh())
            ck = self.t32.next()
            fw.dma(ck[:, :w], self.zT[OFF_CKV:OFF_CKV + 128, t0:t0 + w].fresh())
            kr = kr2.next()
            fw.dma(kr[64:96, 0, :w], self.zT[OFF_KR:OFF_KR + 32, t0:t0 + w].fresh())
            fw.dma(kr[64:80, 1, :w], self.zT[OFF_KR + 16:OFF_KR + 32, t0:t0 + w].fresh())
            fw.dma(kr[80:96, 1, :w], self.zT[OFF_KR:OFF_KR + 16, t0:t0 + w].fresh())
            s = sq2.next()
            fw.act(s[:, :, :w], c[:, :, :w], AF.Square)
            ps = self.psR.next()
            for k in range(2):
                fw.mm(ps[:, :w], self.ones_bf, s[:, k, :w], start=(k == 0), stop=(k == 1))
            rstd = self.t32.next()
            self.rstd_from_ss(rstd[:, :w], ps[:, :w], 256.0, EPS)
            cn = cqn.next()
            for k in range(2):
                fw.stt(cn[:, k, :w], c[:, k, :w], self.vc(C_QN + k), rstd[:, :w], ALU.mult, ALU.mult)
            s2 = self.tbf.next()
            fw.act(s2[:, :w], ck[:, :w], AF.Square)
            ps = self.psR.next()
            fw.mm(ps[:, :w], self.ones_bf, s2[:, :w])
            rstd2 = self.t32.next()
            self.rstd_from_ss(rstd2[:, :w], ps[:, :w], 128.0, EPS)
            ckn = self.tbf.next()
            fw.stt(ckn[:, :w], ck[:, :w], self.vc(C_KVN), rstd2[:, :w], ALU.mult, ALU.mult)
            t1 = self.t32.next()
            t2 = self.t32.next()
            fw.tt(t1[64:96, :w], kr[64:96, 0, :w], self.ropeC[64:96, t0:t0 + w], ALU.mult)
            fw.tt(t2[64:96, :w], kr[64:96, 1, :w], self.ropeS[64:96, t0:t0 + w], ALU.mult)
            kq = krr.next()
            fw.tt(kq[64:96, :w], t1[64:96, :w], t2[64:96, :w], ALU.add)
            for h in range(4):
                p1 = self.psR.next()
                p2 = self.psR.next()
                for k in range(2):
                    fw.mm(p1[0:96, :w], wuq[:, k, h * 96:(h + 1) * 96], cn[:, k, :w], start=(k == 0), stop=(k == 1))
                for k in range(2):
                    fw.mm(p2[0:96, :w], wuq_s[:, k, h * 96:(h + 1) * 96], cn[:, k, :w], start=(k == 0), stop=(k == 1))
                fw.evac(Q[h][0:64, t0:t0 + w], p1[0:64, :w])
                a1 = self.t32.next()
                a2 = self.t32.next()
                fw.tt(a1[64:96, :w], p1[64:96, :w], self.ropeC[64:96, t0:t0 + w], ALU.mult)
                fw.tt(a2[64:96, :w], p2[64:96, :w], self.ropeS[64:96, t0:t0 + w], ALU.mult)
                fw.tt(Q[h][64:96, t0:t0 + w], a1[64:96, :w], a2[64:96, :w], ALU.add)
                p3 = self.psR.next()
                fw.mm(p3[0:64, :w], wkn[:, h, :], ckn[:, :w])
                fw.evac(K[h][0:64, t0:t0 + w], p3[0:64, :w])
                fw.copy(K[h][64:96, t0:t0 + w], kq[64:96, :w], eng="pool")
            for b in range(w // 128):
                p4 = self.psR.next()
                fw.mm(p4[:, 0:256], ckn[:, b * 128:(b + 1) * 128], wvv)
                fw.evac(Vt[:, (t0 // 128) + b, :], p4[:, 0:256])
        Ops, Dps = self.psA[0], self.psA[1]
        for h in range(4):
            for (t0, w) in chunks(0, Lp):
                qb0 = t0 // 128
                kbmax = (t0 + w) // 128 - 1
                for kb in range(kbmax + 1):
                    i = kb - qb0
                    c0 = max(0, i) * 128
                    sp = self.psR.next()
                    fw.mm(sp[:, c0:w], K[h][:, kb * 128:(kb + 1) * 128], Q[h][:, t0 + c0:t0 + w])
                    e = self.tbf.next()
                    fw.act(e[:, c0:w], sp[:, c0:w], AF.Exp, scale=scale)
                    if i >= 0:
                        fw.tt(e[:, c0:c0 + 128], e[:, c0:c0 + 128], self.tri_incl, ALU.mult, eng="pool")
                    last = (kb == kbmax)
                    fw.mm(Ops[0:64, c0:w], Vt[:, kb, h * 64:(h + 1) * 64], e[:, c0:w], start=(kb == 0), stop=last)
                    fw.mm(Dps[0:64, c0:w], (self.ones_pad if kb == 0 else self.ones_bf)[:, 0:64], e[:, c0:w],
                          start=(kb == 0), stop=last)
                den = self.t32.next()
                fw.ts(den[0:64, :w], Dps[0:64, :w], 1e-30, None, op0=ALU.add)
                fw.recip(den[0:64, :w], den[0:64, :w])
                yb = self.tbf.next()
                fw.tt(yb[0:64, :w], Ops[0:64, :w], den[0:64, :w], ALU.mult)
                fw.dma(self.yT[h * 64:(h + 1) * 64, t0:t0 + w].fresh(), yb[0:64, :w], q="sp")

    def sbatt(self, l, es):
        fw, Lp, NB = self.fw, self.Lp, self.NB
        Q = fw.sb([64, 4, Lp], BF16, es)
        K = fw.sb([64, 4, Lp], BF16, es)
        Vt = fw.sb([128, NB, 256], BF16, es)
        ld = Rot([fw.sb([64, 4, 512], F32, es) for _ in range(2)])
        ldv = Rot([fw.sb([128, 2, 512], F32, es) for _ in range(2)])
        pacc = fw.sb([128, 512], BF16, es)
        zq = self.zT[OFF_SB:OFF_SB + 256, :].rr("(h d) t -> d h t", h=4)
        zk = self.zT[OFF_SB + 256:OFF_SB + 512, :].rr("(h d) t -> d h t", h=4)
        zv = self.zT[OFF_SB + 512:OFF_SB + 768, :].rr("(k p) t -> p k t", p=128)
        for (t0, w) in chunks(0, Lp):
            a = ld.next()
            fw.dma(a[:, :, :w], zq[:, :, t0:t0 + w].fresh())
            fw.copy(Q[:, :, t0:t0 + w], a[:, :, :w], eng="act")
            b = ld.next()
            fw.dma(b[:, :, :w], zk[:, :, t0:t0 + w].fresh())
            fw.copy(K[:, :, t0:t0 + w], b[:, :, :w], eng="dve")
            v = ldv.next()
            fw.dma(v[:, :, :w], zv[:, :, t0:t0 + w].fresh())
            for bb in range(w // 128):
                for k in range(2):
                    pt = self.psR.next()
                    fw.transpose(pt[:, 0:128], v[:, k, bb * 128:(bb + 1) * 128], self.ident)
                    fw.evac(Vt[:, t0 // 128 + bb, k * 128:(k + 1) * 128], pt[:, 0:128])
        Ops = self.psA[0]
        import os
        STG = int(os.environ.get("KSTAGE", "9"))
        if STG <= 1:
            return
        for h in range(4):
            for (t0, w) in chunks(0, Lp):
                qb0 = t0 // 128
                kbmax = (t0 + w) // 128 - 1
                fw.memset(pacc[:, :w], 0.0)
                first = True
                for kb in range(kbmax, -1, -1):
                    i = kb - qb0
                    c0 = max(0, i) * 128
                    zp = self.psR.next()
                    fw.mm(zp[:, c0:w], K[:, h, kb * 128:(kb + 1) * 128], Q[:, h, t0 + c0:t0 + w])
                    e1 = self.t32.next()
                    fw.act(e1[:, c0:w], zp[:, c0:w], AF.Exp, scale=0.125)
                    P = self.tbf.next()
                    fw.act(P[:, c0:w], e1[:, c0:w], AF.Ln, bias=self.epscol(1.0))
                    zs = self.t32.next()
                    fw.ts(zs[:, c0:w], zp[:, c0:w], 0.125, None, op0=ALU.mult)
                    if i >= 0:
                        fw.tt(P[:, c0:c0 + 128], P[:, c0:c0 + 128], self.tri_strict, ALU.mult, eng="pool")
                    if STG <= 3:
                        continue
                    cp = self.psR.next()
                    fw.mm(cp[:, c0:w], self.uincl_pad if kb == 0 else self.uincl, P[:, c0:w], start=True, stop=first)
                    if not first:
                        fw.mm(cp[:, c0:w], self.ones_bf, pacc[:, c0:w], start=False, stop=True)
                    lt = self.t32.next()
                    fw.tt(lt[:, c0:w], zs[:, c0:w], cp[:, c0:w], ALU.subtract)
                    if STG <= 4:
                        first = False
                        continue
                    A = self.tbf.next()
                    fw.act(A[:, c0:w], lt[:, c0:w], AF.Exp)
                    if i >= 0:
                        fw.tt(A[:, c0:c0 + 128], A[:, c0:c0 + 128], self.tri_strict, ALU.mult, eng="pool")
                    if first and c0 > 0:
                        fw.memset(A[:, 0:c0], 0.0, eng="pool")
                    cc = 0 if first else c0
                    fw.mm(Ops[0:64, cc:w], Vt[:, kb, h * 64:(h + 1) * 64], A[:, cc:w], start=first, stop=(kb == 0))
                    if kb > 0 and STG > 5:
                        fw.tt(pacc[:, c0:w], pacc[:, c0:w], P[:, c0:w], ALU.add)
                    first = False
                if STG <= 4:
                    continue
                yb = self.tbf.next()
                fw.evac(yb[0:64, :w], Ops[0:64, :w])
                fw.dma(self.yT[512 + h * 64:512 + (h + 1) * 64, t0:t0 + w].fresh(), yb[0:64, :w], q="sp")

    def gla(self, l, es):
        fw, Lp = self.fw, self.Lp
        a2 = self.load_w(self.gla_a2[l], 1, 128, prows=16)
        Sbd = fw.sb([128, 256], F32, es)
        Sbd_bf = fw.sb([128, 256], BF16, es)
        fw.memset(Sbd, 0.0)
        fw.memset(Sbd_bf, 0.0)
        ldr = Rot([fw.sb([64, 4, 512], F32, es) for _ in range(2)])
        ldv = Rot([fw.sb([128, 2, 512], F32, es) for _ in range(2)])
        s128 = Rot([fw.sb([128, 128], F32, es) for _ in range(3)])
        b128 = Rot([fw.sb([128, 128], BF16, es) for _ in range(12)])
        vtok = Rot([fw.sb([128, 256], BF16, es) for _ in range(2)])
        yst = Rot([fw.sb([64, 4, 128], BF16, es) for _ in range(2)])
        L32 = Rot([fw.sb([128, 512], F32, es) for _ in range(14)])
        Lbf = Rot([fw.sb([128, 512], BF16, es) for _ in range(6)])
        og = OFF_GLA
        zr = self.zT[og + 528:og + 784, :].rr("(h d) t -> d h t", h=4)
        zv = self.zT[og + 256:og + 512, :].rr("(k p) t -> p k t", p=128)
        yv = self.yT[768:1024, :].rr("(h d) t -> d h t", h=4)
        for (t0, w) in chunks(0, Lp):
            al = L32.next()
            fw.dma(al[0:16, :w], self.zT[og + 512:og + 528, t0:t0 + w].fresh())
            alb = Lbf.next()
            fw.copy(alb[0:16, :w], al[0:16, :w])
            xp = self.psR.next()
            fw.mm(xp[:, :w], a2[:, 0, :], alb[0:16, :w])
            e = L32.next()
            fw.act(e[:, :w], xp[:, :w], AF.Exp, bias=self.dcols[:, 19:20], scale=-1.0)
            fw.act(e[:, :w], e[:, :w], AF.Ln, bias=self.epscol(1.0))
            fw.ts(e[:, :w], e[:, :w], -1.0 / 16.0, None, op0=ALU.mult)
            gam = L32.next()
            fw.scan(gam[:, :w], self.rm[:, :w], e[:, :w], 0.0, ALU.mult, ALU.add)
            eg = L32.next()
            fw.act(eg[:, :w], gam[:, :w], AF.Exp)
            eng = L32.next()
            fw.act(eng[:, :w], gam[:, :w], AF.Exp, scale=-1.0)
            q = L32.next()
            fw.dma(q[:, :w], self.zT[og:og + 128, t0:t0 + w].fresh())
            k = L32.next()
            fw.dma(k[:, :w], self.zT[og + 128:og + 256, t0:t0 + w].fresh())
            qt = Lbf.next()
            fw.stt(qt[:, :w], q[:, :w], 32.0 ** -0.5, eg[:, :w], ALU.mult, ALU.mult)
            kt = k
            fw.tt(kt[:, :w], k[:, :w], eng[:, :w], ALU.mult)
            ktb = Lbf.next()
            fw.copy(ktb[:, :w], kt[:, :w], eng="pool")
            v = ldv.next()
            fw.dma(v[:, :, :w], zv[:, :, t0:t0 + w].fresh())
            r = ldr.next()
            fw.dma(r[:, :, :w], zr[:, :, t0:t0 + w].fresh())
            fw.act(r[:, :, :w], r[:, :, :w], AF.Silu)
            for c in range(w // 128):
                cs = slice(c * 128, (c + 1) * 128)
                gend = eg[:, c * 128 + 127:c * 128 + 128]
                vt = vtok.next()
                for kk in range(2):
                    pt = self.psR.next()
                    fw.transpose(pt[:, 0:128], v[:, kk, cs], self.ident)
                    fw.evac(vt[:, kk * 128:(kk + 1) * 128], pt[:, 0:128])
                kh = s128.next()
                fw.ts(kh, kt[:, cs], gend, None, op0=ALU.mult)
                pt = self.psR.next()
                fw.transpose(pt[:, 0:128], kh, self.ident)
                kht = b128.next()
                fw.evac(kht, pt[:, 0:128])
                scp = self.psR.next()
                for h in range(4):
                    khh = b128.next()
                    fw.ts(khh, ktb[:, cs], self.headmask[:, h:h + 1], None, op0=ALU.mult, eng="pool")
                    fw.mm(scp[:, h * 128:(h + 1) * 128], khh, qt[:, cs])
                sc = self.tbf.next()
                fw.tt(sc, scp, self.tri_incl4.rr("p h t -> p (h t)"), ALU.mult)
                op_ = self.psR.next()
                for h in range(4):
                    fw.mm(op_[0:64, h * 128:(h + 1) * 128], Sbd_bf[:, h * 64:(h + 1) * 64], qt[:, cs],
                          start=True, stop=False)
                    fw.mm(op_[0:64, h * 128:(h + 1) * 128], vt[:, h * 64:(h + 1) * 64], sc[:, h * 128:(h + 1) * 128],
                          start=False, stop=True)
                kvp = self.psR.next()
                fw.mm(kvp[:, 0:256], kht, vt)
                tmp = self.t32.next()
                fw.tt(tmp[:, 0:256], kvp[:, 0:256], self.bdmask, ALU.mult)
                fw.stt(Sbd, Sbd, gend, tmp[:, 0:256], ALU.mult, ALU.add)
                fw.copy(Sbd_bf, Sbd, eng="pool")
                osb = self.t32.next()
                fw.evac(osb[0:64, :], op_[0:64, :])
                sq = self.tbf.next()
                fw.act(sq[0:64, :], op_[0:64, :], AF.Square)
                ssp = self.psR.next()
                fw.mm(ssp[0:64, :], self.ones_bf[0:64, 0:64], sq[0:64, :])
                rs = self.t32.next()
                self.rstd_from_ss(rs[0:64, :], ssp[0:64, :], 64.0, EPS, rows=64)
                fw.tt(osb[0:64, :], osb[0:64, :], rs[0:64, :], ALU.mult)
                yo = yst.next()
                for h in range(4):
                    fw.stt(yo[:, h, :], osb[0:64, h * 128:(h + 1) * 128], self.vc(C_GNORM + h, rows=64),
                           r[:, h, cs], ALU.mult, ALU.mult)
                fw.dma(yv[:, :, t0 + c * 128:t0 + (c + 1) * 128].fresh(), yo, q="sp")

    def rwkv(self, l, es):
        fw, Lp = self.fw, self.Lp
        w2 = fw.sb([64, 256], BF16, es)
        a2 = fw.sb([64, 256], BF16, es)
        g2 = fw.sb([128, 256], BF16, es)
        t = self.load_w(self.rw_w2[l], 1, 256, prows=64)
        fw.copy(w2, t[:, 0, :], eng="pool")
        t = self.load_w(self.rw_a2[l], 1, 256, prows=64)
        fw.copy(a2, t[:, 0, :], eng="pool")
        t = self.load_w(self.rw_g2[l], 1, 256, prows=128)
        fw.copy(g2, t[:, 0, :], eng="pool")
        T32 = [fw.sb([64, 64], F32, es) for _ in range(4)]
        Tbf = [fw.sb([64, 64], BF16, es) for _ in range(4)]
        for h in range(4):
            fw.memset(T32[h], 0.0)
            fw.memset(Tbf[h], 0.0)
        halo = Rot([fw.sb([128, 513], F32, es) for _ in range(4)])
        f64 = Rot([fw.sb([64, 512], F32, es) for _ in range(28)])
        h64 = Rot([fw.sb([64, 512], BF16, es) for _ in range(12)])
        s128 = Rot([fw.sb([128, 128], F32, es) for _ in range(4)])
        b128 = Rot([fw.sb([128, 128], BF16, es) for _ in range(32)])
        b64 = Rot([fw.sb([128, 64], BF16, es) for _ in range(12)])
        oz = OFF_RW
        c_decay = -math.exp(-0.5)
        wlb_r = Rot([fw.sb([64, 512], BF16, es) for _ in range(2)])
        alb_r = Rot([fw.sb([64, 512], BF16, es) for _ in range(2)])
        glb_r = Rot([fw.sb([128, 512], BF16, es) for _ in range(2)])

        def load_shift(rows0, nrows, t0, w, mucol, omucol, out):
            hl = halo.next()
            if t0 == 0:
                fw.memset(hl[0:nrows, 0:1], 0.0)
                fw.dma(hl[0:nrows, 1:1 + w], self.zT[rows0:rows0 + nrows, 0:w].fresh())
            else:
                fw.dma(hl[0:nrows, 0:1 + w], self.zT[rows0:rows0 + nrows, t0 - 1:t0 + w].fresh())
            tmp = f64.next() if nrows <= 64 else self.t32.next()
            fw.ts(tmp[0:nrows, :w], hl[0:nrows, 0:w], mucol, None, op0=ALU.mult)
            fw.stt(out, hl[0:nrows, 1:1 + w], omucol, tmp[0:nrows, :w], ALU.mult, ALU.add)

        for (t0, w) in chunks(0, Lp):
            nck = w // 128
            wl = f64.next()
            load_shift(oz + 768, 64, t0, w, self.vc(C_MUWL, rows=64), self.dcols[0:64, 12:13], wl[:, :w])
            wlb = wlb_r.next()
            fw.act(wlb[:, :w], wl[:, :w], AF.Tanh)
            al = f64.next()
            load_shift(oz + 832, 64, t0, w, self.vc(C_MUAL, rows=64), self.dcols[0:64, 13:14], al[:, :w])
            alb = alb_r.next()
            fw.copy(alb[:, :w], al[:, :w])
            gl = self.t32.next()
            load_shift(oz + 896, 128, t0, w, self.vc(C_MUGL), self.dcols[:, 14:15], gl[:, :w])
            glb = glb_r.next()
            fw.act(glb[:, :w], gl[:, :w], AF.Sigmoid)
            for h in range(4):
                r32, k32, v32 = f64.next(), f64.next(), f64.next()
                load_shift(oz + h * 64, 64, t0, w, self.vc(C_MURKV + h, rows=64), self.dcols[0:64, h:h + 1], r32[:, :w])
                load_shift(oz + 256 + h * 64, 64, t0, w, self.vc(C_MURKV + 4 + h, rows=64),
                           self.dcols[0:64, 4 + h:5 + h], k32[:, :w])
                load_shift(oz + 512 + h * 64, 64, t0, w, self.vc(C_MURKV + 8 + h, rows=64),
                           self.dcols[0:64, 8 + h:9 + h], v32[:, :w])
                pw = self.psR.next()
                fw.mm(pw[0:64, :w], w2[:, h * 64:(h + 1) * 64], wlb[:, :w])
                logw = f64.next()
                fw.act(logw[:, :w], pw[0:64, :w], AF.Sigmoid, bias=self.vc(C_W0 + h, rows=64))
                fw.ts(logw[:, :w], logw[:, :w], c_decay, None, op0=ALU.mult)
                pa = self.psR.next()
                fw.mm(pa[0:64, :w], a2[:, h * 64:(h + 1) * 64], alb[:, :w])
                alpha = f64.next()
                fw.act(alpha[:, :w], pa[0:64, :w], AF.Sigmoid, bias=self.vc(C_A0 + h, rows=64))
                pg = self.psR.next()
                fw.mm(pg[0:64, :w], g2[:, h * 64:(h + 1) * 64], glb[:, :w])
                g32 = f64.next()
                fw.evac(g32[:, :w], pg[0:64, :w])
                kkr = f64.next()
                fw.ts(kkr[:, :w], k32[:, :w], self.vc(C_KK + h, rows=64), None, op0=ALU.mult)
                sqk = h64.next()
                fw.act(sqk[:, :w], kkr[:, :w], AF.Square)
                pss = self.psR.next()
                fw.mm(pss[0:64, :w], self.ones_bf[0:64, 0:64], sqk[:, :w])
                rn = f64.next()
                fw.act(rn[:, :w], pss[0:64, :w], AF.Ln, bias=self.epscol(1e-24)[0:64])
                fw.act(rn[:, :w], rn[:, :w], AF.Exp, scale=-0.5)
                kk = kkr
                fw.tt(kk[:, :w], kkr[:, :w], rn[:, :w], ALU.mult)
                kmod = f64.next()
                fw.ts(kmod[:, :w], alpha[:, :w], self.vc(C_KA + h, rows=64), self.dcols[0:64, 15 + h:16 + h],
                      op0=ALU.mult, op1=ALU.add)
                fw.tt(kmod[:, :w], kmod[:, :w], k32[:, :w], ALU.mult)
                gam = f64.next()
                fw.scan(gam[:, :w], self.rm[0:64, :w], logw[:, :w], 0.0, ALU.mult, ALU.add)
                eg = f64.next()
                fw.act(eg[:, :w], gam[:, :w], AF.Exp)
                eng = f64.next()
                fw.act(eng[:, :w], gam[:, :w], AF.Exp, scale=-1.0)
                egm = f64.next()
                fw.tt(egm[:, :w], gam[:, :w], logw[:, :w], ALU.subtract)
                fw.act(egm[:, :w], egm[:, :w], AF.Exp)
                rt = h64.next()
                fw.tt(rt[:, :w], r32[:, :w], eg[:, :w], ALU.mult)
                kt32 = f64.next()
                fw.tt(kt32[:, :w], kmod[:, :w], eng[:, :w], ALU.mult)
                ktb = h64.next()
                fw.copy(ktb[:, :w], kt32[:, :w], eng="pool")
                bt32 = f64.next()
                fw.tt(bt32[:, :w], kk[:, :w], alpha[:, :w], ALU.mult)
                fw.tt(bt32[:, :w], bt32[:, :w], eng[:, :w], ALU.mult)
                btb = h64.next()
                fw.copy(btb[:, :w], bt32[:, :w], eng="pool")
                atb = h64.next()
                fw.stt(atb[:, :w], kk[:, :w], -1.0, egm[:, :w], ALU.mult, ALU.mult)
                rk = h64.next()
                fw.stt(rk[:, :w], r32[:, :w], self.vc(C_RK + h, rows=64), kmod[:, :w], ALU.mult, ALU.mult)
                pb = self.psR.next()
                fw.mm(pb[0:64, :w], self.ones_bf[0:64, 0:64], rk[:, :w])
                bon = f64.next()
                fw.tt(bon[:, :w], pb[0:64, :w], v32[:, :w], ALU.mult)
                y32 = f64.next()
                for c in range(nck):
                    cs = slice(c * 128, (c + 1) * 128)
                    gend = eg[:, c * 128 + 127:c * 128 + 128]
                    pt = self.psR.next()
                    fw.transpose(pt[:, 0:64], v32[:, cs], self.ident[0:64, 0:64])
                    vt = b64.next()
                    fw.evac(vt, pt[:, 0:64])
                    kh = s128.next()
                    fw.ts(kh[0:64, :], kt32[:, cs], gend, None, op0=ALU.mult)
                    pt = self.psR.next()
                    fw.transpose(pt[:, 0:64], kh[0:64, :], self.ident[0:64, 0:64])
                    kht = b64.next()
                    fw.evac(kht, pt[:, 0:64])
                    bh = s128.next()
                    fw.ts(bh[0:64, :], bt32[:, cs], gend, None, op0=ALU.mult)
                    pt = self.psR.next()
                    fw.transpose(pt[:, 0:64], bh[0:64, :], self.ident[0:64, 0:64])
                    bht = b64.next()
                    fw.evac(bht, pt[:, 0:64])
                    pn = self.psR.next()
                    fw.mm(pn[:, 0:128], btb[:, cs], atb[:, cs])
                    fw.mm(pn[:, 128:256], atb[:, cs], btb[:, cs])
                    fw.mm(pn[:, 256:384], ktb[:, cs], atb[:, cs])
                    N = b128.next()
                    fw.tt(N, pn[:, 0:128], self.tri_strict32, ALU.mult)
                    NT = b128.next()
                    fw.tt(NT, pn[:, 128:256], self.tri_sl32, ALU.mult)
                    AakT = b128.next()
                    fw.tt(AakT, pn[:, 256:384], self.tri_strict32, ALU.mult)
                    pr = self.psR.next()
                    fw.mm(pr[:, 0:128], ktb[:, cs], rt[:, cs])
                    fw.mm(pr[:, 128:256], btb[:, cs], rt[:, cs])
                    ArkT = b128.next()
                    fw.tt(ArkT, pr[:, 0:128], self.tri_incl32, ALU.mult)
                    ArbT = b128.next()
                    fw.tt(ArbT, pr[:, 128:256], self.tri_incl32, ALU.mult)
                    P = b128.next()
                    fw.tt(P, N, self.ident_bf, ALU.add)
                    for lev in range(6):
                        pq = self.psR.next()
                        fw.mm(pq[:, 128:256], N, NT)
                        if lev < 5:
                            fw.mm(pq[:, 0:128], NT, N)
                            N2 = b128.next()
                            fw.evac(N2, pq[:, 0:128])
                        NT2 = b128.next()
                        fw.evac(NT2, pq[:, 128:256])
                        pp = self.psR.next()
                        fw.mm(pp[:, 0:128], NT2, P)
                        P2 = b128.next()
                        fw.tt(P2, P, pp[:, 0:128], ALU.add)
                        P = P2
                        NT = NT2
                        if lev < 5:
                            N = N2
                    MT = P
                    p0 = self.psR.next()
                    fw.mm(p0[:, 0:64], atb[:, cs], Tbf[h], start=True, stop=False)
                    fw.mm(p0[:, 0:64], AakT, vt, start=False, stop=True)
                    rhs0 = b64.next()
                    fw.evac(rhs0, p0[:, 0:64])
                    pu = self.psR.next()
                    fw.mm(pu[:, 0:64], MT, rhs0)
                    U = b64.next()
                    fw.evac(U, pu[:, 0:64])
                    py = self.psR.next()
                    fw.mm(py[0:64, 0:128], Tbf[h], rt[:, cs], start=True, stop=False)
                    fw.mm(py[0:64, 0:128], vt, ArkT, start=False, stop=False)
                    fw.mm(py[0:64, 0:128], U, ArbT, start=False, stop=True)
                    fw.evac(y32[:, cs], py[0:64, 0:128])
                    pT = self.psR.next()
                    fw.mm(pT[0:64, 0:64], kht, vt, start=True, stop=False)
                    fw.mm(pT[0:64, 0:64], bht, U, start=False, stop=True)
                    fw.stt(T32[h], T32[h], gend, pT[0:64, 0:64], ALU.mult, ALU.add)
                    fw.copy(Tbf[h], T32[h], eng="pool")
                ybf = h64.next()
                fw.copy(ybf[:, :w], y32[:, :w], eng="pool")
                pm = self.psR.next()
                fw.mm(pm[0:64, :w], self.ones_bf[0:64, 0:64], ybf[:, :w])
                yc = f64.next()
                fw.stt(yc[:, :w], pm[0:64, :w], -1.0 / 64.0, y32[:, :w], ALU.mult, ALU.add)
                sq = h64.next()
                fw.act(sq[:, :w], yc[:, :w], AF.Square)
                pv = self.psR.next()
                fw.mm(pv[0:64, :w], self.ones_bf[0:64, 0:64], sq[:, :w])
                rs = f64.next()
                self.rstd_from_ss(rs[:, :w], pv[0:64, :w], 64.0, 64e-5, rows=64)
                fw.tt(yc[:, :w], yc[:, :w], rs[:, :w], ALU.mult)
                fw.ts(yc[:, :w], yc[:, :w], self.vc(C_LNW + h, rows=64), self.vc(C_LNB + h, rows=64),
                      op0=ALU.mult, op1=ALU.add)
                fw.tt(yc[:, :w], yc[:, :w], bon[:, :w], ALU.add)
                yo = h64.next()
                fw.tt(yo[:, :w], yc[:, :w], g32[:, :w], ALU.mult)
                fw.dma(self.yT[256 + h * 64:256 + (h + 1) * 64, t0:t0 + w].fresh(), yo[:, :w], q="sp")

    def phase_C1(self, l, es):
        fw, Lp = self.fw, self.Lp
        TS = 1536
        n = fw.sb([128, 8, TS], BF16, es)
        y = fw.sb([128, 8, TS], BF16, es)
        mg = fw.sb([128, 8, TS], BF16, es)
        acc = fw.sb([128, TS], F32, es)
        nTv = self.nT.rr("(k p) t -> p k t", p=128)
        yTv = self.yT.rr("(k p) t -> p k t", p=128)
        wl = self.w_in[l]
        for (s0, sw) in chunks(0, Lp, TS):
            for (t0, w) in chunks(0, sw):
                fw.dma(n[:, :, t0:t0 + w], nTv[:, :, s0 + t0:s0 + t0 + w].fresh())
                fw.dma(y[:, :, t0:t0 + w], yTv[:, :, s0 + t0:s0 + t0 + w].fresh())
            for d in range(8):
                for m in range(4):
                    c0 = OFF_GATE + m * 1024 + d * 128
                    wg = self.load_w(wl[:, c0:c0 + 128], 8, 128)
                    wb = self.load_w(self.w_branch[l][m][:, d * 128:(d + 1) * 128], 2, 128)
                    for (t0, w) in chunks(0, sw):
                        pg = self.psR.next()
                        for k in range(8):
                            fw.mm(pg[:, :w], wg[:, k, :], n[:, k, t0:t0 + w], start=(k == 0), stop=(k == 7))
                        pb = self.psR.next()
                        for k in range(2):
                            fw.mm(pb[:, :w], wb[:, k, :], y[:, 2 * m + k, t0:t0 + w], start=(k == 0), stop=(k == 1))
                        gt = self.t32.next()
                        fw.act(gt[:, :w], pg[:, :w], AF.Sigmoid, bias=self.vc(C_GATEB + m * 8 + d))
                        if m == 0:
                            fw.tt(acc[:, t0:t0 + w], gt[:, :w], pb[:, :w], ALU.mult)
                        else:
                            fw.tt(gt[:, :w], gt[:, :w], pb[:, :w], ALU.mult)
                            if m < 3:
                                fw.tt(acc[:, t0:t0 + w], acc[:, t0:t0 + w], gt[:, :w], ALU.add, eng="pool")
                            else:
                                fw.tt(mg[:, d, t0:t0 + w], acc[:, t0:t0 + w], gt[:, :w], ALU.add, eng="pool")
            for d in range(8):
                wo = self.load_w(self.w_out[l][:, d * 128:(d + 1) * 128], 8, 128)
                for (t0, w) in chunks(0, sw):
                    g0 = s0 + t0
                    po = self.psR.next()
                    for k in range(8):
                        fw.mm(po[:, :w], wo[:, k, :], mg[:, k, t0:t0 + w], start=(k == 0), stop=(k == 7))
                    lo = PADC if g0 == 0 else 0
                    hc = self.t32.next()
                    cell = self.hcell(d, g0 + lo, w - lo)
                    fw.dma(hc[:, lo:w], cell)
                    fw.tt(hc[:, lo:w], hc[:, lo:w], po[:, lo:w], ALU.add)
                    fw.dma(cell, hc[:, lo:w], q="sp")

    def phase_C2(self, l, es):
        fw, Lp = self.fw, self.Lp
        TS = 1536
        n2 = fw.sb([128, 8, TS], BF16, es)
        g = fw.sb([128, 22, TS], BF16, es)
        carry = fw.sb([128, 22, 2], F32, es)
        fw.memset(carry, 0.0)
        self.h8 = Rot([fw.sb([128, 8, 512], F32, es) for _ in range(1)])
        self.sq8 = Rot([fw.sb([128, 8, 512], BF16, es) for _ in range(1)])
        asb = Rot([fw.sb([128, 514], F32, es) for _ in range(3)])
        wfi = self.w_ffn_in[l]
        for (s0, sw) in chunks(0, Lp, TS):
            for (t0, w) in chunks(0, sw):
                hc = self.h8.next()
                fw.dma(hc[:, :, :w], self.hall(s0 + t0, w))
                self.rmsnorm_chunk(hc, w, C_NFFN, self.vecs, lambda k: n2[:, k, t0:t0 + w])
            for fc in range(22):
                wa = self.load_w(wfi[:, fc * 128:(fc + 1) * 128], 8, 128)
                wu = self.load_w(wfi[:, DFF + fc * 128:DFF + (fc + 1) * 128], 8, 128)
                for (t0, w) in chunks(0, sw):
                    pa = self.psR.next()
                    for k in range(8):
                        fw.mm(pa[:, :w], wa[:, k, :], n2[:, k, t0:t0 + w], start=(k == 0), stop=(k == 7))
                    pu = self.psR.next()
                    for k in range(8):
                        fw.mm(pu[:, :w], wu[:, k, :], n2[:, k, t0:t0 + w], start=(k == 0), stop=(k == 7))
                    a = asb.next()
                    fw.copy(a[:, 0:2], carry[:, fc, :], eng="pool")
                    fw.copy(a[:, 2:2 + w], pa[:, :w], eng="act")
                    fw.copy(carry[:, fc, :], a[:, w:w + 2], eng="pool")
                    c = self.t32.next()
                    fw.ts(c[:, :w], a[:, 0:w], self.vc(C_CONVW + fc), self.vc(C_CONVB + fc), op0=ALU.mult, op1=ALU.add)
                    fw.stt(c[:, :w], a[:, 1:1 + w], self.vc(C_CONVW + 22 + fc), c[:, :w], ALU.mult, ALU.add)
                    fw.stt(c[:, :w], a[:, 2:2 + w], self.vc(C_CONVW + 44 + fc), c[:, :w], ALU.mult, ALU.add)
                    fw.act(c[:, :w], c[:, :w], AF.Silu)
                    fw.tt(g[:, fc, t0:t0 + w], c[:, :w], pu[:, :w], ALU.mult)
            for d in range(8):
                wo = self.load_w(self.w_ffn_out[l][:, d * 128:(d + 1) * 128], 22, 128)
                for (t0, w) in chunks(0, sw):
                    g0 = s0 + t0
                    po = self.psR.next()
                    for k in range(22):
                        fw.mm(po[:, :w], wo[:, k, :], g[:, k, t0:t0 + w], start=(k == 0), stop=(k == 21))
                    lo = PADC if g0 == 0 else 0
                    hc = self.t32.next()
                    cell = self.hcell(d, g0 + lo, w - lo)
                    fw.dma(hc[:, lo:w], cell)
                    fw.tt(hc[:, lo:w], hc[:, lo:w], po[:, lo:w], ALU.add)
                    fw.dma(cell, hc[:, lo:w], q="sp")

    def phase_F(self, es):
        fw, Lp = self.fw, self.Lp
        self.h8 = Rot([fw.sb([128, 8, 512], F32, es) for _ in range(2)])
        self.sq8 = Rot([fw.sb([128, 8, 512], BF16, es) for _ in range(2)])
        o8 = Rot([fw.sb([128, 8, 512], F32, es) for _ in range(2)])
        ov = self.outT.rr("(k p) t -> p k t", p=128)
        for (t0, w) in chunks(0, Lp):
            hc = self.h8.next()
            fw.dma(hc[:, :, :w], self.hall(t0, w))
            o = o8.next()
            self.rmsnorm_chunk(hc, w, 0, self.gvec, lambda k: o[:, k, :w])
            lo = 128 if t0 == 0 else 0
            if w - lo > 0:
                fw.dma(ov[:, :, t0 + lo - 128:t0 + w - 128].fresh(), o[:, :, lo:w], q="sp")


def _cols(v, p=128):
    a = np.asarray(v, np.float32).reshape(-1, p).T
    if p < 128:
        a = np.concatenate([a, np.zeros((128 - p, a.shape[1]), np.float32)], 0)
    return a


def pack_vecs(inp, l):
    out = np.zeros((128, NV), np.float32)

    def put(c, a):
        out[:, c:c + a.shape[1]] = a
    put(C_NMIX, _cols(inp["norm_mix"][l]))
    put(C_NFFN, _cols(inp["norm_ffn"][l]))
    put(C_GATEB, _cols(inp["gate_b"][l].reshape(-1)))
    put(C_CONVW, _cols(inp["ffn_conv_w"][l].reshape(-1)))
    put(C_CONVB, _cols(inp["ffn_conv_b"][l]))
    put(C_QN, _cols(inp["mla_q_norm"][l]))
    put(C_KVN, _cols(inp["mla_kv_norm"][l]))
    mu = inp["rw_mu"][l]
    put(C_MURKV, _cols(mu[0:768], 64))
    put(C_MUWL, _cols(mu[768:832], 64))
    put(C_MUAL, _cols(mu[832:896], 64))
    put(C_MUGL, _cols(mu[896:1024]))
    put(C_W0, _cols(inp["rw_w0"][l], 64))
    put(C_A0, _cols(inp["rw_a0"][l], 64))
    put(C_KK, _cols(inp["rw_k_k"][l], 64))
    put(C_KA, _cols(inp["rw_k_a"][l], 64))
    put(C_RK, _cols(inp["rw_r_k"][l].reshape(-1), 64))
    put(C_LNW, _cols(inp["rw_ln_w"][l], 64))
    put(C_LNB, _cols(inp["rw_ln_b"][l], 64))
    put(C_GAB, _cols(inp["gla_a_b"][l]))
    put(C_GNORM, _cols(inp["gla_norm"][l], 64))
    return out


def rope_table(Lp):
    half = 16
    freqs = (np.float32(10000.0) ** (-np.arange(half, dtype=np.float32) / np.float32(half))).astype(np.float32)
    pos = (np.arange(Lp) - PADC).astype(np.float32)
    ang = (pos[None, :] * freqs[:, None]).astype(np.float32)
    c, s = np.cos(ang).astype(np.float32), np.sin(ang).astype(np.float32)
    return np.concatenate([c, c, -s, s], 0).astype(np.float32)


_CACHE = {}


def run(inputs, NB, DEPTH, dbg=(), n_cores=8, phases=None):
    key = (NB, DEPTH, tuple(dbg), phases)
    if key not in _CACHE:
        _CACHE[key] = Builder(NB, DEPTH, dbg, phases).build()
    nc = _CACHE[key]
    Lp = NB * 128
    x = np.asarray(inputs["x"], np.float32)
    B = x.shape[0]
    meta = np.asarray(inputs["meta_tokens"], np.float32)
    shared = {
        "w_in": np.ascontiguousarray(inputs["w_in"][:DEPTH], np.float32),
        "mla_w_uq": np.ascontiguousarray(inputs["mla_w_uq"][:DEPTH], np.float32),
        "mla_w_ukv": np.ascontiguousarray(inputs["mla_w_ukv"][:DEPTH], np.float32),
        "rw_w2": np.ascontiguousarray(inputs["rw_w2"][:DEPTH], np.float32),
        "rw_a2": np.ascontiguousarray(inputs["rw_a2"][:DEPTH], np.float32),
        "rw_g2": np.ascontiguousarray(inputs["rw_g2"][:DEPTH], np.float32),
        "gla_a2": np.ascontiguousarray(inputs["gla_a2"][:DEPTH], np.float32),
        "w_branch": np.ascontiguousarray(inputs["w_branch"][:DEPTH], np.float32),
        "w_out": np.ascontiguousarray(inputs["w_out"][:DEPTH], np.float32),
        "w_ffn_in": np.ascontiguousarray(inputs["w_ffn_in"][:DEPTH], np.float32),
        "w_ffn_out": np.ascontiguousarray(inputs["w_ffn_out"][:DEPTH], np.float32),
        "vecs": np.stack([pack_vecs(inputs, l) for l in range(DEPTH)], 0),
        "gvec": _cols(inputs["norm_final"]),
        "rope": rope_table(Lp),
    }
    in_maps = []
    for c in range(n_cores):
        b = c % B
        hT0 = np.zeros((1024, Lp), np.float32)
        hT0[:, PADC:PADC + 16] = meta.T
        hT0[:, 128:] = x[b].T
        m = dict(shared)
        m["hT0"] = hT0
        in_maps.append(m)
    res = run_bass_kernel_spmd(nc, in_maps, core_ids=list(range(n_cores)))
    return res.results


def kernel(**inputs):
    x = np.asarray(inputs["x"])
    B, SEQ, D = x.shape
    NB = (SEQ + 128) // 128
    results = run(inputs, NB, 4)
    out = np.stack([np.ascontiguousarray(results[b]["outT"].T) for b in range(B)], 0)
    return out.astype(np.float32)
```

```python
import math
import numpy as np
from contextlib import ExitStack
import concourse.bass as bass
import concourse.mybir as mybir
from concourse.bass_utils import run_bass_kernel_spmd

F32 = mybir.dt.float32
BF16 = mybir.dt.bfloat16
ALU = mybir.AluOpType
AF = mybir.ActivationFunctionType
AX = mybir.AxisListType

NDMA = 24
EPS = 1e-6
PADC = 112
NMIX = 2992
OFF_CQ, OFF_CKV, OFF_KR, OFF_RW, OFF_SB, OFF_GLA, OFF_GATE = 0, 256, 384, 416, 1440, 2208, 2992
DFF = 2816
NV = 192
(C_NMIX, C_NFFN, C_GATEB, C_CONVW, C_CONVB, C_QN, C_KVN, C_MURKV, C_MUWL, C_MUAL, C_MUGL,
 C_W0, C_A0, C_KK, C_KA, C_RK, C_LNW, C_LNB, C_GAB, C_GNORM) = (
    0, 8, 16, 48, 114, 136, 138, 139, 151, 152, 153, 154, 158, 162, 166, 170, 174, 178, 182, 183)


class Res:
    __slots__ = ("w", "r", "excl")

    def __init__(self, excl=False):
        self.w = {}
        self.r = {}
        self.excl = excl


def _rl(v):
    r = v.res
    return r if isinstance(r, tuple) else (r,)


class V:
    __slots__ = ("ap", "res")

    def __init__(self, ap, res=None):
        self.ap = ap
        self.res = res if res is not None else Res()

    def __getitem__(self, idx):
        return V(self.ap[idx], self.res)

    def rr(self, pat, **kw):
        return V(self.ap.rearrange(pat, **kw), self.res)

    def fresh(self):
        return V(self.ap, Res())

    def withres(self, res):
        return V(self.ap, res)


class _Eng:
    def __init__(self, name, obj, sem):
        self.name, self.obj, self.sem = name, obj, sem
        self.n = 0
        self.waited = {}
        self.pending = False


class Rot:
    def __init__(self, items):
        self.items = items
        self.i = 0

    def next(self):
        x = self.items[self.i]
        self.i = (self.i + 1) % len(self.items)
        return x


class FW:
    def __init__(self, nc, es):
        self.nc = nc
        self.es = es
        self.engs = {}
        for name, obj in (("pe", nc.tensor), ("act", nc.scalar), ("dve", nc.vector),
                          ("pool", nc.gpsimd), ("sp", nc.sync)):
            sem = es.enter_context(nc.semaphore("s_" + name))
            self.engs[name] = _Eng(name, obj, sem)
        self.dma_sems = [es.enter_context(nc.semaphore("d%d" % i)) for i in range(NDMA)]
        self.dma_cnt = [0] * NDMA
        self.dma_next = 0
        self.nins = 0
        self._uid = 0
        self._ev = 0

    def sb(self, shape, dt=F32, es=None):
        self._uid += 1
        t = (es or self.es).enter_context(self.nc.sbuf_tensor("sb%d" % self._uid, list(shape), dt))
        return V(t[:], Res())

    def ps(self, shape, dt=F32, es=None):
        self._uid += 1
        t = (es or self.es).enter_context(self.nc.psum_tensor("ps%d" % self._uid, list(shape), dt))
        return V(t[:], Res(excl=True))

    def _wait(self, eng, tok):
        key, sem, val, src = tok
        if src == "pe" and eng.name == "pe":
            return
        if eng.waited.get(key, 0) >= val:
            return
        eng.obj.wait_ge(sem, val)
        eng.waited[key] = val
        self.nins += 1

    def _deps(self, reads, writes):
        toks = []
        for v in reads:
            for r in _rl(v):
                toks.extend(r.w.values())
                if r.excl:
                    toks.extend(t for t in r.r.values() if t[3] != self._cur)
        for v in writes:
            for r in _rl(v):
                toks.extend(r.w.values())
                toks.extend(r.r.values())
        return toks

    def _mark(self, key, tok, reads, writes):
        wres = []
        for v in writes:
            for r in _rl(v):
                r.w = {key: tok}
                r.r = {}
                wres.append(r)
        for v in reads:
            for r in _rl(v):
                if r not in wres:
                    r.r[key] = tok

    def op(self, engname, fn, reads, writes, inc=True):
        eng = self.engs[engname]
        self._cur = engname
        for t in self._deps(reads, writes):
            self._wait(eng, t)
        ins = fn(eng.obj)
        self.nins += 1
        if inc:
            eng.n += 1
            ins.then_inc(eng.sem, 1)
            tok = (engname, eng.sem, eng.n, engname)
            eng.pending = False
        else:
            tok = (engname, eng.sem, eng.n + 1, engname)
            eng.pending = True
        self._mark(engname, tok, reads, writes)
        return ins

    def dma(self, out, in_, q="sp"):
        eng = self.engs[q]
        self._cur = q
        for t in self._deps([in_], [out]):
            self._wait(eng, t)
        i = self.dma_next
        self.dma_next = (i + 1) % NDMA
        key = ("dma", i)
        if self.dma_cnt[i] > 0:
            self._wait(eng, (key, self.dma_sems[i], self.dma_cnt[i], None))
        self.dma_cnt[i] += 16
        eng.obj.dma_start(out=out.ap, in_=in_.ap).then_inc(self.dma_sems[i], 16)
        self.nins += 1
        tok = (key, self.dma_sems[i], self.dma_cnt[i], None)
        self._mark(key, tok, [in_], [out])

    def barrier(self, engines=("pe", "act", "dve", "pool", "sp")):
        for en in engines:
            eng = self.engs[en]
            for i in range(NDMA):
                if self.dma_cnt[i] > 0:
                    self._wait(eng, (("dma", i), self.dma_sems[i], self.dma_cnt[i], None))
            for name, e in self.engs.items():
                assert not e.pending, name
                if e.n > 0 and name != en:
                    self._wait(eng, (name, e.sem, e.n, None))

    def mm(self, out, lhsT, rhs, start=True, stop=True):
        return self.op("pe", lambda e: e.matmul(out.ap, lhsT.ap, rhs.ap, start=start, stop=stop),
                       [lhsT, rhs], [out], inc=stop)

    def transpose(self, out, in_, ident):
        return self.op("pe", lambda e: e.transpose(out.ap, in_.ap, ident.ap), [in_, ident], [out])

    def act(self, out, in_, func, bias=None, scale=None):
        reads = [in_]
        kw = {}
        if bias is not None:
            if isinstance(bias, V):
                reads.append(bias)
                kw["bias"] = bias.ap
            else:
                kw["bias"] = bias
        if scale is not None:
            if isinstance(scale, V):
                reads.append(scale)
                kw["scale"] = scale.ap
            else:
                kw["scale"] = scale
        return self.op("act", lambda e: e.activation(out.ap, in_.ap, func, **kw), reads, [out])

    def tt(self, out, in0, in1, op, eng="dve"):
        return self.op(eng, lambda e: e.tensor_tensor(out.ap, in0.ap, in1.ap, op), [in0, in1], [out])

    def ts(self, out, in0, s1, s2=None, op0=ALU.mult, op1=None, eng="dve"):
        reads = [in0]
        a1 = s1
        if isinstance(s1, V):
            reads.append(s1)
            a1 = s1.ap
        a2 = s2
        if isinstance(s2, V):
            reads.append(s2)
            a2 = s2.ap
        kw = {}
        if op1 is not None:
            kw["op1"] = op1
        return self.op(eng, lambda e: e.tensor_scalar(out.ap, in0.ap, a1, a2, op0, **kw), reads, [out])

    def stt(self, out, in0, scalar, in1, op0, op1, eng="dve"):
        reads = [in0, in1]
        a = scalar
        if isinstance(scalar, V):
            reads.append(scalar)
            a = scalar.ap
        return self.op(eng, lambda e: e.scalar_tensor_tensor(out.ap, in0.ap, a, in1.ap, op0, op1), reads, [out])

    def copy(self, out, in_, eng="dve"):
        if eng == "act":
            return self.act(out, in_, AF.Copy)
        return self.op(eng, lambda e: e.tensor_copy(out.ap, in_.ap), [in_], [out])

    def evac(self, out, in_):
        self._ev ^= 1
        return self.copy(out, in_, eng="act" if self._ev else "dve")

    def memset(self, out, val, eng="dve"):
        return self.op(eng, lambda e: e.memset(out.ap, val), [], [out])

    def scan(self, out, d0, d1, init, op0, op1):
        return self.op("dve", lambda e: e.tensor_tensor_scan(out.ap, d0.ap, d1.ap, init, op0, op1), [d0, d1], [out])

    def recip(self, out, in_):
        return self.op("dve", lambda e: e.reciprocal(out.ap, in_.ap), [in_], [out])

    def aselect(self, t, cmp, fill, base, pattern, cm):
        return self.op("pool", lambda g: g.affine_select(out=t.ap, in_=t.ap, compare_op=cmp, fill=fill, base=base,
                                                         pattern=pattern, channel_multiplier=cm), [t], [t])


def chunks(t0, t1, maxw=512):
    out = []
    t = t0
    while t < t1:
        w = min(maxw, t1 - t)
        out.append((t, w))
        t += w
    return out


class Builder:
    def __init__(self, NB, DEPTH, dbg=(), phases=None):
        self.phases = phases
        self.NB = NB
        self.Lp = NB * 128
        self.SEQ = self.Lp - 128
        self.DEPTH = DEPTH
        self.dbg = dbg
        self.nc = bass.Bass("TRN2", target_bir_lowering=False)

    def din(self, name, shape, dt=F32):
        return V(self.nc.dram_tensor(name, list(shape), dt, kind="ExternalInput").ap())

    def dscr(self, name, shape, dt=F32):
        kind = "ExternalOutput" if name in self.dbg else "Internal"
        return V(self.nc.dram_tensor(name, list(shape), dt, kind=kind).ap())

    def build(self):
        nc, Lp, DEPTH = self.nc, self.Lp, self.DEPTH
        self.hT0 = self.din("hT0", [1024, Lp])
        self.w_in = self.din("w_in", [DEPTH, 1024, 7088])
        self.w_uq = self.din("mla_w_uq", [DEPTH, 256, 384])
        self.w_ukv = self.din("mla_w_ukv", [DEPTH, 128, 512])
        self.rw_w2 = self.din("rw_w2", [DEPTH, 64, 256])
        self.rw_a2 = self.din("rw_a2", [DEPTH, 64, 256])
        self.rw_g2 = self.din("rw_g2", [DEPTH, 128, 256])
        self.gla_a2 = self.din("gla_a2", [DEPTH, 16, 128])
        self.w_branch = self.din("w_branch", [DEPTH, 4, 256, 1024])
        self.w_out = self.din("w_out", [DEPTH, 1024, 1024])
        self.w_ffn_in = self.din("w_ffn_in", [DEPTH, 1024, 5632])
        self.w_ffn_out = self.din("w_ffn_out", [DEPTH, 2816, 1024])
        self.vecs_d = self.din("vecs", [DEPTH, 128, NV])
        self.gvec_d = self.din("gvec", [128, 8])
        self.rope_d = self.din("rope", [64, Lp])
        self.outT = V(nc.dram_tensor("outT", [1024, self.SEQ], F32, kind="ExternalOutput").ap())
        self.hT = self.dscr("hT", [1024, Lp])
        self.nT = self.dscr("nT", [1024, Lp], BF16)
        self.zT = self.dscr("zT", [NMIX, Lp])
        self.yT = self.dscr("yT", [1024, Lp], BF16)
        nch = (Lp + 511) // 512
        self.hres = [[Res() for _ in range(nch)] for _ in range(8)]

        with ExitStack() as es:
            self.fw = fw = FW(nc, es)
            self.consts(es)
            for k in range(8):
                fw.dma(self.hT[k * 128:(k + 1) * 128, :].withres(tuple(self.hres[k])),
                       self.hT0[k * 128:(k + 1) * 128, :])
            fw.barrier()
            print("sbuf remaining after consts:", nc.sbuf_bytes_remaining)
            for l in range(DEPTH):
                self.layer_setup(l)
                for nm, fn in (("A", self.phase_A), ("mla", self.mla), ("sb", self.sbatt), ("gla", self.gla),
                               ("rwkv", self.rwkv), ("C1", self.phase_C1), ("C2", self.phase_C2)):
                    if self.phases is not None and nm not in self.phases:
                        continue
                    with ExitStack() as e2, nc.named_scope("L%d_%s" % (l, nm)):
                        fn(l, e2)
                        fw.barrier()
            with ExitStack() as e2:
                self.phase_F(e2)
            fw.barrier()
            print("instructions:", fw.nins, {k: e.n for k, e in fw.engs.items()})
        return nc

    def hcell(self, k, t0, w):
        c0, c1 = t0 // 512, (t0 + w - 1) // 512
        res = tuple(self.hres[k][c] for c in range(c0, c1 + 1))
        return self.hT[k * 128:(k + 1) * 128, t0:t0 + w].withres(res if len(res) > 1 else res[0])

    def hall(self, t0, w):
        c0, c1 = t0 // 512, (t0 + w - 1) // 512
        res = tuple(self.hres[k][c] for k in range(8) for c in range(c0, c1 + 1))
        return self.hT.rr("(k p) t -> p k t", p=128)[:, :, t0:t0 + w].withres(res)

    def consts(self, es):
        fw = self.fw
        Lp = self.Lp
        self.ident = fw.sb([128, 128], F32, es)
        fw.memset(self.ident, 0.0, eng="pool")
        fw.aselect(self.ident, ALU.not_equal, 1.0, 0, [[-1, 128]], 1)
        self.ident_bf = fw.sb([128, 128], BF16, es)
        fw.copy(self.ident_bf, self.ident)
        self.ones_bf = fw.sb([128, 128], BF16, es)
        fw.memset(self.ones_bf, 1.0)
        self.zeros_bf = fw.sb([128, 128], BF16, es)
        fw.memset(self.zeros_bf, 0.0)
        padcol = fw.sb([128, 1], F32, es)
        fw.memset(padcol, 1.0, eng="pool")
        fw.aselect(padcol, ALU.is_ge, 0.0, -PADC, [[0, 1]], 1)
        self.ones_pad = fw.sb([128, 128], BF16, es)
        fw.ts(self.ones_pad, self.ones_bf, padcol[:, 0:1], None, op0=ALU.mult)

        def tri(cmp, base, pat, cm):
            t32 = fw.sb([128, 128], F32, es)
            fw.memset(t32, 1.0, eng="pool")
            fw.aselect(t32, cmp, 0.0, base, pat, cm)
            tb = fw.sb([128, 128], BF16, es)
            fw.copy(tb, t32)
            return t32, tb
        self.tri_incl32, self.tri_incl = tri(ALU.is_ge, 0, [[1, 128]], -1)
        self.tri_strict32, self.tri_strict = tri(ALU.is_gt, 0, [[1, 128]], -1)
        self.tri_sl32, self.tri_sl = tri(ALU.is_gt, 0, [[-1, 128]], 1)
        self.tri_ge32, self.uincl = tri(ALU.is_ge, 0, [[-1, 128]], 1)
        self.uincl_pad = fw.sb([128, 128], BF16, es)
        fw.ts(self.uincl_pad, self.uincl, padcol[:, 0:1], None, op0=ALU.mult)
        self.tri_incl4 = fw.sb([128, 4, 128], F32, es)
        for h in range(4):
            fw.copy(self.tri_incl4[:, h, :], self.tri_incl32)
        self.rm = fw.sb([128, 512], F32, es)
        fw.memset(self.rm, 1.0)
        for k in range(4):
            fw.memset(self.rm[:, k * 128:k * 128 + 1], 0.0)
        self.headmask = fw.sb([128, 4], F32, es)
        fw.memset(self.headmask, 1.0, eng="pool")
        fw.aselect(self.headmask, ALU.is_ge, 0.0, 0, [[-32, 4]], 1)
        fw.aselect(self.headmask, ALU.is_ge, 0.0, 31, [[32, 4]], -1)
        self.bdmask = fw.sb([128, 256], F32, es)
        fw.memset(self.bdmask, 1.0, eng="pool")
        bd3 = self.bdmask.rr("p (h c) -> p h c", h=4)
        fw.aselect(bd3, ALU.is_ge, 0.0, 0, [[-32, 4], [0, 64]], 1)
        fw.aselect(bd3, ALU.is_ge, 0.0, 31, [[32, 4], [0, 64]], -1)
        self.vecs = fw.sb([128, NV], F32, es)
        self.gvec = fw.sb([128, 8], F32, es)
        fw.dma(self.gvec, self.gvec_d)
        self.dcols = fw.sb([128, 32], F32, es)
        self._eps = {}
        for ev in (EPS, 1.0, 1e-24, 64e-5):
            t = fw.sb([128, 1], F32, es)
            fw.memset(t, ev)
            self._eps[ev] = t
        banks = [fw.ps([128, 512], F32, es) for _ in range(8)]
        self.banks = banks
        self.psA = banks[0:3]
        self.psR = Rot(banks[3:8])
        self.wst = Rot([fw.sb([128, 1408], F32, es) for _ in range(2)])
        self.wbf = Rot([fw.sb([128, 2816], BF16, es) for _ in range(3)])
        self.t32 = Rot([fw.sb([128, 512], F32, es) for _ in range(6)])
        self.tbf = Rot([fw.sb([128, 512], BF16, es) for _ in range(6)])

    def vc(self, c, n=1, rows=128):
        return self.vecs[0:rows, c:c + n]

    def layer_setup(self, l):
        fw = self.fw
        fw.dma(self.vecs, self.vecs_d[l])
        d = self.dcols
        fw.ts(d[:, 0:15], self.vecs[:, C_MURKV:C_MURKV + 15], -1.0, 1.0, op0=ALU.mult, op1=ALU.add)
        fw.ts(d[:, 15:19], self.vecs[:, C_KA:C_KA + 4], -1.0, 1.0, op0=ALU.mult, op1=ALU.add)
        fw.ts(d[:, 19:20], self.vecs[:, C_GAB:C_GAB + 1], -1.0, None, op0=ALU.mult)

    def load_w(self, src2d, kc, ow, prows=128):
        fw = self.fw
        wb = self.wbf.next()[0:prows, 0:kc * ow].rr("p (k o) -> p k o", k=kc)
        srcv = src2d.rr("(k p) o -> p k o", p=prows)
        kmax = max(1, 1408 // ow)
        k0 = 0
        while k0 < kc:
            kn = min(kmax, kc - k0)
            st = self.wst.next()[0:prows, 0:kn * ow].rr("p (k o) -> p k o", k=kn)
            fw.dma(st, srcv[:, k0:k0 + kn, :])
            fw.copy(wb[:, k0:k0 + kn, :], st, eng="pool")
            k0 += kn
        return wb

    def rstd_from_ss(self, out, ss_ps, n, eps, rows=128):
        fw = self.fw
        fw.act(out, ss_ps, AF.Ln, bias=self.epscol(eps)[0:rows], scale=1.0 / n)
        fw.act(out, out, AF.Exp, scale=-0.5)

    def epscol(self, eps):
        return self._eps[eps]

    def rmsnorm_chunk(self, hc, w, gcol0, gsrc, out_fn):
        fw = self.fw
        sq = self.sq8.next()
        fw.act(sq[:, :, :w], hc[:, :, :w], AF.Square)
        ps = self.psR.next()
        for k in range(8):
            fw.mm(ps[:, :w], self.ones_bf, sq[:, k, :w], start=(k == 0), stop=(k == 7))
        rstd = self.t32.next()
        self.rstd_from_ss(rstd[:, :w], ps[:, :w], 1024.0, EPS)
        for k in range(8):
            fw.stt(out_fn(k), hc[:, k, :w], gsrc[:, gcol0 + k:gcol0 + k + 1], rstd[:, :w], ALU.mult, ALU.mult)

    def phase_A(self, l, es):
        fw, Lp = self.fw, self.Lp
        n = fw.sb([128, 8, Lp], BF16, es)
        self.h8 = Rot([fw.sb([128, 8, 512], F32, es) for _ in range(2)])
        self.sq8 = Rot([fw.sb([128, 8, 512], BF16, es) for _ in range(2)])
        nTv = self.nT.rr("(k p) t -> p k t", p=128)
        for (t0, w) in chunks(0, Lp):
            hc = self.h8.next()
            fw.dma(hc[:, :, :w], self.hall(t0, w))
            self.rmsnorm_chunk(hc, w, C_NMIX, self.vecs, lambda k: n[:, k, t0:t0 + w])
            fw.dma(nTv[:, :, t0:t0 + w].fresh(), n[:, :, t0:t0 + w], q="sp")
        ocs = [(o, min(128, NMIX - o)) for o in range(0, NMIX, 128)]
        wl = self.w_in[l]
        cur = self.load_w(wl[:, 0:ocs[0][1]], 8, ocs[0][1])
        for i, (o0, ow) in enumerate(ocs):
            nxt = None
            if i + 1 < len(ocs):
                o1, ow1 = ocs[i + 1]
                nxt = self.load_w(wl[:, o1:o1 + ow1], 8, ow1)
            for (t0, w) in chunks(0, Lp):
                ps = self.psR.next()
                for k in range(8):
                    fw.mm(ps[:ow, :w], cur[:, k, :], n[:, k, t0:t0 + w], start=(k == 0), stop=(k == 7))
                ot = self.t32.next()
                fw.evac(ot[:ow, :w], ps[:ow, :w])
                fw.dma(self.zT[o0:o0 + ow, t0:t0 + w].fresh(), ot[:ow, :w], q="sp")
            cur = nxt

    def mla(self, l, es):
        fw, Lp, NB = self.fw, self.Lp, self.NB
        scale = 96.0 ** -0.5
        Q = [fw.sb([96, Lp], BF16, es) for _ in range(4)]
        K = [fw.sb([96, Lp], BF16, es) for _ in range(4)]
        Vt = fw.sb([128, NB, 256], BF16, es)
        self.ropeC = fw.sb([96, Lp], F32, es)
        self.ropeS = fw.sb([96, Lp], F32, es)
        fw.dma(self.ropeC[64:96, :], self.rope_d[0:32, :])
        fw.dma(self.ropeS[64:96, :], self.rope_d[32:64, :])
        wuq_t = self.load_w(self.w_uq[l], 2, 384)
        wuq = fw.sb([128, 2, 384], BF16, es)
        fw.copy(wuq, wuq_t, eng="pool")
        wuq_s = fw.sb([128, 2, 384], BF16, es)
        fw.copy(wuq_s, wuq_t, eng="pool")
        src = self.w_uq[l].rr("(k p) (h c) -> p k h c", p=128, h=4)
        st2 = self.wst.next()[:, 0:256].rr("p (k c) -> p k c", k=2)
        for k in range(2):
            fw.dma(st2[:, k, 0:64].rr("p (h c) -> p h c", h=4), src[:, k, :, 80:96])
            fw.dma(st2[:, k, 64:128].rr("p (h c) -> p h c", h=4), src[:, k, :, 64:80])
        wv = wuq_s.rr("p k (h c) -> p k h c", h=4)
        for k in range(2):
            fw.copy(wv[:, k, :, 64:80], st2[:, k, 0:64].rr("p (h c) -> p h c", h=4), eng="pool")
            fw.copy(wv[:, k, :, 80:96], st2[:, k, 64:128].rr("p (h c) -> p h c", h=4), eng="pool")
        wukv = self.load_w(self.w_ukv[l], 1, 512)
        wkn = fw.sb([128, 4, 64], BF16, es)
        wvv = fw.sb([128, 256], BF16, es)
        wk4 = wukv[:, 0, :].rr("p (h c) -> p h c", h=4)
        fw.copy(wkn, wk4[:, :, 0:64], eng="pool")
        fw.copy(wvv.rr("p (h c) -> p h c", h=4), wk4[:, :, 64:128], eng="pool")
        cq2 = Rot([fw.sb([128, 2, 512], F32, es) for _ in range(2)])
        sq2 = Rot([fw.sb([128, 2, 512], BF16, es) for _ in range(2)])
        cqn = Rot([fw.sb([128, 2, 512], BF16, es) for _ in range(2)])
        kr2 = Rot([fw.sb([96, 2, 512], F32, es) for _ in range(2)])
        krr = Rot([fw.sb([96, 512], BF16, es) for _ in range(2)])
        zcq = self.zT[0:256, :].rr("(k p) t -> p k t", p=128)
        for (t0, w) in chunks(0, Lp):
            c = cq2.next()
            fw.dma(c[:, :, :w], zcq[:, 0:2, t0:t0 + w].fresh())
            ck = self.t32.next()
            fw.dma(ck[:, :w], self.zT[OFF_CKV:OFF_CKV + 128, t0:t0 + w].fresh())
            kr = kr2.next()
            fw.dma(kr[64:96, 0, :w], self.zT[OFF_KR:OFF_KR + 32, t0:t0 + w].fresh())
            fw.dma(kr[64:80, 1, :w], self.zT[OFF_KR + 16:OFF_KR + 32, t0:t0 + w].fresh())
            fw.dma(kr[80:96, 1, :w], self.zT[OFF_KR:OFF_KR + 16, t0:t0 + w].fresh())
            s = sq2.next()
            fw.act(s[:, :, :w], c[:, :, :w], AF.Square)
            ps = self.psR.next()
            for k in range(2):
                fw.mm(ps[:, :w], self.ones_bf, s[:, k, :w], start=(k == 0), stop=(k == 1))
            rstd = self.t32.next()
            self.rstd_from_ss(rstd[:, :w], ps[:, :w], 256.0, EPS)
            cn = cqn.next()
            for k in range(2):
                fw.stt(cn[:, k, :w], c[:, k, :w], self.vc(C_QN + k), rstd[:, :w], ALU.mult, ALU.mult)
            s2 = self.tbf.next()
            fw.act(s2[:, :w], ck[:, :w], AF.Square)
            ps = self.psR.next()
            fw.mm(ps[:, :w], self.ones_bf, s2[:, :w])
            rstd2 = self.t32.next()
            self.rstd_from_ss(rstd2[:, :w], ps[:, :w], 128.0, EPS)
            ckn = self.tbf.next()
            fw.stt(ckn[:, :w], ck[:, :w], self.vc(C_KVN), rstd2[:, :w], ALU.mult, ALU.mult)
            t1 = self.t32.next()
            t2 = self.t32.next()
            fw.tt(t1[64:96, :w], kr[64:96, 0, :w], self.ropeC[64:96, t0:t0 + w], ALU.mult)
            fw.tt(t2[64:96, :w], kr[64:96, 1, :w], self.ropeS[64:96, t0:t0 + w], ALU.mult)
            kq = krr.next()
            fw.tt(kq[64:96, :w], t1[64:96, :w], t2[64:96, :w], ALU.add)
            for h in range(4):
                p1 = self.psR.next()
                p2 = self.psR.next()
                for k in range(2):
                    fw.mm(p1[0:96, :w], wuq[:, k, h * 96:(h + 1) * 96], cn[:, k, :w], start=(k == 0), stop=(k == 1))
                for k in range(2):
                    fw.mm(p2[0:96, :w], wuq_s[:, k, h * 96:(h + 1) * 96], cn[:, k, :w], start=(k == 0), stop=(k == 1))
                fw.evac(Q[h][0:64, t0:t0 + w], p1[0:64, :w])
                a1 = self.t32.next()
                a2 = self.t32.next()
                fw.tt(a1[64:96, :w], p1[64:96, :w], self.ropeC[64:96, t0:t0 + w], ALU.mult)
                fw.tt(a2[64:96, :w], p2[64:96, :w], self.ropeS[64:96, t0:t0 + w], ALU.mult)
                fw.tt(Q[h][64:96, t0:t0 + w], a1[64:96, :w], a2[64:96, :w], ALU.add)
                p3 = self.psR.next()
                fw.mm(p3[0:64, :w], wkn[:, h, :], ckn[:, :w])
                fw.evac(K[h][0:64, t0:t0 + w], p3[0:64, :w])
                fw.copy(K[h][64:96, t0:t0 + w], kq[64:96, :w], eng="pool")
            for b in range(w // 128):
                p4 = self.psR.next()
                fw.mm(p4[:, 0:256], ckn[:, b * 128:(b + 1) * 128], wvv)
                fw.evac(Vt[:, (t0 // 128) + b, :], p4[:, 0:256])
        acc = [(self.banks[0], self.banks[1]), (self.banks[2], self.banks[3])]
        psS = Rot(self.banks[4:8])
        its = []
        grp = 0
        for h in range(4):
            for (t0, w) in chunks(0, Lp):
                kbmax = (t0 + w) // 128 - 1
                for kb in range(kbmax + 1):
                    its.append((h, t0, w, kb, kbmax, grp))
                grp += 1

        def s_stage(it):
            h, t0, w, kb, kbmax, g = it
            c0 = max(0, kb - t0 // 128) * 128
            sp = psS.next()
            fw.mm(sp[:, c0:w], K[h][:, kb * 128:(kb + 1) * 128], Q[h][:, t0 + c0:t0 + w])
            return sp
        sp_next = s_stage(its[0])
        for idx, it in enumerate(its):
            h, t0, w, kb, kbmax, g = it
            Ops, Dps = acc[g % 2]
            sp = sp_next
            if idx + 1 < len(its):
                sp_next = s_stage(its[idx + 1])
            i = kb - t0 // 128
            c0 = max(0, i) * 128
            e = self.tbf.next()
            fw.act(e[:, c0:w], sp[:, c0:w], AF.Exp, scale=scale)
            if i >= 0:
                fw.tt(e[:, c0:c0 + 128], e[:, c0:c0 + 128], self.tri_incl, ALU.mult, eng="pool")
            last = (kb == kbmax)
            fw.mm(Ops[0:64, c0:w], Vt[:, kb, h * 64:(h + 1) * 64], e[:, c0:w], start=(kb == 0), stop=last)
            fw.mm(Dps[0:64, c0:w], (self.ones_pad if kb == 0 else self.ones_bf)[:, 0:64], e[:, c0:w],
                  start=(kb == 0), stop=last)
            if last:
                den = self.t32.next()
                fw.ts(den[0:64, :w], Dps[0:64, :w], 1e-30, None, op0=ALU.add)
                fw.recip(den[0:64, :w], den[0:64, :w])
                yb = self.tbf.next()
                fw.tt(yb[0:64, :w], Ops[0:64, :w], den[0:64, :w], ALU.mult)
                fw.dma(self.yT[h * 64:(h + 1) * 64, t0:t0 + w].fresh(), yb[0:64, :w], q="sp")

    def sbatt(self, l, es):
        fw, Lp, NB = self.fw, self.Lp, self.NB
        Q = fw.sb([64, 4, Lp], BF16, es)
        K = fw.sb([64, 4, Lp], BF16, es)
        Vt = fw.sb([128, NB, 256], BF16, es)
        ld = Rot([fw.sb([64, 4, 512], F32, es) for _ in range(2)])
        ldv = Rot([fw.sb([128, 2, 512], F32, es) for _ in range(2)])
        pacc = fw.sb([128, 512], BF16, es)
        zq = self.zT[OFF_SB:OFF_SB + 256, :].rr("(h d) t -> d h t", h=4)
        zk = self.zT[OFF_SB + 256:OFF_SB + 512, :].rr("(h d) t -> d h t", h=4)
        zv = self.zT[OFF_SB + 512:OFF_SB + 768, :].rr("(k p) t -> p k t", p=128)
        for (t0, w) in chunks(0, Lp):
            a = ld.next()
            fw.dma(a[:, :, :w], zq[:, :, t0:t0 + w].fresh())
            fw.copy(Q[:, :, t0:t0 + w], a[:, :, :w], eng="act")
            b = ld.next()
            fw.dma(b[:, :, :w], zk[:, :, t0:t0 + w].fresh())
            fw.copy(K[:, :, t0:t0 + w], b[:, :, :w], eng="dve")
            v = ldv.next()
            fw.dma(v[:, :, :w], zv[:, :, t0:t0 + w].fresh())
            for bb in range(w // 128):
                for k in range(2):
                    pt = self.psR.next()
                    fw.transpose(pt[:, 0:128], v[:, k, bb * 128:(bb + 1) * 128], self.ident)
                    fw.evac(Vt[:, t0 // 128 + bb, k * 128:(k + 1) * 128], pt[:, 0:128])
        L32 = Rot([fw.sb([128, 512], F32, es) for _ in range(10)])
        Lbf = Rot([fw.sb([128, 512], BF16, es) for _ in range(8)])
        OpsL = [self.banks[0], self.banks[1]]
        psZ = Rot(self.banks[2:8])
        its = []
        grp = 0
        for h in range(4):
            for (t0, w) in chunks(0, Lp):
                kbmax = (t0 + w) // 128 - 1
                for kb in range(kbmax, -1, -1):
                    its.append((h, t0, w, kb, kbmax, grp))
                grp += 1

        def stage1(it):
            h, t0, w, kb, kbmax, g = it
            first = (kb == kbmax)
            i = kb - t0 // 128
            c0 = max(0, i) * 128
            if first:
                fw.memset(pacc[:, :w], 0.0)
            zp = psZ.next()
            fw.mm(zp[:, c0:w], K[:, h, kb * 128:(kb + 1) * 128], Q[:, h, t0 + c0:t0 + w])
            e1 = L32.next()
            fw.act(e1[:, c0:w], zp[:, c0:w], AF.Exp, scale=0.125)
            P = Lbf.next()
            fw.act(P[:, c0:w], e1[:, c0:w], AF.Ln, bias=self.epscol(1.0))
            zs = L32.next()
            fw.ts(zs[:, c0:w], zp[:, c0:w], 0.125, None, op0=ALU.mult)
            if i >= 0:
                fw.tt(P[:, c0:c0 + 128], P[:, c0:c0 + 128], self.tri_strict, ALU.mult, eng="pool")
            cp = psZ.next()
            fw.mm(cp[:, c0:w], self.uincl_pad if kb == 0 else self.uincl, P[:, c0:w], start=True, stop=first)
            if not first:
                fw.mm(cp[:, c0:w], self.ones_bf, pacc[:, c0:w], start=False, stop=True)
            lt = L32.next()
            fw.tt(lt[:, c0:w], zs[:, c0:w], cp[:, c0:w], ALU.subtract)
            if kb > 0:
                fw.tt(pacc[:, c0:w], pacc[:, c0:w], P[:, c0:w], ALU.add)
            return lt

        def stage2(it, lt):
            h, t0, w, kb, kbmax, g = it
            Ops = OpsL[g % 2]
            first = (kb == kbmax)
            i = kb - t0 // 128
            c0 = max(0, i) * 128
            A = Lbf.next()
            fw.act(A[:, c0:w], lt[:, c0:w], AF.Exp)
            if i >= 0:
                fw.tt(A[:, c0:c0 + 128], A[:, c0:c0 + 128], self.tri_strict, ALU.mult, eng="pool")
            if first and c0 > 0:
                fw.memset(A[:, 0:c0], 0.0, eng="pool")
            cc = 0 if first else c0
            fw.mm(Ops[0:64, cc:w], Vt[:, kb, h * 64:(h + 1) * 64], A[:, cc:w], start=first, stop=(kb == 0))
            if kb == 0:
                yb = Lbf.next()
                fw.evac(yb[0:64, :w], Ops[0:64, :w])
                fw.dma(self.yT[512 + h * 64:512 + (h + 1) * 64, t0:t0 + w].fresh(), yb[0:64, :w], q="sp")

        lt_next = stage1(its[0])
        for idx, it in enumerate(its):
            lt = lt_next
            if idx + 1 < len(its):
                lt_next = stage1(its[idx + 1])
            stage2(it, lt)

    def gla(self, l, es):
        fw, Lp = self.fw, self.Lp
        a2 = self.load_w(self.gla_a2[l], 1, 128, prows=16)
        Sbd = fw.sb([128, 256], F32, es)
        Sbd_bf = fw.sb([128, 256], BF16, es)
        fw.memset(Sbd, 0.0)
        fw.memset(Sbd_bf, 0.0)
        ldr = Rot([fw.sb([64, 4, 512], F32, es) for _ in range(2)])
        ldv = Rot([fw.sb([128, 2, 512], F32, es) for _ in range(2)])
        s128 = Rot([fw.sb([128, 128], F32, es) for _ in range(3)])
        b128 = Rot([fw.sb([128, 128], BF16, es) for _ in range(12)])
        vtok = Rot([fw.sb([128, 256], BF16, es) for _ in range(2)])
        yst = Rot([fw.sb([64, 4, 128], BF16, es) for _ in range(2)])
        L32 = Rot([fw.sb([128, 512], F32, es) for _ in range(14)])
        Lbf = Rot([fw.sb([128, 512], BF16, es) for _ in range(6)])
        og = OFF_GLA
        zr = self.zT[og + 528:og + 784, :].rr("(h d) t -> d h t", h=4)
        zv = self.zT[og + 256:og + 512, :].rr("(k p) t -> p k t", p=128)
        yv = self.yT[768:1024, :].rr("(h d) t -> d h t", h=4)
        for (t0, w) in chunks(0, Lp):
            al = L32.next()
            fw.dma(al[0:16, :w], self.zT[og + 512:og + 528, t0:t0 + w].fresh())
            alb = Lbf.next()
            fw.copy(alb[0:16, :w], al[0:16, :w])
            xp = self.psR.next()
            fw.mm(xp[:, :w], a2[:, 0, :], alb[0:16, :w])
            e = L32.next()
            fw.act(e[:, :w], xp[:, :w], AF.Exp, bias=self.dcols[:, 19:20], scale=-1.0)
            fw.act(e[:, :w], e[:, :w], AF.Ln, bias=self.epscol(1.0))
            fw.ts(e[:, :w], e[:, :w], -1.0 / 16.0, None, op0=ALU.mult)
            gam = L32.next()
            fw.scan(gam[:, :w], self.rm[:, :w], e[:, :w], 0.0, ALU.mult, ALU.add)
            eg = L32.next()
            fw.act(eg[:, :w], gam[:, :w], AF.Exp)
            eng = L32.next()
            fw.act(eng[:, :w], gam[:, :w], AF.Exp, scale=-1.0)
            q = L32.next()
            fw.dma(q[:, :w], self.zT[og:og + 128, t0:t0 + w].fresh())
            k = L32.next()
            fw.dma(k[:, :w], self.zT[og + 128:og + 256, t0:t0 + w].fresh())
            qt = Lbf.next()
            fw.stt(qt[:, :w], q[:, :w], 32.0 ** -0.5, eg[:, :w], ALU.mult, ALU.mult)
            kt = k
            fw.tt(kt[:, :w], k[:, :w], eng[:, :w], ALU.mult)
            ktb = Lbf.next()
            fw.copy(ktb[:, :w], kt[:, :w], eng="pool")
            v = ldv.next()
            fw.dma(v[:, :, :w], zv[:, :, t0:t0 + w].fresh())
            r = ldr.next()
            fw.dma(r[:, :, :w], zr[:, :, t0:t0 + w].fresh())
            fw.act(r[:, :, :w], r[:, :, :w], AF.Silu)
            for c in range(w // 128):
                cs = slice(c * 128, (c + 1) * 128)
                gend = eg[:, c * 128 + 127:c * 128 + 128]
                vt = vtok.next()
                for kk in range(2):
                    pt = self.psR.next()
                    fw.transpose(pt[:, 0:128], v[:, kk, cs], self.ident)
                    fw.evac(vt[:, kk * 128:(kk + 1) * 128], pt[:, 0:128])
                kh = s128.next()
                fw.ts(kh, kt[:, cs], gend, None, op0=ALU.mult)
                pt = self.psR.next()
                fw.transpose(pt[:, 0:128], kh, self.ident)
                kht = b128.next()
                fw.evac(kht, pt[:, 0:128])
                scp = self.psR.next()
                for h in range(4):
                    khh = b128.next()
                    fw.ts(khh, ktb[:, cs], self.headmask[:, h:h + 1], None, op0=ALU.mult, eng="pool")
                    fw.mm(scp[:, h * 128:(h + 1) * 128], khh, qt[:, cs])
                sc = self.tbf.next()
                fw.tt(sc, scp, self.tri_incl4.rr("p h t -> p (h t)"), ALU.mult)
                op_ = self.psR.next()
                for h in range(4):
                    fw.mm(op_[0:64, h * 128:(h + 1) * 128], Sbd_bf[:, h * 64:(h + 1) * 64], qt[:, cs],
                          start=True, stop=False)
                    fw.mm(op_[0:64, h * 128:(h + 1) * 128], vt[:, h * 64:(h + 1) * 64], sc[:, h * 128:(h + 1) * 128],
                          start=False, stop=True)
                kvp = self.psR.next()
                fw.mm(kvp[:, 0:256], kht, vt)
                tmp = self.t32.next()
                fw.tt(tmp[:, 0:256], kvp[:, 0:256], self.bdmask, ALU.mult)
                fw.stt(Sbd, Sbd, gend, tmp[:, 0:256], ALU.mult, ALU.add)
                fw.copy(Sbd_bf, Sbd, eng="pool")
                osb = self.t32.next()
                fw.evac(osb[0:64, :], op_[0:64, :])
                sq = self.tbf.next()
                fw.act(sq[0:64, :], op_[0:64, :], AF.Square)
                ssp = self.psR.next()
                fw.mm(ssp[0:64, :], self.ones_bf[0:64, 0:64], sq[0:64, :])
                rs = self.t32.next()
                self.rstd_from_ss(rs[0:64, :], ssp[0:64, :], 64.0, EPS, rows=64)
                fw.tt(osb[0:64, :], osb[0:64, :], rs[0:64, :], ALU.mult)
                yo = yst.next()
                for h in range(4):
                    fw.stt(yo[:, h, :], osb[0:64, h * 128:(h + 1) * 128], self.vc(C_GNORM + h, rows=64),
                           r[:, h, cs], ALU.mult, ALU.mult)
                fw.dma(yv[:, :, t0 + c * 128:t0 + (c + 1) * 128].fresh(), yo, q="sp")

    def rwkv(self, l, es):
        fw, Lp = self.fw, self.Lp
        w2 = fw.sb([64, 256], BF16, es)
        a2 = fw.sb([64, 256], BF16, es)
        g2 = fw.sb([128, 256], BF16, es)
        t = self.load_w(self.rw_w2[l], 1, 256, prows=64)
        fw.copy(w2, t[:, 0, :], eng="pool")
        t = self.load_w(self.rw_a2[l], 1, 256, prows=64)
        fw.copy(a2, t[:, 0, :], eng="pool")
        t = self.load_w(self.rw_g2[l], 1, 256, prows=128)
        fw.copy(g2, t[:, 0, :], eng="pool")
        T32 = [fw.sb([64, 64], F32, es) for _ in range(4)]
        Tbf = [fw.sb([64, 64], BF16, es) for _ in range(4)]
        for h in range(4):
            fw.memset(T32[h], 0.0)
            fw.memset(Tbf[h], 0.0)
        halo = Rot([fw.sb([128, 513], F32, es) for _ in range(4)])
        f64 = Rot([fw.sb([64, 512], F32, es) for _ in range(28)])
        h64 = Rot([fw.sb([64, 512], BF16, es) for _ in range(12)])
        s128 = Rot([fw.sb([128, 128], F32, es) for _ in range(4)])
        b128 = Rot([fw.sb([128, 128], BF16, es) for _ in range(32)])
        b64 = Rot([fw.sb([128, 64], BF16, es) for _ in range(12)])
        oz = OFF_RW
        c_decay = -math.exp(-0.5)
        wlb_r = Rot([fw.sb([64, 512], BF16, es) for _ in range(2)])
        alb_r = Rot([fw.sb([64, 512], BF16, es) for _ in range(2)])
        glb_r = Rot([fw.sb([128, 512], BF16, es) for _ in range(2)])

        def load_shift(rows0, nrows, t0, w, mucol, omucol, out):
            hl = halo.next()
            if t0 == 0:
                fw.memset(hl[0:nrows, 0:1], 0.0)
                fw.dma(hl[0:nrows, 1:1 + w], self.zT[rows0:rows0 + nrows, 0:w].fresh())
            else:
                fw.dma(hl[0:nrows, 0:1 + w], self.zT[rows0:rows0 + nrows, t0 - 1:t0 + w].fresh())
            tmp = f64.next() if nrows <= 64 else self.t32.next()
            fw.ts(tmp[0:nrows, :w], hl[0:nrows, 0:w], mucol, None, op0=ALU.mult)
            fw.stt(out, hl[0:nrows, 1:1 + w], omucol, tmp[0:nrows, :w], ALU.mult, ALU.add)

        for (t0, w) in chunks(0, Lp):
            nck = w // 128
            wl = f64.next()
            load_shift(oz + 768, 64, t0, w, self.vc(C_MUWL, rows=64), self.dcols[0:64, 12:13], wl[:, :w])
            wlb = wlb_r.next()
            fw.act(wlb[:, :w], wl[:, :w], AF.Tanh)
            al = f64.next()
            load_shift(oz + 832, 64, t0, w, self.vc(C_MUAL, rows=64), self.dcols[0:64, 13:14], al[:, :w])
            alb = alb_r.next()
            fw.copy(alb[:, :w], al[:, :w])
            gl = self.t32.next()
            load_shift(oz + 896, 128, t0, w, self.vc(C_MUGL), self.dcols[:, 14:15], gl[:, :w])
            glb = glb_r.next()
            fw.act(glb[:, :w], gl[:, :w], AF.Sigmoid)
            for h in range(4):
                r32, k32, v32 = f64.next(), f64.next(), f64.next()
                load_shift(oz + h * 64, 64, t0, w, self.vc(C_MURKV + h, rows=64), self.dcols[0:64, h:h + 1], r32[:, :w])
                load_shift(oz + 256 + h * 64, 64, t0, w, self.vc(C_MURKV + 4 + h, rows=64),
                           self.dcols[0:64, 4 + h:5 + h], k32[:, :w])
                load_shift(oz + 512 + h * 64, 64, t0, w, self.vc(C_MURKV + 8 + h, rows=64),
                           self.dcols[0:64, 8 + h:9 + h], v32[:, :w])
                pw = self.psR.next()
                fw.mm(pw[0:64, :w], w2[:, h * 64:(h + 1) * 64], wlb[:, :w])
                logw = f64.next()
                fw.act(logw[:, :w], pw[0:64, :w], AF.Sigmoid, bias=self.vc(C_W0 + h, rows=64))
                fw.ts(logw[:, :w], logw[:, :w], c_decay, None, op0=ALU.mult)
                pa = self.psR.next()
                fw.mm(pa[0:64, :w], a2[:, h * 64:(h + 1) * 64], alb[:, :w])
                alpha = f64.next()
                fw.act(alpha[:, :w], pa[0:64, :w], AF.Sigmoid, bias=self.vc(C_A0 + h, rows=64))
                pg = self.psR.next()
                fw.mm(pg[0:64, :w], g2[:, h * 64:(h + 1) * 64], glb[:, :w])
                g32 = f64.next()
                fw.evac(g32[:, :w], pg[0:64, :w])
                kkr = f64.next()
                fw.ts(kkr[:, :w], k32[:, :w], self.vc(C_KK + h, rows=64), None, op0=ALU.mult)
                sqk = h64.next()
                fw.act(sqk[:, :w], kkr[:, :w], AF.Square)
                pss = self.psR.next()
                fw.mm(pss[0:64, :w], self.ones_bf[0:64, 0:64], sqk[:, :w])
                rn = f64.next()
                fw.act(rn[:, :w], pss[0:64, :w], AF.Ln, bias=self.epscol(1e-24)[0:64])
                fw.act(rn[:, :w], rn[:, :w], AF.Exp, scale=-0.5)
                kk = kkr
                fw.tt(kk[:, :w], kkr[:, :w], rn[:, :w], ALU.mult)
                kmod = f64.next()
                fw.ts(kmod[:, :w], alpha[:, :w], self.vc(C_KA + h, rows=64), self.dcols[0:64, 15 + h:16 + h],
                      op0=ALU.mult, op1=ALU.add)
                fw.tt(kmod[:, :w], kmod[:, :w], k32[:, :w], ALU.mult)
                gam = f64.next()
                fw.scan(gam[:, :w], self.rm[0:64, :w], logw[:, :w], 0.0, ALU.mult, ALU.add)
                eg = f64.next()
                fw.act(eg[:, :w], gam[:, :w], AF.Exp)
                eng = f64.next()
                fw.act(eng[:, :w], gam[:, :w], AF.Exp, scale=-1.0)
                egm = f64.next()
                fw.tt(egm[:, :w], gam[:, :w], logw[:, :w], ALU.subtract)
                fw.act(egm[:, :w], egm[:, :w], AF.Exp)
                rt = h64.next()
                fw.tt(rt[:, :w], r32[:, :w], eg[:, :w], ALU.mult)
                kt32 = f64.next()
                fw.tt(kt32[:, :w], kmod[:, :w], eng[:, :w], ALU.mult)
                ktb = h64.next()
                fw.copy(ktb[:, :w], kt32[:, :w], eng="pool")
                bt32 = f64.next()
                fw.tt(bt32[:, :w], kk[:, :w], alpha[:, :w], ALU.mult)
                fw.tt(bt32[:, :w], bt32[:, :w], eng[:, :w], ALU.mult)
                btb = h64.next()
                fw.copy(btb[:, :w], bt32[:, :w], eng="pool")
                atb = h64.next()
                fw.stt(atb[:, :w], kk[:, :w], -1.0, egm[:, :w], ALU.mult, ALU.mult)
                rk = h64.next()
                fw.stt(rk[:, :w], r32[:, :w], self.vc(C_RK + h, rows=64), kmod[:, :w], ALU.mult, ALU.mult)
                pb = self.psR.next()
                fw.mm(pb[0:64, :w], self.ones_bf[0:64, 0:64], rk[:, :w])
                bon = f64.next()
                fw.tt(bon[:, :w], pb[0:64, :w], v32[:, :w], ALU.mult)
                y32 = f64.next()
                for c in range(nck):
                    cs = slice(c * 128, (c + 1) * 128)
                    gend = eg[:, c * 128 + 127:c * 128 + 128]
                    pt = self.psR.next()
                    fw.transpose(pt[:, 0:64], v32[:, cs], self.ident[0:64, 0:64])
                    vt = b64.next()
                    fw.evac(vt, pt[:, 0:64])
                    kh = s128.next()
                    fw.ts(kh[0:64, :], kt32[:, cs], gend, None, op0=ALU.mult)
                    pt = self.psR.next()
                    fw.transpose(pt[:, 0:64], kh[0:64, :], self.ident[0:64, 0:64])
                    kht = b64.next()
                    fw.evac(kht, pt[:, 0:64])
                    bh = s128.next()
                    fw.ts(bh[0:64, :], bt32[:, cs], gend, None, op0=ALU.mult)
                    pt = self.psR.next()
                    fw.transpose(pt[:, 0:64], bh[0:64, :], self.ident[0:64, 0:64])
                    bht = b64.next()
                    fw.evac(bht, pt[:, 0:64])
                    pn = self.psR.next()
                    fw.mm(pn[:, 0:128], btb[:, cs], atb[:, cs])
                    fw.mm(pn[:, 128:256], atb[:, cs], btb[:, cs])
                    fw.mm(pn[:, 256:384], ktb[:, cs], atb[:, cs])
                    N = b128.next()
                    fw.tt(N, pn[:, 0:128], self.tri_strict32, ALU.mult)
                    NT = b128.next()
                    fw.tt(NT, pn[:, 128:256], self.tri_sl32, ALU.mult)
                    AakT = b128.next()
                    fw.tt(AakT, pn[:, 256:384], self.tri_strict32, ALU.mult)
                    pr = self.psR.next()
                    fw.mm(pr[:, 0:128], ktb[:, cs], rt[:, cs])
                    fw.mm(pr[:, 128:256], btb[:, cs], rt[:, cs])
                    ArkT = b128.next()
                    fw.tt(ArkT, pr[:, 0:128], self.tri_incl32, ALU.mult)
                    ArbT = b128.next()
                    fw.tt(ArbT, pr[:, 128:256], self.tri_incl32, ALU.mult)
                    P = b128.next()
                    fw.tt(P, N, self.ident_bf, ALU.add)
                    for lev in range(6):
                        pq = self.psR.next()
                        fw.mm(pq[:, 128:256], N, NT)
                        if lev < 5:
                            fw.mm(pq[:, 0:128], NT, N)
                            N2 = b128.next()
                            fw.evac(N2, pq[:, 0:128])
                        NT2 = b128.next()
                        fw.evac(NT2, pq[:, 128:256])
                        pp = self.psR.next()
                        fw.mm(pp[:, 0:128], NT2, P)
                        P2 = b128.next()
                        fw.tt(P2, P, pp[:, 0:128], ALU.add)
                        P = P2
                        NT = NT2
                        if lev < 5:
                            N = N2
                    MT = P
                    p0 = self.psR.next()
                    fw.mm(p0[:, 0:64], atb[:, cs], Tbf[h], start=True, stop=False)
                    fw.mm(p0[:, 0:64], AakT, vt, start=False, stop=True)
                    rhs0 = b64.next()
                    fw.evac(rhs0, p0[:, 0:64])
                    pu = self.psR.next()
                    fw.mm(pu[:, 0:64], MT, rhs0)
                    U = b64.next()
                    fw.evac(U, pu[:, 0:64])
                    py = self.psR.next()
                    fw.mm(py[0:64, 0:128], Tbf[h], rt[:, cs], start=True, stop=False)
                    fw.mm(py[0:64, 0:128], vt, ArkT, start=False, stop=False)
                    fw.mm(py[0:64, 0:128], U, ArbT, start=False, stop=True)
                    fw.evac(y32[:, cs], py[0:64, 0:128])
                    pT = self.psR.next()
                    fw.mm(pT[0:64, 0:64], kht, vt, start=True, stop=False)
                    fw.mm(pT[0:64, 0:64], bht, U, start=False, stop=True)
                    fw.stt(T32[h], T32[h], gend, pT[0:64, 0:64], ALU.mult, ALU.add)
                    fw.copy(Tbf[h], T32[h], eng="pool")
                ybf = h64.next()
                fw.copy(ybf[:, :w], y32[:, :w], eng="pool")
                pm = self.psR.next()
                fw.mm(pm[0:64, :w], self.ones_bf[0:64, 0:64], ybf[:, :w])
                yc = f64.next()
                fw.stt(yc[:, :w], pm[0:64, :w], -1.0 / 64.0, y32[:, :w], ALU.mult, ALU.add)
                sq = h64.next()
                fw.act(sq[:, :w], yc[:, :w], AF.Square)
                pv = self.psR.next()
                fw.mm(pv[0:64, :w], self.ones_bf[0:64, 0:64], sq[:, :w])
                rs = f64.next()
                self.rstd_from_ss(rs[:, :w], pv[0:64, :w], 64.0, 64e-5, rows=64)
                fw.tt(yc[:, :w], yc[:, :w], rs[:, :w], ALU.mult)
                fw.ts(yc[:, :w], yc[:, :w], self.vc(C_LNW + h, rows=64), self.vc(C_LNB + h, rows=64),
                      op0=ALU.mult, op1=ALU.add)
                fw.tt(yc[:, :w], yc[:, :w], bon[:, :w], ALU.add)
                yo = h64.next()
                fw.tt(yo[:, :w], yc[:, :w], g32[:, :w], ALU.mult)
                fw.dma(self.yT[256 + h * 64:256 + (h + 1) * 64, t0:t0 + w].fresh(), yo[:, :w], q="sp")

    def phase_C1(self, l, es):
        fw, Lp = self.fw, self.Lp
        TS = 1536
        n = fw.sb([128, 8, TS], BF16, es)
        y = fw.sb([128, 8, TS], BF16, es)
        mg = fw.sb([128, 8, TS], BF16, es)
        acc = fw.sb([128, TS], F32, es)
        nTv = self.nT.rr("(k p) t -> p k t", p=128)
        yTv = self.yT.rr("(k p) t -> p k t", p=128)
        wl = self.w_in[l]
        for (s0, sw) in chunks(0, Lp, TS):
            for (t0, w) in chunks(0, sw):
                fw.dma(n[:, :, t0:t0 + w], nTv[:, :, s0 + t0:s0 + t0 + w].fresh())
                fw.dma(y[:, :, t0:t0 + w], yTv[:, :, s0 + t0:s0 + t0 + w].fresh())
            for d in range(8):
                for m in range(4):
                    c0 = OFF_GATE + m * 1024 + d * 128
                    wg = self.load_w(wl[:, c0:c0 + 128], 8, 128)
                    wb = self.load_w(self.w_branch[l][m][:, d * 128:(d + 1) * 128], 2, 128)
                    for (t0, w) in chunks(0, sw):
                        pg = self.psR.next()
                        for k in range(8):
                            fw.mm(pg[:, :w], wg[:, k, :], n[:, k, t0:t0 + w], start=(k == 0), stop=(k == 7))
                        pb = self.psR.next()
                        for k in range(2):
                            fw.mm(pb[:, :w], wb[:, k, :], y[:, 2 * m + k, t0:t0 + w], start=(k == 0), stop=(k == 1))
                        gt = self.t32.next()
                        fw.act(gt[:, :w], pg[:, :w], AF.Sigmoid, bias=self.vc(C_GATEB + m * 8 + d))
                        if m == 0:
                            fw.tt(acc[:, t0:t0 + w], gt[:, :w], pb[:, :w], ALU.mult)
                        else:
                            fw.tt(gt[:, :w], gt[:, :w], pb[:, :w], ALU.mult)
                            if m < 3:
                                fw.tt(acc[:, t0:t0 + w], acc[:, t0:t0 + w], gt[:, :w], ALU.add, eng="pool")
                            else:
                                fw.tt(mg[:, d, t0:t0 + w], acc[:, t0:t0 + w], gt[:, :w], ALU.add, eng="pool")
            for d in range(8):
                wo = self.load_w(self.w_out[l][:, d * 128:(d + 1) * 128], 8, 128)
                for (t0, w) in chunks(0, sw):
                    g0 = s0 + t0
                    po = self.psR.next()
                    for k in range(8):
                        fw.mm(po[:, :w], wo[:, k, :], mg[:, k, t0:t0 + w], start=(k == 0), stop=(k == 7))
                    lo = PADC if g0 == 0 else 0
                    hc = self.t32.next()
                    cell = self.hcell(d, g0 + lo, w - lo)
                    fw.dma(hc[:, lo:w], cell)
                    fw.tt(hc[:, lo:w], hc[:, lo:w], po[:, lo:w], ALU.add)
                    fw.dma(cell, hc[:, lo:w], q="sp")

    def phase_C2(self, l, es):
        fw, Lp = self.fw, self.Lp
        TS = 1536
        n2 = fw.sb([128, 8, TS], BF16, es)
        g = fw.sb([128, 22, TS], BF16, es)
        carry = fw.sb([128, 22, 2], F32, es)
        fw.memset(carry, 0.0)
        self.h8 = Rot([fw.sb([128, 8, 512], F32, es) for _ in range(1)])
        self.sq8 = Rot([fw.sb([128, 8, 512], BF16, es) for _ in range(1)])
        asb = Rot([fw.sb([128, 514], F32, es) for _ in range(3)])
        wfi = self.w_ffn_in[l]
        for (s0, sw) in chunks(0, Lp, TS):
            for (t0, w) in chunks(0, sw):
                hc = self.h8.next()
                fw.dma(hc[:, :, :w], self.hall(s0 + t0, w))
                self.rmsnorm_chunk(hc, w, C_NFFN, self.vecs, lambda k: n2[:, k, t0:t0 + w])
            for fc in range(22):
                wa = self.load_w(wfi[:, fc * 128:(fc + 1) * 128], 8, 128)
                wu = self.load_w(wfi[:, DFF + fc * 128:DFF + (fc + 1) * 128], 8, 128)
                for (t0, w) in chunks(0, sw):
                    pa = self.psR.next()
                    for k in range(8):
                        fw.mm(pa[:, :w], wa[:, k, :], n2[:, k, t0:t0 + w], start=(k == 0), stop=(k == 7))
                    pu = self.psR.next()
                    for k in range(8):
                        fw.mm(pu[:, :w], wu[:, k, :], n2[:, k, t0:t0 + w], start=(k == 0), stop=(k == 7))
                    a = asb.next()
                    fw.copy(a[:, 0:2], carry[:, fc, :], eng="pool")
                    fw.copy(a[:, 2:2 + w], pa[:, :w], eng="act")
                    fw.copy(carry[:, fc, :], a[:, w:w + 2], eng="pool")
                    c = self.t32.next()
                    fw.ts(c[:, :w], a[:, 0:w], self.vc(C_CONVW + fc), self.vc(C_CONVB + fc), op0=ALU.mult, op1=ALU.add)
                    fw.stt(c[:, :w], a[:, 1:1 + w], self.vc(C_CONVW + 22 + fc), c[:, :w], ALU.mult, ALU.add)
                    fw.stt(c[:, :w], a[:, 2:2 + w], self.vc(C_CONVW + 44 + fc), c[:, :w], ALU.mult, ALU.add)
                    fw.act(c[:, :w], c[:, :w], AF.Silu)
                    fw.tt(g[:, fc, t0:t0 + w], c[:, :w], pu[:, :w], ALU.mult)
            for d in range(8):
                wo = self.load_w(self.w_ffn_out[l][:, d * 128:(d + 1) * 128], 22, 128)
                for (t0, w) in chunks(0, sw):
                    g0 = s0 + t0
                    po = self.psR.next()
                    for k in range(22):
                        fw.mm(po[:, :w], wo[:, k, :], g[:, k, t0:t0 + w], start=(k == 0), stop=(k == 21))
                    lo = PADC if g0 == 0 else 0
                    hc = self.t32.next()
                    cell = self.hcell(d, g0 + lo, w - lo)
                    fw.dma(hc[:, lo:w], cell)
                    fw.tt(hc[:, lo:w], hc[:, lo:w], po[:, lo:w], ALU.add)
                    fw.dma(cell, hc[:, lo:w], q="sp")

    def phase_F(self, es):
        fw, Lp = self.fw, self.Lp
        self.h8 = Rot([fw.sb([128, 8, 512], F32, es) for _ in range(2)])
        self.sq8 = Rot([fw.sb([128, 8, 512], BF16, es) for _ in range(2)])
        o8 = Rot([fw.sb([128, 8, 512], F32, es) for _ in range(2)])
        ov = self.outT.rr("(k p) t -> p k t", p=128)
        for (t0, w) in chunks(0, Lp):
            hc = self.h8.next()
            fw.dma(hc[:, :, :w], self.hall(t0, w))
            o = o8.next()
            self.rmsnorm_chunk(hc, w, 0, self.gvec, lambda k: o[:, k, :w])
            lo = 128 if t0 == 0 else 0
            if w - lo > 0:
                fw.dma(ov[:, :, t0 + lo - 128:t0 + w - 128].fresh(), o[:, :, lo:w], q="sp")


def _cols(v, p=128):
    a = np.asarray(v, np.float32).reshape(-1, p).T
    if p < 128:
        a = np.concatenate([a, np.zeros((128 - p, a.shape[1]), np.float32)], 0)
    return a


def pack_vecs(inp, l):
    out = np.zeros((128, NV), np.float32)

    def put(c, a):
        out[:, c:c + a.shape[1]] = a
    put(C_NMIX, _cols(inp["norm_mix"][l]))
    put(C_NFFN, _cols(inp["norm_ffn"][l]))
    put(C_GATEB, _cols(inp["gate_b"][l].reshape(-1)))
    put(C_CONVW, _cols(inp["ffn_conv_w"][l].reshape(-1)))
    put(C_CONVB, _cols(inp["ffn_conv_b"][l]))
    put(C_QN, _cols(inp["mla_q_norm"][l]))
    put(C_KVN, _cols(inp["mla_kv_norm"][l]))
    mu = inp["rw_mu"][l]
    put(C_MURKV, _cols(mu[0:768], 64))
    put(C_MUWL, _cols(mu[768:832], 64))
    put(C_MUAL, _cols(mu[832:896], 64))
    put(C_MUGL, _cols(mu[896:1024]))
    put(C_W0, _cols(inp["rw_w0"][l], 64))
    put(C_A0, _cols(inp["rw_a0"][l], 64))
    put(C_KK, _cols(inp["rw_k_k"][l], 64))
    put(C_KA, _cols(inp["rw_k_a"][l], 64))
    put(C_RK, _cols(inp["rw_r_k"][l].reshape(-1), 64))
    put(C_LNW, _cols(inp["rw_ln_w"][l], 64))
    put(C_LNB, _cols(inp["rw_ln_b"][l], 64))
    put(C_GAB, _cols(inp["gla_a_b"][l]))
    put(C_GNORM, _cols(inp["gla_norm"][l], 64))
    return out


def rope_table(Lp):
    half = 16
    freqs = (np.float32(10000.0) ** (-np.arange(half, dtype=np.float32) / np.float32(half))).astype(np.float32)
    pos = (np.arange(Lp) - PADC).astype(np.float32)
    ang = (pos[None, :] * freqs[:, None]).astype(np.float32)
    c, s = np.cos(ang).astype(np.float32), np.sin(ang).astype(np.float32)
    return np.concatenate([c, c, -s, s], 0).astype(np.float32)


_CACHE = {}


def run(inputs, NB, DEPTH, dbg=(), n_cores=8, phases=None):
    key = (NB, DEPTH, tuple(dbg), phases)
    if key not in _CACHE:
        _CACHE[key] = Builder(NB, DEPTH, dbg, phases).build()
    nc = _CACHE[key]
    Lp = NB * 128
    x = np.asarray(inputs["x"], np.float32)
    B = x.shape[0]
    meta = np.asarray(inputs["meta_tokens"], np.float32)
    shared = {
        "w_in": np.ascontiguousarray(inputs["w_in"][:DEPTH], np.float32),
        "mla_w_uq": np.ascontiguousarray(inputs["mla_w_uq"][:DEPTH], np.float32),
        "mla_w_ukv": np.ascontiguousarray(inputs["mla_w_ukv"][:DEPTH], np.float32),
        "rw_w2": np.ascontiguousarray(inputs["rw_w2"][:DEPTH], np.float32),
        "rw_a2": np.ascontiguousarray(inputs["rw_a2"][:DEPTH], np.float32),
        "rw_g2": np.ascontiguousarray(inputs["rw_g2"][:DEPTH], np.float32),
        "gla_a2": np.ascontiguousarray(inputs["gla_a2"][:DEPTH], np.float32),
        "w_branch": np.ascontiguousarray(inputs["w_branch"][:DEPTH], np.float32),
        "w_out": np.ascontiguousarray(inputs["w_out"][:DEPTH], np.float32),
        "w_ffn_in": np.ascontiguousarray(inputs["w_ffn_in"][:DEPTH], np.float32),
        "w_ffn_out": np.ascontiguousarray(inputs["w_ffn_out"][:DEPTH], np.float32),
        "vecs": np.stack([pack_vecs(inputs, l) for l in range(DEPTH)], 0),
        "gvec": _cols(inputs["norm_final"]),
        "rope": rope_table(Lp),
    }
    in_maps = []
    for c in range(n_cores):
        b = c % B
        hT0 = np.zeros((1024, Lp), np.float32)
        hT0[:, PADC:PADC + 16] = meta.T
        hT0[:, 128:] = x[b].T
        m = dict(shared)
        m["hT0"] = hT0
        in_maps.append(m)
    res = run_bass_kernel_spmd(nc, in_maps, core_ids=list(range(n_cores)))
    return res.results


def kernel(**inputs):
    x = np.asarray(inputs["x"])
    B, SEQ, D = x.shape
    NB = (SEQ + 128) // 128
    results = run(inputs, NB, 4)
    out = np.stack([np.ascontiguousarray(results[b]["outT"].T) for b in range(B)], 0)
    return out.astype(np.float32)
```

```python
import math
import numpy as np
from contextlib import ExitStack
import concourse.bass as bass
import concourse.mybir as mybir
from concourse.bass_utils import run_bass_kernel_spmd

F32 = mybir.dt.float32
BF16 = mybir.dt.bfloat16
ALU = mybir.AluOpType
AF = mybir.ActivationFunctionType
AX = mybir.AxisListType

NDMA = 24
EPS = 1e-6
PADC = 112
NMIX = 2992
OFF_CQ, OFF_CKV, OFF_KR, OFF_RW, OFF_SB, OFF_GLA, OFF_GATE = 0, 256, 384, 416, 1440, 2208, 2992
DFF = 2816
NV = 192
(C_NMIX, C_NFFN, C_GATEB, C_CONVW, C_CONVB, C_QN, C_KVN, C_MURKV, C_MUWL, C_MUAL, C_MUGL,
 C_W0, C_A0, C_KK, C_KA, C_RK, C_LNW, C_LNB, C_GAB, C_GNORM) = (
    0, 8, 16, 48, 114, 136, 138, 139, 151, 152, 153, 154, 158, 162, 166, 170, 174, 178, 182, 183)


class Res:
    __slots__ = ("w", "r", "excl")

    def __init__(self, excl=False):
        self.w = {}
        self.r = {}
        self.excl = excl


def _rl(v):
    r = v.res
    return r if isinstance(r, tuple) else (r,)


class V:
    __slots__ = ("ap", "res")

    def __init__(self, ap, res=None):
        self.ap = ap
        self.res = res if res is not None else Res()

    def __getitem__(self, idx):
        return V(self.ap[idx], self.res)

    def rr(self, pat, **kw):
        return V(self.ap.rearrange(pat, **kw), self.res)

    def fresh(self):
        return V(self.ap, Res())

    def withres(self, res):
        return V(self.ap, res)


class _Eng:
    def __init__(self, name, obj, sem):
        self.name, self.obj, self.sem = name, obj, sem
        self.n = 0
        self.waited = {}
        self.pending = False


class Rot:
    def __init__(self, items):
        self.items = items
        self.i = 0

    def next(self):
        x = self.items[self.i]
        self.i = (self.i + 1) % len(self.items)
        return x


class FW:
    def __init__(self, nc, es):
        self.nc = nc
        self.es = es
        self.engs = {}
        for name, obj in (("pe", nc.tensor), ("act", nc.scalar), ("dve", nc.vector),
                          ("pool", nc.gpsimd), ("sp", nc.sync)):
            sem = es.enter_context(nc.semaphore("s_" + name))
            self.engs[name] = _Eng(name, obj, sem)
        self.dma_sems = [es.enter_context(nc.semaphore("d%d" % i)) for i in range(NDMA)]
        self.dma_cnt = [0] * NDMA
        self.dma_next = 0
        self.nins = 0
        self._uid = 0
        self._ev = 0

    def sb(self, shape, dt=F32, es=None):
        self._uid += 1
        t = (es or self.es).enter_context(self.nc.sbuf_tensor("sb%d" % self._uid, list(shape), dt))
        return V(t[:], Res())

    def ps(self, shape, dt=F32, es=None):
        self._uid += 1
        t = (es or self.es).enter_context(self.nc.psum_tensor("ps%d" % self._uid, list(shape), dt))
        return V(t[:], Res(excl=True))

    def _wait(self, eng, tok):
        key, sem, val, src = tok
        if src == "pe" and eng.name == "pe":
            return
        if eng.waited.get(key, 0) >= val:
            return
        eng.obj.wait_ge(sem, val)
        eng.waited[key] = val
        self.nins += 1

    def _deps(self, reads, writes):
        toks = []
        for v in reads:
            for r in _rl(v):
                toks.extend(r.w.values())
                if r.excl:
                    toks.extend(t for t in r.r.values() if t[3] != self._cur)
        for v in writes:
            for r in _rl(v):
                toks.extend(r.w.values())
                toks.extend(r.r.values())
        return toks

    def _mark(self, key, tok, reads, writes):
        wres = []
        for v in writes:
            for r in _rl(v):
                r.w = {key: tok}
                r.r = {}
                wres.append(r)
        for v in reads:
            for r in _rl(v):
                if r not in wres:
                    r.r[key] = tok

    def op(self, engname, fn, reads, writes, inc=True):
        eng = self.engs[engname]
        self._cur = engname
        for t in self._deps(reads, writes):
            self._wait(eng, t)
        ins = fn(eng.obj)
        self.nins += 1
        if inc:
            eng.n += 1
            ins.then_inc(eng.sem, 1)
            tok = (engname, eng.sem, eng.n, engname)
            eng.pending = False
        else:
            tok = (engname, eng.sem, eng.n + 1, engname)
            eng.pending = True
        self._mark(engname, tok, reads, writes)
        return ins

    def dma(self, out, in_, q="sp"):
        eng = self.engs[q]
        self._cur = q
        for t in self._deps([in_], [out]):
            self._wait(eng, t)
        i = self.dma_next
        self.dma_next = (i + 1) % NDMA
        key = ("dma", i)
        if self.dma_cnt[i] > 0:
            self._wait(eng, (key, self.dma_sems[i], self.dma_cnt[i], None))
        self.dma_cnt[i] += 16
        eng.obj.dma_start(out=out.ap, in_=in_.ap).then_inc(self.dma_sems[i], 16)
        self.nins += 1
        tok = (key, self.dma_sems[i], self.dma_cnt[i], None)
        self._mark(key, tok, [in_], [out])

    def barrier(self, engines=("pe", "act", "dve", "pool", "sp")):
        for en in engines:
            eng = self.engs[en]
            for i in range(NDMA):
                if self.dma_cnt[i] > 0:
                    self._wait(eng, (("dma", i), self.dma_sems[i], self.dma_cnt[i], None))
            for name, e in self.engs.items():
                assert not e.pending, name
                if e.n > 0 and name != en:
                    self._wait(eng, (name, e.sem, e.n, None))

    def mm(self, out, lhsT, rhs, start=True, stop=True):
        return self.op("pe", lambda e: e.matmul(out.ap, lhsT.ap, rhs.ap, start=start, stop=stop),
                       [lhsT, rhs], [out], inc=stop)

    def transpose(self, out, in_, ident):
        return self.op("pe", lambda e: e.transpose(out.ap, in_.ap, ident.ap), [in_, ident], [out])

    def act(self, out, in_, func, bias=None, scale=None):
        reads = [in_]
        kw = {}
        if bias is not None:
            if isinstance(bias, V):
                reads.append(bias)
                kw["bias"] = bias.ap
            else:
                kw["bias"] = bias
        if scale is not None:
            if isinstance(scale, V):
                reads.append(scale)
                kw["scale"] = scale.ap
            else:
                kw["scale"] = scale
        return self.op("act", lambda e: e.activation(out.ap, in_.ap, func, **kw), reads, [out])

    def tt(self, out, in0, in1, op, eng="dve"):
        return self.op(eng, lambda e: e.tensor_tensor(out.ap, in0.ap, in1.ap, op), [in0, in1], [out])

    def ts(self, out, in0, s1, s2=None, op0=ALU.mult, op1=None, eng="dve"):
        reads = [in0]
        a1 = s1
        if isinstance(s1, V):
            reads.append(s1)
            a1 = s1.ap
        a2 = s2
        if isinstance(s2, V):
            reads.append(s2)
            a2 = s2.ap
        kw = {}
        if op1 is not None:
            kw["op1"] = op1
        return self.op(eng, lambda e: e.tensor_scalar(out.ap, in0.ap, a1, a2, op0, **kw), reads, [out])

    def stt(self, out, in0, scalar, in1, op0, op1, eng="dve"):
        reads = [in0, in1]
        a = scalar
        if isinstance(scalar, V):
            reads.append(scalar)
            a = scalar.ap
        return self.op(eng, lambda e: e.scalar_tensor_tensor(out.ap, in0.ap, a, in1.ap, op0, op1), reads, [out])

    def copy(self, out, in_, eng="dve"):
        if eng == "act":
            return self.act(out, in_, AF.Copy)
        return self.op(eng, lambda e: e.tensor_copy(out.ap, in_.ap), [in_], [out])

    def evac(self, out, in_):
        self._ev ^= 1
        return self.copy(out, in_, eng="act" if self._ev else "dve")

    def memset(self, out, val, eng="dve"):
        return self.op(eng, lambda e: e.memset(out.ap, val), [], [out])

    def scan(self, out, d0, d1, init, op0, op1):
        return self.op("dve", lambda e: e.tensor_tensor_scan(out.ap, d0.ap, d1.ap, init, op0, op1), [d0, d1], [out])

    def recip(self, out, in_):
        return self.op("dve", lambda e: e.reciprocal(out.ap, in_.ap), [in_], [out])

    def aselect(self, t, cmp, fill, base, pattern, cm):
        return self.op("pool", lambda g: g.affine_select(out=t.ap, in_=t.ap, compare_op=cmp, fill=fill, base=base,
                                                         pattern=pattern, channel_multiplier=cm), [t], [t])


def chunks(t0, t1, maxw=512):
    out = []
    t = t0
    while t < t1:
        w = min(maxw, t1 - t)
        out.append((t, w))
        t += w
    return out


class Builder:
    def __init__(self, NB, DEPTH, dbg=(), phases=None):
        self.phases = phases
        self.NB = NB
        self.Lp = NB * 128
        self.SEQ = self.Lp - 128
        self.DEPTH = DEPTH
        self.dbg = dbg
        self.nc = bass.Bass("TRN2", target_bir_lowering=False)

    def din(self, name, shape, dt=F32):
        return V(self.nc.dram_tensor(name, list(shape), dt, kind="ExternalInput").ap())

    def dscr(self, name, shape, dt=F32):
        kind = "ExternalOutput" if name in self.dbg else "Internal"
        return V(self.nc.dram_tensor(name, list(shape), dt, kind=kind).ap())

    def build(self):
        nc, Lp, DEPTH = self.nc, self.Lp, self.DEPTH
        self.hT0 = self.din("hT0", [1024, Lp])
        self.w_in = self.din("w_in", [DEPTH, 1024, 7088])
        self.w_uq = self.din("mla_w_uq", [DEPTH, 256, 384])
        self.w_ukv = self.din("mla_w_ukv", [DEPTH, 128, 512])
        self.rw_w2 = self.din("rw_w2", [DEPTH, 64, 256])
        self.rw_a2 = self.din("rw_a2", [DEPTH, 64, 256])
        self.rw_g2 = self.din("rw_g2", [DEPTH, 128, 256])
        self.gla_a2 = self.din("gla_a2", [DEPTH, 16, 128])
        self.w_branch = self.din("w_branch", [DEPTH, 4, 256, 1024])
        self.w_out = self.din("w_out", [DEPTH, 1024, 1024])
        self.w_ffn_in = self.din("w_ffn_in", [DEPTH, 1024, 5632])
        self.w_ffn_out = self.din("w_ffn_out", [DEPTH, 2816, 1024])
        self.vecs_d = self.din("vecs", [DEPTH, 128, NV])
        self.gvec_d = self.din("gvec", [128, 8])
        self.rope_d = self.din("rope", [64, Lp])
        self.outT = V(nc.dram_tensor("outT", [1024, self.SEQ], F32, kind="ExternalOutput").ap())
        self.hT = self.dscr("hT", [1024, Lp])
        self.nT = self.dscr("nT", [1024, Lp], BF16)
        self.zT = self.dscr("zT", [NMIX, Lp])
        self.yT = self.dscr("yT", [1024, Lp], BF16)
        nch = (Lp + 511) // 512
        self.hres = [[Res() for _ in range(nch)] for _ in range(8)]

        with ExitStack() as es:
            self.fw = fw = FW(nc, es)
            self.consts(es)
            for k in range(8):
                fw.dma(self.hT[k * 128:(k + 1) * 128, :].withres(tuple(self.hres[k])),
                       self.hT0[k * 128:(k + 1) * 128, :])
            fw.barrier()
            print("sbuf remaining after consts:", nc.sbuf_bytes_remaining)
            for l in range(DEPTH):
                self.layer_setup(l)
                for nm, fn in (("A", self.phase_A), ("mla", self.mla), ("sb", self.sbatt), ("gla", self.gla),
                               ("rwkv", self.rwkv), ("C1", self.phase_C1), ("C2", self.phase_C2)):
                    if self.phases is not None and nm not in self.phases:
                        continue
                    with ExitStack() as e2, nc.named_scope("L%d_%s" % (l, nm)):
                        fn(l, e2)
                        fw.barrier()
            with ExitStack() as e2:
                self.phase_F(e2)
            fw.barrier()
            print("instructions:", fw.nins, {k: e.n for k, e in fw.engs.items()})
        return nc

    def hcell(self, k, t0, w):
        c0, c1 = t0 // 512, (t0 + w - 1) // 512
        res = tuple(self.hres[k][c] for c in range(c0, c1 + 1))
        return self.hT[k * 128:(k + 1) * 128, t0:t0 + w].withres(res if len(res) > 1 else res[0])

    def hall(self, t0, w):
        c0, c1 = t0 // 512, (t0 + w - 1) // 512
        res = tuple(self.hres[k][c] for k in range(8) for c in range(c0, c1 + 1))
        return self.hT.rr("(k p) t -> p k t", p=128)[:, :, t0:t0 + w].withres(res)

    def consts(self, es):
        fw = self.fw
        Lp = self.Lp
        self.ident = fw.sb([128, 128], F32, es)
        fw.memset(self.ident, 0.0, eng="pool")
        fw.aselect(self.ident, ALU.not_equal, 1.0, 0, [[-1, 128]], 1)
        self.ident_bf = fw.sb([128, 128], BF16, es)
        fw.copy(self.ident_bf, self.ident)
        self.ones_bf = fw.sb([128, 128], BF16, es)
        fw.memset(self.ones_bf, 1.0)
        self.zeros_bf = fw.sb([128, 128], BF16, es)
        fw.memset(self.zeros_bf, 0.0)
        padcol = fw.sb([128, 1], F32, es)
        fw.memset(padcol, 1.0, eng="pool")
        fw.aselect(padcol, ALU.is_ge, 0.0, -PADC, [[0, 1]], 1)
        self.ones_pad = fw.sb([128, 128], BF16, es)
        fw.ts(self.ones_pad, self.ones_bf, padcol[:, 0:1], None, op0=ALU.mult)

        def tri(cmp, base, pat, cm):
            t32 = fw.sb([128, 128], F32, es)
            fw.memset(t32, 1.0, eng="pool")
            fw.aselect(t32, cmp, 0.0, base, pat, cm)
            tb = fw.sb([128, 128], BF16, es)
            fw.copy(tb, t32)
            return t32, tb
        self.tri_incl32, self.tri_incl = tri(ALU.is_ge, 0, [[1, 128]], -1)
        self.tri_strict32, self.tri_strict = tri(ALU.is_gt, 0, [[1, 128]], -1)
        self.tri_sl32, self.tri_sl = tri(ALU.is_gt, 0, [[-1, 128]], 1)
        self.tri_ge32, self.uincl = tri(ALU.is_ge, 0, [[-1, 128]], 1)
        self.uincl_pad = fw.sb([128, 128], BF16, es)
        fw.ts(self.uincl_pad, self.uincl, padcol[:, 0:1], None, op0=ALU.mult)
        self.tri_incl4 = fw.sb([128, 4, 128], F32, es)
        for h in range(4):
            fw.copy(self.tri_incl4[:, h, :], self.tri_incl32)
        self.rm = fw.sb([128, 512], F32, es)
        fw.memset(self.rm, 1.0)
        for k in range(4):
            fw.memset(self.rm[:, k * 128:k * 128 + 1], 0.0)
        self.headmask = fw.sb([128, 4], F32, es)
        fw.memset(self.headmask, 1.0, eng="pool")
        fw.aselect(self.headmask, ALU.is_ge, 0.0, 0, [[-32, 4]], 1)
        fw.aselect(self.headmask, ALU.is_ge, 0.0, 31, [[32, 4]], -1)
        self.bdmask = fw.sb([128, 256], F32, es)
        fw.memset(self.bdmask, 1.0, eng="pool")
        bd3 = self.bdmask.rr("p (h c) -> p h c", h=4)
        fw.aselect(bd3, ALU.is_ge, 0.0, 0, [[-32, 4], [0, 64]], 1)
        fw.aselect(bd3, ALU.is_ge, 0.0, 31, [[32, 4], [0, 64]], -1)
        self.vecs = fw.sb([128, NV], F32, es)
        self.gvec = fw.sb([128, 8], F32, es)
        fw.dma(self.gvec, self.gvec_d)
        self.dcols = fw.sb([128, 32], F32, es)
        self._eps = {}
        for ev in (EPS, 1.0, 1e-24, 64e-5):
            t = fw.sb([128, 1], F32, es)
            fw.memset(t, ev)
            self._eps[ev] = t
        banks = [fw.ps([128, 512], F32, es) for _ in range(8)]
        self.banks = banks
        self.psA = banks[0:3]
        self.psR = Rot(banks[3:8])
        self.wst = Rot([fw.sb([128, 1408], F32, es) for _ in range(4)])
        self.wbf = Rot([fw.sb([128, 2816], BF16, es) for _ in range(5)])
        self.t32 = Rot([fw.sb([128, 512], F32, es) for _ in range(6)])
        self.tbf = Rot([fw.sb([128, 512], BF16, es) for _ in range(6)])

    def vc(self, c, n=1, rows=128):
        return self.vecs[0:rows, c:c + n]

    def layer_setup(self, l):
        fw = self.fw
        fw.dma(self.vecs, self.vecs_d[l])
        d = self.dcols
        fw.ts(d[:, 0:15], self.vecs[:, C_MURKV:C_MURKV + 15], -1.0, 1.0, op0=ALU.mult, op1=ALU.add)
        fw.ts(d[:, 15:19], self.vecs[:, C_KA:C_KA + 4], -1.0, 1.0, op0=ALU.mult, op1=ALU.add)
        fw.ts(d[:, 19:20], self.vecs[:, C_GAB:C_GAB + 1], -1.0, None, op0=ALU.mult)

    def load_w(self, src2d, kc, ow, prows=128):
        fw = self.fw
        wb = self.wbf.next()[0:prows, 0:kc * ow].rr("p (k o) -> p k o", k=kc)
        srcv = src2d.rr("(k p) o -> p k o", p=prows)
        kmax = max(1, 1408 // ow)
        k0 = 0
        while k0 < kc:
            kn = min(kmax, kc - k0)
            st = self.wst.next()[0:prows, 0:kn * ow].rr("p (k o) -> p k o", k=kn)
            fw.dma(st, srcv[:, k0:k0 + kn, :])
            fw.copy(wb[:, k0:k0 + kn, :], st, eng="pool")
            k0 += kn
        return wb

    def rstd_from_ss(self, out, ss_ps, n, eps, rows=128):
        fw = self.fw
        fw.act(out, ss_ps, AF.Ln, bias=self.epscol(eps)[0:rows], scale=1.0 / n)
        fw.act(out, out, AF.Exp, scale=-0.5)

    def epscol(self, eps):
        return self._eps[eps]

    def rmsnorm_chunk(self, hc, w, gcol0, gsrc, out_fn):
        fw = self.fw
        sq = self.sq8.next()
        fw.act(sq[:, :, :w], hc[:, :, :w], AF.Square)
        ps = self.psR.next()
        for k in range(8):
            fw.mm(ps[:, :w], self.ones_bf, sq[:, k, :w], start=(k == 0), stop=(k == 7))
        rstd = self.t32.next()
        self.rstd_from_ss(rstd[:, :w], ps[:, :w], 1024.0, EPS)
        for k in range(8):
            fw.stt(out_fn(k), hc[:, k, :w], gsrc[:, gcol0 + k:gcol0 + k + 1], rstd[:, :w], ALU.mult, ALU.mult)

    def phase_A(self, l, es):
        fw, Lp = self.fw, self.Lp
        n = fw.sb([128, 8, Lp], BF16, es)
        self.h8 = Rot([fw.sb([128, 8, 512], F32, es) for _ in range(2)])
        self.sq8 = Rot([fw.sb([128, 8, 512], BF16, es) for _ in range(2)])
        nTv = self.nT.rr("(k p) t -> p k t", p=128)
        for (t0, w) in chunks(0, Lp):
            hc = self.h8.next()
            fw.dma(hc[:, :, :w], self.hall(t0, w))
            self.rmsnorm_chunk(hc, w, C_NMIX, self.vecs, lambda k: n[:, k, t0:t0 + w])
            fw.dma(nTv[:, :, t0:t0 + w].fresh(), n[:, :, t0:t0 + w], q="sp")
        ocs = [(o, min(128, NMIX - o)) for o in range(0, NMIX, 128)]
        wl = self.w_in[l]
        cur = self.load_w(wl[:, 0:ocs[0][1]], 8, ocs[0][1])
        for i, (o0, ow) in enumerate(ocs):
            nxt = None
            if i + 1 < len(ocs):
                o1, ow1 = ocs[i + 1]
                nxt = self.load_w(wl[:, o1:o1 + ow1], 8, ow1)
            for (t0, w) in chunks(0, Lp):
                ps = self.psR.next()
                for k in range(8):
                    fw.mm(ps[:ow, :w], cur[:, k, :], n[:, k, t0:t0 + w], start=(k == 0), stop=(k == 7))
                ot = self.t32.next()
                fw.evac(ot[:ow, :w], ps[:ow, :w])
                fw.dma(self.zT[o0:o0 + ow, t0:t0 + w].fresh(), ot[:ow, :w], q="sp")
            cur = nxt

    def mla(self, l, es):
        fw, Lp, NB = self.fw, self.Lp, self.NB
        scale = 96.0 ** -0.5
        Q = [fw.sb([96, Lp], BF16, es) for _ in range(4)]
        K = [fw.sb([96, Lp], BF16, es) for _ in range(4)]
        Vt = fw.sb([128, NB, 256], BF16, es)
        ropeb = Rot([fw.sb([96, 2, 512], F32, es) for _ in range(2)])
        wuq_t = self.load_w(self.w_uq[l], 2, 384)
        wuq = fw.sb([128, 2, 384], BF16, es)
        fw.copy(wuq, wuq_t, eng="pool")
        wuq_s = fw.sb([128, 2, 384], BF16, es)
        fw.copy(wuq_s, wuq_t, eng="pool")
        src = self.w_uq[l].rr("(k p) (h c) -> p k h c", p=128, h=4)
        st2 = self.wst.next()[:, 0:256].rr("p (k c) -> p k c", k=2)
        for k in range(2):
            fw.dma(st2[:, k, 0:64].rr("p (h c) -> p h c", h=4), src[:, k, :, 80:96])
            fw.dma(st2[:, k, 64:128].rr("p (h c) -> p h c", h=4), src[:, k, :, 64:80])
        wv = wuq_s.rr("p k (h c) -> p k h c", h=4)
        for k in range(2):
            fw.copy(wv[:, k, :, 64:80], st2[:, k, 0:64].rr("p (h c) -> p h c", h=4), eng="pool")
            fw.copy(wv[:, k, :, 80:96], st2[:, k, 64:128].rr("p (h c) -> p h c", h=4), eng="pool")
        wukv = self.load_w(self.w_ukv[l], 1, 512)
        wkn = fw.sb([128, 4, 64], BF16, es)
        wvv = fw.sb([128, 256], BF16, es)
        wk4 = wukv[:, 0, :].rr("p (h c) -> p h c", h=4)
        fw.copy(wkn, wk4[:, :, 0:64], eng="pool")
        fw.copy(wvv.rr("p (h c) -> p h c", h=4), wk4[:, :, 64:128], eng="pool")
        cq2 = Rot([fw.sb([128, 2, 512], F32, es) for _ in range(2)])
        sq2 = Rot([fw.sb([128, 2, 512], BF16, es) for _ in range(2)])
        cqn = Rot([fw.sb([128, 2, 512], BF16, es) for _ in range(2)])
        kr2 = Rot([fw.sb([96, 2, 512], F32, es) for _ in range(2)])
        krr = Rot([fw.sb([96, 512], BF16, es) for _ in range(2)])
        zcq = self.zT[0:256, :].rr("(k p) t -> p k t", p=128)
        for (t0, w) in chunks(0, Lp):
            c = cq2.next()
            fw.dma(c[:, :, :w], zcq[:, 0:2, t0:t0 + w].fresh())
            ck = self.t32.next()
            fw.dma(ck[:, :w], self.zT[OFF_CKV:OFF_CKV + 128, t0:t0 + w].fresh())
            rc = ropeb.next()
            fw.dma(rc[64:96, 0, :w], self.rope_d[0:32, t0:t0 + w])
            fw.dma(rc[64:96, 1, :w], self.rope_d[32:64, t0:t0 + w])
            kr = kr2.next()
            fw.dma(kr[64:96, 0, :w], self.zT[OFF_KR:OFF_KR + 32, t0:t0 + w].fresh())
            fw.dma(kr[64:80, 1, :w], self.zT[OFF_KR + 16:OFF_KR + 32, t0:t0 + w].fresh())
            fw.dma(kr[80:96, 1, :w], self.zT[OFF_KR:OFF_KR + 16, t0:t0 + w].fresh())
            s = sq2.next()
            fw.act(s[:, :, :w], c[:, :, :w], AF.Square)
            ps = self.psR.next()
            for k in range(2):
                fw.mm(ps[:, :w], self.ones_bf, s[:, k, :w], start=(k == 0), stop=(k == 1))
            rstd = self.t32.next()
            self.rstd_from_ss(rstd[:, :w], ps[:, :w], 256.0, EPS)
            cn = cqn.next()
            for k in range(2):
                fw.stt(cn[:, k, :w], c[:, k, :w], self.vc(C_QN + k), rstd[:, :w], ALU.mult, ALU.mult)
            s2 = self.tbf.next()
            fw.act(s2[:, :w], ck[:, :w], AF.Square)
            ps = self.psR.next()
            fw.mm(ps[:, :w], self.ones_bf, s2[:, :w])
            rstd2 = self.t32.next()
            self.rstd_from_ss(rstd2[:, :w], ps[:, :w], 128.0, EPS)
            ckn = self.tbf.next()
            fw.stt(ckn[:, :w], ck[:, :w], self.vc(C_KVN), rstd2[:, :w], ALU.mult, ALU.mult)
            t1 = self.t32.next()
            t2 = self.t32.next()
            fw.tt(t1[64:96, :w], kr[64:96, 0, :w], rc[64:96, 0, :w], ALU.mult)
            fw.tt(t2[64:96, :w], kr[64:96, 1, :w], rc[64:96, 1, :w], ALU.mult)
            kq = krr.next()
            fw.tt(kq[64:96, :w], t1[64:96, :w], t2[64:96, :w], ALU.add)
            for h in range(4):
                p1 = self.psR.next()
                p2 = self.psR.next()
                for k in range(2):
                    fw.mm(p1[0:96, :w], wuq[:, k, h * 96:(h + 1) * 96], cn[:, k, :w], start=(k == 0), stop=(k == 1))
                for k in range(2):
                    fw.mm(p2[0:96, :w], wuq_s[:, k, h * 96:(h + 1) * 96], cn[:, k, :w], start=(k == 0), stop=(k == 1))
                fw.evac(Q[h][0:64, t0:t0 + w], p1[0:64, :w])
                a1 = self.t32.next()
                a2 = self.t32.next()
                fw.tt(a1[64:96, :w], p1[64:96, :w], rc[64:96, 0, :w], ALU.mult)
                fw.tt(a2[64:96, :w], p2[64:96, :w], rc[64:96, 1, :w], ALU.mult)
                fw.tt(Q[h][64:96, t0:t0 + w], a1[64:96, :w], a2[64:96, :w], ALU.add)
                p3 = self.psR.next()
                fw.mm(p3[0:64, :w], wkn[:, h, :], ckn[:, :w])
                fw.evac(K[h][0:64, t0:t0 + w], p3[0:64, :w])
                fw.copy(K[h][64:96, t0:t0 + w], kq[64:96, :w], eng="pool")
            for b in range(w // 128):
                p4 = self.psR.next()
                fw.mm(p4[:, 0:256], ckn[:, b * 128:(b + 1) * 128], wvv)
                fw.evac(Vt[:, (t0 // 128) + b, :], p4[:, 0:256])
        acc = [(self.banks[0], self.banks[1]), (self.banks[2], self.banks[3])]
        psS = Rot(self.banks[4:8])
        its = []
        grp = 0
        for h in range(4):
            for (t0, w) in chunks(0, Lp):
                kbmax = (t0 + w) // 128 - 1
                for kb in range(kbmax + 1):
                    its.append((h, t0, w, kb, kbmax, grp))
                grp += 1

        def s_stage(it):
            h, t0, w, kb, kbmax, g = it
            c0 = max(0, kb - t0 // 128) * 128
            sp = psS.next()
            fw.mm(sp[:, c0:w], K[h][:, kb * 128:(kb + 1) * 128], Q[h][:, t0 + c0:t0 + w])
            return sp
        sp_next = s_stage(its[0])
        for idx, it in enumerate(its):
            h, t0, w, kb, kbmax, g = it
            Ops, Dps = acc[g % 2]
            sp = sp_next
            if idx + 1 < len(its):
                sp_next = s_stage(its[idx + 1])
            i = kb - t0 // 128
            c0 = max(0, i) * 128
            e = self.tbf.next()
            fw.act(e[:, c0:w], sp[:, c0:w], AF.Exp, scale=scale)
            if i >= 0:
                fw.tt(e[:, c0:c0 + 128], e[:, c0:c0 + 128], self.tri_incl, ALU.mult, eng="pool")
            last = (kb == kbmax)
            fw.mm(Ops[0:64, c0:w], Vt[:, kb, h * 64:(h + 1) * 64], e[:, c0:w], start=(kb == 0), stop=last)
            fw.mm(Dps[0:64, c0:w], (self.ones_pad if kb == 0 else self.ones_bf)[:, 0:64], e[:, c0:w],
                  start=(kb == 0), stop=last)
            if last:
                den = self.t32.next()
                fw.ts(den[0:64, :w], Dps[0:64, :w], 1e-30, None, op0=ALU.add)
                fw.recip(den[0:64, :w], den[0:64, :w])
                yb = self.tbf.next()
                fw.tt(yb[0:64, :w], Ops[0:64, :w], den[0:64, :w], ALU.mult)
                fw.dma(self.yT[h * 64:(h + 1) * 64, t0:t0 + w].fresh(), yb[0:64, :w], q="sp")

    def sbatt(self, l, es):
        fw, Lp, NB = self.fw, self.Lp, self.NB
        Q = fw.sb([64, 4, Lp], BF16, es)
        K = fw.sb([64, 4, Lp], BF16, es)
        Vt = fw.sb([128, NB, 256], BF16, es)
        ld = Rot([fw.sb([64, 4, 512], F32, es) for _ in range(1)])
        ldv = Rot([fw.sb([128, 2, 512], F32, es) for _ in range(2)])
        pacc = fw.sb([128, 512], BF16, es)
        zq = self.zT[OFF_SB:OFF_SB + 256, :].rr("(h d) t -> d h t", h=4)
        zk = self.zT[OFF_SB + 256:OFF_SB + 512, :].rr("(h d) t -> d h t", h=4)
        zv = self.zT[OFF_SB + 512:OFF_SB + 768, :].rr("(k p) t -> p k t", p=128)
        for (t0, w) in chunks(0, Lp):
            a = ld.next()
            fw.dma(a[:, :, :w], zq[:, :, t0:t0 + w].fresh())
            fw.copy(Q[:, :, t0:t0 + w], a[:, :, :w], eng="act")
            b = ld.next()
            fw.dma(b[:, :, :w], zk[:, :, t0:t0 + w].fresh())
            fw.copy(K[:, :, t0:t0 + w], b[:, :, :w], eng="dve")
            v = ldv.next()
            fw.dma(v[:, :, :w], zv[:, :, t0:t0 + w].fresh())
            for bb in range(w // 128):
                for k in range(2):
                    pt = self.psR.next()
                    fw.transpose(pt[:, 0:128], v[:, k, bb * 128:(bb + 1) * 128], self.ident)
                    fw.evac(Vt[:, t0 // 128 + bb, k * 128:(k + 1) * 128], pt[:, 0:128])
        L32 = Rot([fw.sb([128, 512], F32, es) for _ in range(8)])
        Lbf = Rot([fw.sb([128, 512], BF16, es) for _ in range(8)])
        OpsL = [self.banks[0], self.banks[1]]
        psZ = Rot(self.banks[2:8])
        its = []
        grp = 0
        for h in range(4):
            for (t0, w) in chunks(0, Lp):
                kbmax = (t0 + w) // 128 - 1
                for kb in range(kbmax, -1, -1):
                    its.append((h, t0, w, kb, kbmax, grp))
                grp += 1

        def stage1(it):
            h, t0, w, kb, kbmax, g = it
            first = (kb == kbmax)
            i = kb - t0 // 128
            c0 = max(0, i) * 128
            if first:
                fw.memset(pacc[:, :w], 0.0)
            zp = psZ.next()
            fw.mm(zp[:, c0:w], K[:, h, kb * 128:(kb + 1) * 128], Q[:, h, t0 + c0:t0 + w])
            e1 = L32.next()
            fw.act(e1[:, c0:w], zp[:, c0:w], AF.Exp, scale=0.125)
            P = Lbf.next()
            fw.act(P[:, c0:w], e1[:, c0:w], AF.Ln, bias=self.epscol(1.0))
            zs = L32.next()
            fw.ts(zs[:, c0:w], zp[:, c0:w], 0.125, None, op0=ALU.mult)
            if i >= 0:
                fw.tt(P[:, c0:c0 + 128], P[:, c0:c0 + 128], self.tri_strict, ALU.mult, eng="pool")
            cp = psZ.next()
            fw.mm(cp[:, c0:w], self.uincl_pad if kb == 0 else self.uincl, P[:, c0:w], start=True, stop=first)
            if not first:
                fw.mm(cp[:, c0:w], self.ones_bf, pacc[:, c0:w], start=False, stop=True)
            lt = L32.next()
            fw.tt(lt[:, c0:w], zs[:, c0:w], cp[:, c0:w], ALU.subtract)
            if kb > 0:
                fw.tt(pacc[:, c0:w], pacc[:, c0:w], P[:, c0:w], ALU.add)
            return lt

        def stage2(it, lt):
            h, t0, w, kb, kbmax, g = it
            Ops = OpsL[g % 2]
            first = (kb == kbmax)
            i = kb - t0 // 128
            c0 = max(0, i) * 128
            A = Lbf.next()
            fw.act(A[:, c0:w], lt[:, c0:w], AF.Exp)
            if i >= 0:
                fw.tt(A[:, c0:c0 + 128], A[:, c0:c0 + 128], self.tri_strict, ALU.mult, eng="pool")
            if first and c0 > 0:
                fw.memset(A[:, 0:c0], 0.0, eng="pool")
            cc = 0 if first else c0
            fw.mm(Ops[0:64, cc:w], Vt[:, kb, h * 64:(h + 1) * 64], A[:, cc:w], start=first, stop=(kb == 0))
            if kb == 0:
                yb = Lbf.next()
                fw.evac(yb[0:64, :w], Ops[0:64, :w])
                fw.dma(self.yT[512 + h * 64:512 + (h + 1) * 64, t0:t0 + w].fresh(), yb[0:64, :w], q="sp")

        lt_next = stage1(its[0])
        for idx, it in enumerate(its):
            lt = lt_next
            if idx + 1 < len(its):
                lt_next = stage1(its[idx + 1])
            stage2(it, lt)

    def gla(self, l, es):
        fw, Lp = self.fw, self.Lp
        a2 = self.load_w(self.gla_a2[l], 1, 128, prows=16)
        Sbd = fw.sb([128, 256], F32, es)
        Sbd_bf = fw.sb([128, 256], BF16, es)
        fw.memset(Sbd, 0.0)
        fw.memset(Sbd_bf, 0.0)
        ldr = Rot([fw.sb([64, 4, 512], F32, es) for _ in range(2)])
        ldv = Rot([fw.sb([128, 2, 512], F32, es) for _ in range(2)])
        s128 = Rot([fw.sb([128, 128], F32, es) for _ in range(3)])
        b128 = Rot([fw.sb([128, 128], BF16, es) for _ in range(12)])
        vtok = Rot([fw.sb([128, 256], BF16, es) for _ in range(2)])
        yst = Rot([fw.sb([64, 4, 128], BF16, es) for _ in range(2)])
        L32 = Rot([fw.sb([128, 512], F32, es) for _ in range(14)])
        Lbf = Rot([fw.sb([128, 512], BF16, es) for _ in range(6)])
        og = OFF_GLA
        zr = self.zT[og + 528:og + 784, :].rr("(h d) t -> d h t", h=4)
        zv = self.zT[og + 256:og + 512, :].rr("(k p) t -> p k t", p=128)
        yv = self.yT[768:1024, :].rr("(h d) t -> d h t", h=4)
        for (t0, w) in chunks(0, Lp):
            al = L32.next()
            fw.dma(al[0:16, :w], self.zT[og + 512:og + 528, t0:t0 + w].fresh())
            alb = Lbf.next()
            fw.copy(alb[0:16, :w], al[0:16, :w])
            xp = self.psR.next()
            fw.mm(xp[:, :w], a2[:, 0, :], alb[0:16, :w])
            e = L32.next()
            fw.act(e[:, :w], xp[:, :w], AF.Exp, bias=self.dcols[:, 19:20], scale=-1.0)
            fw.act(e[:, :w], e[:, :w], AF.Ln, bias=self.epscol(1.0))
            fw.ts(e[:, :w], e[:, :w], -1.0 / 16.0, None, op0=ALU.mult)
            gam = L32.next()
            fw.scan(gam[:, :w], self.rm[:, :w], e[:, :w], 0.0, ALU.mult, ALU.add)
            eg = L32.next()
            fw.act(eg[:, :w], gam[:, :w], AF.Exp)
            eng = L32.next()
            fw.act(eng[:, :w], gam[:, :w], AF.Exp, scale=-1.0)
            q = L32.next()
            fw.dma(q[:, :w], self.zT[og:og + 128, t0:t0 + w].fresh())
            k = L32.next()
            fw.dma(k[:, :w], self.zT[og + 128:og + 256, t0:t0 + w].fresh())
            qt = Lbf.next()
            fw.stt(qt[:, :w], q[:, :w], 32.0 ** -0.5, eg[:, :w], ALU.mult, ALU.mult)
            kt = k
            fw.tt(kt[:, :w], k[:, :w], eng[:, :w], ALU.mult)
            ktb = Lbf.next()
            fw.copy(ktb[:, :w], kt[:, :w], eng="pool")
            v = ldv.next()
            fw.dma(v[:, :, :w], zv[:, :, t0:t0 + w].fresh())
            r = ldr.next()
            fw.dma(r[:, :, :w], zr[:, :, t0:t0 + w].fresh())
            fw.act(r[:, :, :w], r[:, :, :w], AF.Silu)
            for c in range(w // 128):
                cs = slice(c * 128, (c + 1) * 128)
                gend = eg[:, c * 128 + 127:c * 128 + 128]
                vt = vtok.next()
                for kk in range(2):
                    pt = self.psR.next()
                    fw.transpose(pt[:, 0:128], v[:, kk, cs], self.ident)
                    fw.evac(vt[:, kk * 128:(kk + 1) * 128], pt[:, 0:128])
                kh = s128.next()
                fw.ts(kh, kt[:, cs], gend, None, op0=ALU.mult)
                pt = self.psR.next()
                fw.transpose(pt[:, 0:128], kh, self.ident)
                kht = b128.next()
                fw.evac(kht, pt[:, 0:128])
                scp = self.psR.next()
                for h in range(4):
                    khh = b128.next()
                    fw.ts(khh, ktb[:, cs], self.headmask[:, h:h + 1], None, op0=ALU.mult, eng="pool")
                    fw.mm(scp[:, h * 128:(h + 1) * 128], khh, qt[:, cs])
                sc = self.tbf.next()
                fw.tt(sc, scp, self.tri_incl4.rr("p h t -> p (h t)"), ALU.mult)
                op_ = self.psR.next()
                for h in range(4):
                    fw.mm(op_[0:64, h * 128:(h + 1) * 128], Sbd_bf[:, h * 64:(h + 1) * 64], qt[:, cs],
                          start=True, stop=False)
                    fw.mm(op_[0:64, h * 128:(h + 1) * 128], vt[:, h * 64:(h + 1) * 64], sc[:, h * 128:(h + 1) * 128],
                          start=False, stop=True)
                kvp = self.psR.next()
                fw.mm(kvp[:, 0:256], kht, vt)
                tmp = self.t32.next()
                fw.tt(tmp[:, 0:256], kvp[:, 0:256], self.bdmask, ALU.mult)
                fw.stt(Sbd, Sbd, gend, tmp[:, 0:256], ALU.mult, ALU.add)
                fw.copy(Sbd_bf, Sbd, eng="pool")
                osb = self.t32.next()
                fw.evac(osb[0:64, :], op_[0:64, :])
                sq = self.tbf.next()
                fw.act(sq[0:64, :], op_[0:64, :], AF.Square)
                ssp = self.psR.next()
                fw.mm(ssp[0:64, :], self.ones_bf[0:64, 0:64], sq[0:64, :])
                rs = self.t32.next()
                self.rstd_from_ss(rs[0:64, :], ssp[0:64, :], 64.0, EPS, rows=64)
                fw.tt(osb[0:64, :], osb[0:64, :], rs[0:64, :], ALU.mult)
                yo = yst.next()
                for h in range(4):
                    fw.stt(yo[:, h, :], osb[0:64, h * 128:(h + 1) * 128], self.vc(C_GNORM + h, rows=64),
                           r[:, h, cs], ALU.mult, ALU.mult)
                fw.dma(yv[:, :, t0 + c * 128:t0 + (c + 1) * 128].fresh(), yo, q="sp")

    def rwkv(self, l, es):
        fw, Lp = self.fw, self.Lp
        w2 = fw.sb([64, 256], BF16, es)
        a2 = fw.sb([64, 256], BF16, es)
        g2 = fw.sb([128, 256], BF16, es)
        t = self.load_w(self.rw_w2[l], 1, 256, prows=64)
        fw.copy(w2, t[:, 0, :], eng="pool")
        t = self.load_w(self.rw_a2[l], 1, 256, prows=64)
        fw.copy(a2, t[:, 0, :], eng="pool")
        t = self.load_w(self.rw_g2[l], 1, 256, prows=128)
        fw.copy(g2, t[:, 0, :], eng="pool")
        T32 = [fw.sb([64, 64], F32, es) for _ in range(4)]
        Tbf = [fw.sb([64, 64], BF16, es) for _ in range(4)]
        for h in range(4):
            fw.memset(T32[h], 0.0)
            fw.memset(Tbf[h], 0.0)
        halo = Rot([fw.sb([128, 513], F32, es) for _ in range(4)])
        f64 = Rot([fw.sb([64, 512], F32, es) for _ in range(28)])
        h64 = Rot([fw.sb([64, 512], BF16, es) for _ in range(12)])
        s128 = Rot([fw.sb([128, 128], F32, es) for _ in range(4)])
        b128 = Rot([fw.sb([128, 128], BF16, es) for _ in range(32)])
        b64 = Rot([fw.sb([128, 64], BF16, es) for _ in range(12)])
        oz = OFF_RW
        c_decay = -math.exp(-0.5)
        wlb_r = Rot([fw.sb([64, 512], BF16, es) for _ in range(2)])
        alb_r = Rot([fw.sb([64, 512], BF16, es) for _ in range(2)])
        glb_r = Rot([fw.sb([128, 512], BF16, es) for _ in range(2)])

        def load_shift(rows0, nrows, t0, w, mucol, omucol, out):
            hl = halo.next()
            if t0 == 0:
                fw.memset(hl[0:nrows, 0:1], 0.0)
                fw.dma(hl[0:nrows, 1:1 + w], self.zT[rows0:rows0 + nrows, 0:w].fresh())
            else:
                fw.dma(hl[0:nrows, 0:1 + w], self.zT[rows0:rows0 + nrows, t0 - 1:t0 + w].fresh())
            tmp = f64.next() if nrows <= 64 else self.t32.next()
            fw.ts(tmp[0:nrows, :w], hl[0:nrows, 0:w], mucol, None, op0=ALU.mult)
            fw.stt(out, hl[0:nrows, 1:1 + w], omucol, tmp[0:nrows, :w], ALU.mult, ALU.add)

        for (t0, w) in chunks(0, Lp):
            nck = w // 128
            wl = f64.next()
            load_shift(oz + 768, 64, t0, w, self.vc(C_MUWL, rows=64), self.dcols[0:64, 12:13], wl[:, :w])
            wlb = wlb_r.next()
            fw.act(wlb[:, :w], wl[:, :w], AF.Tanh)
            al = f64.next()
            load_shift(oz + 832, 64, t0, w, self.vc(C_MUAL, rows=64), self.dcols[0:64, 13:14], al[:, :w])
            alb = alb_r.next()
            fw.copy(alb[:, :w], al[:, :w])
            gl = self.t32.next()
            load_shift(oz + 896, 128, t0, w, self.vc(C_MUGL), self.dcols[:, 14:15], gl[:, :w])
            glb = glb_r.next()
            fw.act(glb[:, :w], gl[:, :w], AF.Sigmoid)
            for h in range(4):
                r32, k32, v32 = f64.next(), f64.next(), f64.next()
                load_shift(oz + h * 64, 64, t0, w, self.vc(C_MURKV + h, rows=64), self.dcols[0:64, h:h + 1], r32[:, :w])
                load_shift(oz + 256 + h * 64, 64, t0, w, self.vc(C_MURKV + 4 + h, rows=64),
                           self.dcols[0:64, 4 + h:5 + h], k32[:, :w])
                load_shift(oz + 512 + h * 64, 64, t0, w, self.vc(C_MURKV + 8 + h, rows=64),
                           self.dcols[0:64, 8 + h:9 + h], v32[:, :w])
                pw = self.psR.next()
                fw.mm(pw[0:64, :w], w2[:, h * 64:(h + 1) * 64], wlb[:, :w])
                logw = f64.next()
                fw.act(logw[:, :w], pw[0:64, :w], AF.Sigmoid, bias=self.vc(C_W0 + h, rows=64))
                fw.ts(logw[:, :w], logw[:, :w], c_decay, None, op0=ALU.mult)
                pa = self.psR.next()
                fw.mm(pa[0:64, :w], a2[:, h * 64:(h + 1) * 64], alb[:, :w])
                alpha = f64.next()
                fw.act(alpha[:, :w], pa[0:64, :w], AF.Sigmoid, bias=self.vc(C_A0 + h, rows=64))
                pg = self.psR.next()
                fw.mm(pg[0:64, :w], g2[:, h * 64:(h + 1) * 64], glb[:, :w])
                g32 = f64.next()
                fw.evac(g32[:, :w], pg[0:64, :w])
                kkr = f64.next()
                fw.ts(kkr[:, :w], k32[:, :w], self.vc(C_KK + h, rows=64), None, op0=ALU.mult)
                sqk = h64.next()
                fw.act(sqk[:, :w], kkr[:, :w], AF.Square)
                pss = self.psR.next()
                fw.mm(pss[0:64, :w], self.ones_bf[0:64, 0:64], sqk[:, :w])
                rn = f64.next()
                fw.act(rn[:, :w], pss[0:64, :w], AF.Ln, bias=self.epscol(1e-24)[0:64])
                fw.act(rn[:, :w], rn[:, :w], AF.Exp, scale=-0.5)
                kk = kkr
                fw.tt(kk[:, :w], kkr[:, :w], rn[:, :w], ALU.mult)
                kmod = f64.next()
                fw.ts(kmod[:, :w], alpha[:, :w], self.vc(C_KA + h, rows=64), self.dcols[0:64, 15 + h:16 + h],
                      op0=ALU.mult, op1=ALU.add)
                fw.tt(kmod[:, :w], kmod[:, :w], k32[:, :w], ALU.mult)
                gam = f64.next()
                fw.scan(gam[:, :w], self.rm[0:64, :w], logw[:, :w], 0.0, ALU.mult, ALU.add)
                eg = f64.next()
                fw.act(eg[:, :w], gam[:, :w], AF.Exp)
                eng = f64.next()
                fw.act(eng[:, :w], gam[:, :w], AF.Exp, scale=-1.0)
                egm = f64.next()
                fw.tt(egm[:, :w], gam[:, :w], logw[:, :w], ALU.subtract)
                fw.act(egm[:, :w], egm[:, :w], AF.Exp)
                rt = h64.next()
                fw.tt(rt[:, :w], r32[:, :w], eg[:, :w], ALU.mult)
                kt32 = f64.next()
                fw.tt(kt32[:, :w], kmod[:, :w], eng[:, :w], ALU.mult)
                ktb = h64.next()
                fw.copy(ktb[:, :w], kt32[:, :w], eng="pool")
                bt32 = f64.next()
                fw.tt(bt32[:, :w], kk[:, :w], alpha[:, :w], ALU.mult)
                fw.tt(bt32[:, :w], bt32[:, :w], eng[:, :w], ALU.mult)
                btb = h64.next()
                fw.copy(btb[:, :w], bt32[:, :w], eng="pool")
                atb = h64.next()
                fw.stt(atb[:, :w], kk[:, :w], -1.0, egm[:, :w], ALU.mult, ALU.mult)
                rk = h64.next()
                fw.stt(rk[:, :w], r32[:, :w], self.vc(C_RK + h, rows=64), kmod[:, :w], ALU.mult, ALU.mult)
                pb = self.psR.next()
                fw.mm(pb[0:64, :w], self.ones_bf[0:64, 0:64], rk[:, :w])
                bon = f64.next()
                fw.tt(bon[:, :w], pb[0:64, :w], v32[:, :w], ALU.mult)
                y32 = f64.next()
                for c in range(nck):
                    cs = slice(c * 128, (c + 1) * 128)
                    gend = eg[:, c * 128 + 127:c * 128 + 128]
                    pt = self.psR.next()
                    fw.transpose(pt[:, 0:64], v32[:, cs], self.ident[0:64, 0:64])
                    vt = b64.next()
                    fw.evac(vt, pt[:, 0:64])
                    kh = s128.next()
                    fw.ts(kh[0:64, :], kt32[:, cs], gend, None, op0=ALU.mult)
                    pt = self.psR.next()
                    fw.transpose(pt[:, 0:64], kh[0:64, :], self.ident[0:64, 0:64])
                    kht = b64.next()
                    fw.evac(kht, pt[:, 0:64])
                    bh = s128.next()
                    fw.ts(bh[0:64, :], bt32[:, cs], gend, None, op0=ALU.mult)
                    pt = self.psR.next()
                    fw.transpose(pt[:, 0:64], bh[0:64, :], self.ident[0:64, 0:64])
                    bht = b64.next()
                    fw.evac(bht, pt[:, 0:64])
                    pn = self.psR.next()
                    fw.mm(pn[:, 0:128], btb[:, cs], atb[:, cs])
                    fw.mm(pn[:, 128:256], atb[:, cs], btb[:, cs])
                    fw.mm(pn[:, 256:384], ktb[:, cs], atb[:, cs])
                    N = b128.next()
                    fw.tt(N, pn[:, 0:128], self.tri_strict32, ALU.mult)
                    NT = b128.next()
                    fw.tt(NT, pn[:, 128:256], self.tri_sl32, ALU.mult)
                    AakT = b128.next()
                    fw.tt(AakT, pn[:, 256:384], self.tri_strict32, ALU.mult)
                    pr = self.psR.next()
                    fw.mm(pr[:, 0:128], ktb[:, cs], rt[:, cs])
                    fw.mm(pr[:, 128:256], btb[:, cs], rt[:, cs])
                    ArkT = b128.next()
                    fw.tt(ArkT, pr[:, 0:128], self.tri_incl32, ALU.mult)
                    ArbT = b128.next()
                    fw.tt(ArbT, pr[:, 128:256], self.tri_incl32, ALU.mult)
                    P = b128.next()
                    fw.tt(P, N, self.ident_bf, ALU.add)
                    for lev in range(6):
                        pq = self.psR.next()
                        fw.mm(pq[:, 128:256], N, NT)
                        if lev < 5:
                            fw.mm(pq[:, 0:128], NT, N)
                            N2 = b128.next()
                            fw.evac(N2, pq[:, 0:128])
                        NT2 = b128.next()
                        fw.evac(NT2, pq[:, 128:256])
                        pp = self.psR.next()
                        fw.mm(pp[:, 0:128], NT2, P)
                        P2 = b128.next()
                        fw.tt(P2, P, pp[:, 0:128], ALU.add)
                        P = P2
                        NT = NT2
                        if lev < 5:
                            N = N2
                    MT = P
                    p0 = self.psR.next()
                    fw.mm(p0[:, 0:64], atb[:, cs], Tbf[h], start=True, stop=False)
                    fw.mm(p0[:, 0:64], AakT, vt, start=False, stop=True)
                    rhs0 = b64.next()
                    fw.evac(rhs0, p0[:, 0:64])
                    pu = self.psR.next()
                    fw.mm(pu[:, 0:64], MT, rhs0)
                    U = b64.next()
                    fw.evac(U, pu[:, 0:64])
                    py = self.psR.next()
                    fw.mm(py[0:64, 0:128], Tbf[h], rt[:, cs], start=True, stop=False)
                    fw.mm(py[0:64, 0:128], vt, ArkT, start=False, stop=False)
                    fw.mm(py[0:64, 0:128], U, ArbT, start=False, stop=True)
                    fw.evac(y32[:, cs], py[0:64, 0:128])
                    pT = self.psR.next()
                    fw.mm(pT[0:64, 0:64], kht, vt, start=True, stop=False)
                    fw.mm(pT[0:64, 0:64], bht, U, start=False, stop=True)
                    fw.stt(T32[h], T32[h], gend, pT[0:64, 0:64], ALU.mult, ALU.add)
                    fw.copy(Tbf[h], T32[h], eng="pool")
                ybf = h64.next()
                fw.copy(ybf[:, :w], y32[:, :w], eng="pool")
                pm = self.psR.next()
                fw.mm(pm[0:64, :w], self.ones_bf[0:64, 0:64], ybf[:, :w])
                yc = f64.next()
                fw.stt(yc[:, :w], pm[0:64, :w], -1.0 / 64.0, y32[:, :w], ALU.mult, ALU.add)
                sq = h64.next()
                fw.act(sq[:, :w], yc[:, :w], AF.Square)
                pv = self.psR.next()
                fw.mm(pv[0:64, :w], self.ones_bf[0:64, 0:64], sq[:, :w])
                rs = f64.next()
                self.rstd_from_ss(rs[:, :w], pv[0:64, :w], 64.0, 64e-5, rows=64)
                fw.tt(yc[:, :w], yc[:, :w], rs[:, :w], ALU.mult)
                fw.ts(yc[:, :w], yc[:, :w], self.vc(C_LNW + h, rows=64), self.vc(C_LNB + h, rows=64),
                      op0=ALU.mult, op1=ALU.add)
                fw.tt(yc[:, :w], yc[:, :w], bon[:, :w], ALU.add)
                yo = h64.next()
                fw.tt(yo[:, :w], yc[:, :w], g32[:, :w], ALU.mult)
                fw.dma(self.yT[256 + h * 64:256 + (h + 1) * 64, t0:t0 + w].fresh(), yo[:, :w], q="sp")

    def phase_C1(self, l, es):
        fw, Lp = self.fw, self.Lp
        TS = 1536
        n = fw.sb([128, 8, TS], BF16, es)
        y = fw.sb([128, 8, TS], BF16, es)
        mg = fw.sb([128, 8, TS], BF16, es)
        acc = fw.sb([128, TS], F32, es)
        nTv = self.nT.rr("(k p) t -> p k t", p=128)
        yTv = self.yT.rr("(k p) t -> p k t", p=128)
        wl = self.w_in[l]
        for (s0, sw) in chunks(0, Lp, TS):
            for (t0, w) in chunks(0, sw):
                fw.dma(n[:, :, t0:t0 + w], nTv[:, :, s0 + t0:s0 + t0 + w].fresh())
                fw.dma(y[:, :, t0:t0 + w], yTv[:, :, s0 + t0:s0 + t0 + w].fresh())
            jobs = [(d, m) for d in range(8) for m in range(4)]

            def loadj(j):
                d, m = j
                c0 = OFF_GATE + m * 1024 + d * 128
                return (self.load_w(wl[:, c0:c0 + 128], 8, 128),
                        self.load_w(self.w_branch[l][m][:, d * 128:(d + 1) * 128], 2, 128))
            curw = loadj(jobs[0])
            for ji, (d, m) in enumerate(jobs):
                    nxtw = loadj(jobs[ji + 1]) if ji + 1 < len(jobs) else None
                    wg, wb = curw
                    curw = nxtw
                    for (t0, w) in chunks(0, sw):
                        pg = self.psR.next()
                        for k in range(8):
                            fw.mm(pg[:, :w], wg[:, k, :], n[:, k, t0:t0 + w], start=(k == 0), stop=(k == 7))
                        pb = self.psR.next()
                        for k in range(2):
                            fw.mm(pb[:, :w], wb[:, k, :], y[:, 2 * m + k, t0:t0 + w], start=(k == 0), stop=(k == 1))
                        gt = self.t32.next()
                        fw.act(gt[:, :w], pg[:, :w], AF.Sigmoid, bias=self.vc(C_GATEB + m * 8 + d))
                        if m == 0:
                            fw.tt(acc[:, t0:t0 + w], gt[:, :w], pb[:, :w], ALU.mult)
                        else:
                            fw.tt(gt[:, :w], gt[:, :w], pb[:, :w], ALU.mult)
                            if m < 3:
                                fw.tt(acc[:, t0:t0 + w], acc[:, t0:t0 + w], gt[:, :w], ALU.add, eng="pool")
                            else:
                                fw.tt(mg[:, d, t0:t0 + w], acc[:, t0:t0 + w], gt[:, :w], ALU.add, eng="pool")
            curo = self.load_w(self.w_out[l][:, 0:128], 8, 128)
            for d in range(8):
                wo = curo
                if d + 1 < 8:
                    curo = self.load_w(self.w_out[l][:, (d + 1) * 128:(d + 2) * 128], 8, 128)
                for (t0, w) in chunks(0, sw):
                    g0 = s0 + t0
                    po = self.psR.next()
                    for k in range(8):
                        fw.mm(po[:, :w], wo[:, k, :], mg[:, k, t0:t0 + w], start=(k == 0), stop=(k == 7))
                    lo = PADC if g0 == 0 else 0
                    hc = self.t32.next()
                    cell = self.hcell(d, g0 + lo, w - lo)
                    fw.dma(hc[:, lo:w], cell)
                    fw.tt(hc[:, lo:w], hc[:, lo:w], po[:, lo:w], ALU.add)
                    fw.dma(cell, hc[:, lo:w], q="sp")

    def phase_C2(self, l, es):
        fw, Lp = self.fw, self.Lp
        TS = 1536
        n2 = fw.sb([128, 8, TS], BF16, es)
        g = fw.sb([128, 22, TS], BF16, es)
        carry = fw.sb([128, 22, 2], F32, es)
        fw.memset(carry, 0.0)
        self.h8 = Rot([fw.sb([128, 8, 512], F32, es) for _ in range(1)])
        self.sq8 = Rot([fw.sb([128, 8, 512], BF16, es) for _ in range(1)])
        asb = Rot([fw.sb([128, 514], F32, es) for _ in range(3)])
        wfi = self.w_ffn_in[l]
        for (s0, sw) in chunks(0, Lp, TS):
            for (t0, w) in chunks(0, sw):
                hc = self.h8.next()
                fw.dma(hc[:, :, :w], self.hall(s0 + t0, w))
                self.rmsnorm_chunk(hc, w, C_NFFN, self.vecs, lambda k: n2[:, k, t0:t0 + w])
            def loadf(fc):
                return (self.load_w(wfi[:, fc * 128:(fc + 1) * 128], 8, 128),
                        self.load_w(wfi[:, DFF + fc * 128:DFF + (fc + 1) * 128], 8, 128))
            curw = loadf(0)
            for fc in range(22):
                wa, wu = curw
                if fc + 1 < 22:
                    curw = loadf(fc + 1)
                for (t0, w) in chunks(0, sw):
                    pa = self.psR.next()
                    for k in range(8):
                        fw.mm(pa[:, :w], wa[:, k, :], n2[:, k, t0:t0 + w], start=(k == 0), stop=(k == 7))
                    pu = self.psR.next()
                    for k in range(8):
                        fw.mm(pu[:, :w], wu[:, k, :], n2[:, k, t0:t0 + w], start=(k == 0), stop=(k == 7))
                    a = asb.next()
                    fw.copy(a[:, 0:2], carry[:, fc, :], eng="pool")
                    fw.copy(a[:, 2:2 + w], pa[:, :w], eng="act")
                    fw.copy(carry[:, fc, :], a[:, w:w + 2], eng="pool")
                    c = self.t32.next()
                    fw.ts(c[:, :w], a[:, 0:w], self.vc(C_CONVW + fc), self.vc(C_CONVB + fc), op0=ALU.mult, op1=ALU.add)
                    fw.stt(c[:, :w], a[:, 1:1 + w], self.vc(C_CONVW + 22 + fc), c[:, :w], ALU.mult, ALU.add)
                    fw.stt(c[:, :w], a[:, 2:2 + w], self.vc(C_CONVW + 44 + fc), c[:, :w], ALU.mult, ALU.add)
                    fw.act(c[:, :w], c[:, :w], AF.Silu)
                    fw.tt(g[:, fc, t0:t0 + w], c[:, :w], pu[:, :w], ALU.mult)
            curo = self.load_w(self.w_ffn_out[l][:, 0:128], 22, 128)
            for d in range(8):
                wo = curo
                if d + 1 < 8:
                    curo = self.load_w(self.w_ffn_out[l][:, (d + 1) * 128:(d + 2) * 128], 22, 128)
                for (t0, w) in chunks(0, sw):
                    g0 = s0 + t0
                    po = self.psR.next()
                    for k in range(22):
                        fw.mm(po[:, :w], wo[:, k, :], g[:, k, t0:t0 + w], start=(k == 0), stop=(k == 21))
                    lo = PADC if g0 == 0 else 0
                    hc = self.t32.next()
                    cell = self.hcell(d, g0 + lo, w - lo)
                    fw.dma(hc[:, lo:w], cell)
                    fw.tt(hc[:, lo:w], hc[:, lo:w], po[:, lo:w], ALU.add)
                    fw.dma(cell, hc[:, lo:w], q="sp")

    def phase_F(self, es):
        fw, Lp = self.fw, self.Lp
        self.h8 = Rot([fw.sb([128, 8, 512], F32, es) for _ in range(2)])
        self.sq8 = Rot([fw.sb([128, 8, 512], BF16, es) for _ in range(2)])
        o8 = Rot([fw.sb([128, 8, 512], F32, es) for _ in range(2)])
        ov = self.outT.rr("(k p) t -> p k t", p=128)
        for (t0, w) in chunks(0, Lp):
            hc = self.h8.next()
            fw.dma(hc[:, :, :w], self.hall(t0, w))
            o = o8.next()
            self.rmsnorm_chunk(hc, w, 0, self.gvec, lambda k: o[:, k, :w])
            lo = 128 if t0 == 0 else 0
            if w - lo > 0:
                fw.dma(ov[:, :, t0 + lo - 128:t0 + w - 128].fresh(), o[:, :, lo:w], q="sp")


def _cols(v, p=128):
    a = np.asarray(v, np.float32).reshape(-1, p).T
    if p < 128:
        a = np.concatenate([a, np.zeros((128 - p, a.shape[1]), np.float32)], 0)
    return a


def pack_vecs(inp, l):
    out = np.zeros((128, NV), np.float32)

    def put(c, a):
        out[:, c:c + a.shape[1]] = a
    put(C_NMIX, _cols(inp["norm_mix"][l]))
    put(C_NFFN, _cols(inp["norm_ffn"][l]))
    put(C_GATEB, _cols(inp["gate_b"][l].reshape(-1)))
    put(C_CONVW, _cols(inp["ffn_conv_w"][l].reshape(-1)))
    put(C_CONVB, _cols(inp["ffn_conv_b"][l]))
    put(C_QN, _cols(inp["mla_q_norm"][l]))
    put(C_KVN, _cols(inp["mla_kv_norm"][l]))
    mu = inp["rw_mu"][l]
    put(C_MURKV, _cols(mu[0:768], 64))
    put(C_MUWL, _cols(mu[768:832], 64))
    put(C_MUAL, _cols(mu[832:896], 64))
    put(C_MUGL, _cols(mu[896:1024]))
    put(C_W0, _cols(inp["rw_w0"][l], 64))
    put(C_A0, _cols(inp["rw_a0"][l], 64))
    put(C_KK, _cols(inp["rw_k_k"][l], 64))
    put(C_KA, _cols(inp["rw_k_a"][l], 64))
    put(C_RK, _cols(inp["rw_r_k"][l].reshape(-1), 64))
    put(C_LNW, _cols(inp["rw_ln_w"][l], 64))
    put(C_LNB, _cols(inp["rw_ln_b"][l], 64))
    put(C_GAB, _cols(inp["gla_a_b"][l]))
    put(C_GNORM, _cols(inp["gla_norm"][l], 64))
    return out


def rope_table(Lp):
    half = 16
    freqs = (np.float32(10000.0) ** (-np.arange(half, dtype=np.float32) / np.float32(half))).astype(np.float32)
    pos = (np.arange(Lp) - PADC).astype(np.float32)
    ang = (pos[None, :] * freqs[:, None]).astype(np.float32)
    c, s = np.cos(ang).astype(np.float32), np.sin(ang).astype(np.float32)
    return np.concatenate([c, c, -s, s], 0).astype(np.float32)


_CACHE = {}


def run(inputs, NB, DEPTH, dbg=(), n_cores=8, phases=None):
    key = (NB, DEPTH, tuple(dbg), phases)
    if key not in _CACHE:
        _CACHE[key] = Builder(NB, DEPTH, dbg, phases).build()
    nc = _CACHE[key]
    Lp = NB * 128
    x = np.asarray(inputs["x"], np.float32)
    B = x.shape[0]
    meta = np.asarray(inputs["meta_tokens"], np.float32)
    shared = {
        "w_in": np.ascontiguousarray(inputs["w_in"][:DEPTH], np.float32),
        "mla_w_uq": np.ascontiguousarray(inputs["mla_w_uq"][:DEPTH], np.float32),
        "mla_w_ukv": np.ascontiguousarray(inputs["mla_w_ukv"][:DEPTH], np.float32),
        "rw_w2": np.ascontiguousarray(inputs["rw_w2"][:DEPTH], np.float32),
        "rw_a2": np.ascontiguousarray(inputs["rw_a2"][:DEPTH], np.float32),
        "rw_g2": np.ascontiguousarray(inputs["rw_g2"][:DEPTH], np.float32),
        "gla_a2": np.ascontiguousarray(inputs["gla_a2"][:DEPTH], np.float32),
        "w_branch": np.ascontiguousarray(inputs["w_branch"][:DEPTH], np.float32),
        "w_out": np.ascontiguousarray(inputs["w_out"][:DEPTH], np.float32),
        "w_ffn_in": np.ascontiguousarray(inputs["w_ffn_in"][:DEPTH], np.float32),
        "w_ffn_out": np.ascontiguousarray(inputs["w_ffn_out"][:DEPTH], np.float32),
        "vecs": np.stack([pack_vecs(inputs, l) for l in range(DEPTH)], 0),
        "gvec": _cols(inputs["norm_final"]),
        "rope": rope_table(Lp),
    }
    in_maps = []
    for c in range(n_cores):
        b = c % B
        hT0 = np.zeros((1024, Lp), np.float32)
        hT0[:, PADC:PADC + 16] = meta.T
        hT0[:, 128:] = x[b].T
        m = dict(shared)
        m["hT0"] = hT0
        in_maps.append(m)
    res = run_bass_kernel_spmd(nc, in_maps, core_ids=list(range(n_cores)))
    return res.results


def kernel(**inputs):
    x = np.asarray(inputs["x"])
    B, SEQ, D = x.shape
    NB = (SEQ + 128) // 128
    results = run(inputs, NB, 4)
    out = np.stack([np.ascontiguousarray(results[b]["outT"].T) for b in range(B)], 0)
    return out.astype(np.float32)
```

```python
import math
import numpy as np
from contextlib import ExitStack
import concourse.bass as bass
import concourse.mybir as mybir
from concourse.bass_utils import run_bass_kernel_spmd

F32 = mybir.dt.float32
BF16 = mybir.dt.bfloat16
ALU = mybir.AluOpType
AF = mybir.ActivationFunctionType
AX = mybir.AxisListType

NDMA = 24
EPS = 1e-6
PADC = 112
NMIX = 2992
OFF_CQ, OFF_CKV, OFF_KR, OFF_RW, OFF_SB, OFF_GLA, OFF_GATE = 0, 256, 384, 416, 1440, 2208, 2992
DFF = 2816
NV = 192
(C_NMIX, C_NFFN, C_GATEB, C_CONVW, C_CONVB, C_QN, C_KVN, C_MURKV, C_MUWL, C_MUAL, C_MUGL,
 C_W0, C_A0, C_KK, C_KA, C_RK, C_LNW, C_LNB, C_GAB, C_GNORM) = (
    0, 8, 16, 48, 114, 136, 138, 139, 151, 152, 153, 154, 158, 162, 166, 170, 174, 178, 182, 183)


class Res:
    __slots__ = ("w", "r", "excl")

    def __init__(self, excl=False):
        self.w = {}
        self.r = {}
        self.excl = excl


def _rl(v):
    r = v.res
    return r if isinstance(r, tuple) else (r,)


class V:
    __slots__ = ("ap", "res")

    def __init__(self, ap, res=None):
        self.ap = ap
        self.res = res if res is not None else Res()

    def __getitem__(self, idx):
        return V(self.ap[idx], self.res)

    def rr(self, pat, **kw):
        return V(self.ap.rearrange(pat, **kw), self.res)

    def fresh(self):
        return V(self.ap, Res())

    def withres(self, res):
        return V(self.ap, res)


class _Eng:
    def __init__(self, name, obj, sem):
        self.name, self.obj, self.sem = name, obj, sem
        self.n = 0
        self.waited = {}
        self.pending = False


class Rot:
    def __init__(self, items):
        self.items = items
        self.i = 0

    def next(self):
        x = self.items[self.i]
        self.i = (self.i + 1) % len(self.items)
        return x


class FW:
    def __init__(self, nc, es):
        self.nc = nc
        self.es = es
        self.engs = {}
        for name, obj in (("pe", nc.tensor), ("act", nc.scalar), ("dve", nc.vector),
                          ("pool", nc.gpsimd), ("sp", nc.sync)):
            sem = es.enter_context(nc.semaphore("s_" + name))
            self.engs[name] = _Eng(name, obj, sem)
        self.dma_sems = [es.enter_context(nc.semaphore("d%d" % i)) for i in range(NDMA)]
        self.dma_cnt = [0] * NDMA
        self.dma_next = 0
        self.nins = 0
        self._uid = 0
        self._ev = 0

    def sb(self, shape, dt=F32, es=None):
        self._uid += 1
        t = (es or self.es).enter_context(self.nc.sbuf_tensor("sb%d" % self._uid, list(shape), dt))
        return V(t[:], Res())

    def ps(self, shape, dt=F32, es=None):
        self._uid += 1
        t = (es or self.es).enter_context(self.nc.psum_tensor("ps%d" % self._uid, list(shape), dt))
        return V(t[:], Res(excl=True))

    def _wait(self, eng, tok):
        key, sem, val, src = tok
        if src == "pe" and eng.name == "pe":
            return
        if eng.waited.get(key, 0) >= val:
            return
        eng.obj.wait_ge(sem, val)
        eng.waited[key] = val
        self.nins += 1

    def _deps(self, reads, writes):
        toks = []
        for v in reads:
            for r in _rl(v):
                toks.extend(r.w.values())
                if r.excl:
                    toks.extend(t for t in r.r.values() if t[3] != self._cur)
        for v in writes:
            for r in _rl(v):
                toks.extend(r.w.values())
                toks.extend(r.r.values())
        return toks

    def _mark(self, key, tok, reads, writes):
        wres = []
        for v in writes:
            for r in _rl(v):
                r.w = {key: tok}
                r.r = {}
                wres.append(r)
        for v in reads:
            for r in _rl(v):
                if r not in wres:
                    r.r[key] = tok

    def op(self, engname, fn, reads, writes, inc=True):
        eng = self.engs[engname]
        self._cur = engname
        for t in self._deps(reads, writes):
            self._wait(eng, t)
        ins = fn(eng.obj)
        self.nins += 1
        if inc:
            eng.n += 1
            ins.then_inc(eng.sem, 1)
            tok = (engname, eng.sem, eng.n, engname)
            eng.pending = False
        else:
            tok = (engname, eng.sem, eng.n + 1, engname)
            eng.pending = True
        self._mark(engname, tok, reads, writes)
        return ins

    def dma(self, out, in_, q="sp"):
        eng = self.engs[q]
        self._cur = q
        for t in self._deps([in_], [out]):
            self._wait(eng, t)
        i = self.dma_next
        self.dma_next = (i + 1) % NDMA
        key = ("dma", i)
        if self.dma_cnt[i] > 0:
            self._wait(eng, (key, self.dma_sems[i], self.dma_cnt[i], None))
        self.dma_cnt[i] += 16
        eng.obj.dma_start(out=out.ap, in_=in_.ap).then_inc(self.dma_sems[i], 16)
        self.nins += 1
        tok = (key, self.dma_sems[i], self.dma_cnt[i], None)
        self._mark(key, tok, [in_], [out])

    def barrier(self, engines=("pe", "act", "dve", "pool", "sp")):
        for en in engines:
            eng = self.engs[en]
            for i in range(NDMA):
                if self.dma_cnt[i] > 0:
                    self._wait(eng, (("dma", i), self.dma_sems[i], self.dma_cnt[i], None))
            for name, e in self.engs.items():
                assert not e.pending, name
                if e.n > 0 and name != en:
                    self._wait(eng, (name, e.sem, e.n, None))

    def mm(self, out, lhsT, rhs, start=True, stop=True):
        return self.op("pe", lambda e: e.matmul(out.ap, lhsT.ap, rhs.ap, start=start, stop=stop),
                       [lhsT, rhs], [out], inc=stop)

    def transpose(self, out, in_, ident):
        return self.op("pe", lambda e: e.transpose(out.ap, in_.ap, ident.ap), [in_, ident], [out])

    def act(self, out, in_, func, bias=None, scale=None):
        reads = [in_]
        kw = {}
        if bias is not None:
            if isinstance(bias, V):
                reads.append(bias)
                kw["bias"] = bias.ap
            else:
                kw["bias"] = bias
        if scale is not None:
            if isinstance(scale, V):
                reads.append(scale)
                kw["scale"] = scale.ap
            else:
                kw["scale"] = scale
        return self.op("act", lambda e: e.activation(out.ap, in_.ap, func, **kw), reads, [out])

    def tt(self, out, in0, in1, op, eng="dve"):
        return self.op(eng, lambda e: e.tensor_tensor(out.ap, in0.ap, in1.ap, op), [in0, in1], [out])

    def ts(self, out, in0, s1, s2=None, op0=ALU.mult, op1=None, eng="dve"):
        reads = [in0]
        a1 = s1
        if isinstance(s1, V):
            reads.append(s1)
            a1 = s1.ap
        a2 = s2
        if isinstance(s2, V):
            reads.append(s2)
            a2 = s2.ap
        kw = {}
        if op1 is not None:
            kw["op1"] = op1
        return self.op(eng, lambda e: e.tensor_scalar(out.ap, in0.ap, a1, a2, op0, **kw), reads, [out])

    def stt(self, out, in0, scalar, in1, op0, op1, eng="dve"):
        reads = [in0, in1]
        a = scalar
        if isinstance(scalar, V):
            reads.append(scalar)
            a = scalar.ap
        return self.op(eng, lambda e: e.scalar_tensor_tensor(out.ap, in0.ap, a, in1.ap, op0, op1), reads, [out])

    def copy(self, out, in_, eng="dve"):
        if eng == "act":
            return self.act(out, in_, AF.Copy)
        return self.op(eng, lambda e: e.tensor_copy(out.ap, in_.ap), [in_], [out])

    def evac(self, out, in_):
        self._ev ^= 1
        return self.copy(out, in_, eng="act" if self._ev else "dve")

    def memset(self, out, val, eng="dve"):
        return self.op(eng, lambda e: e.memset(out.ap, val), [], [out])

    def scan(self, out, d0, d1, init, op0, op1):
        return self.op("dve", lambda e: e.tensor_tensor_scan(out.ap, d0.ap, d1.ap, init, op0, op1), [d0, d1], [out])

    def recip(self, out, in_):
        return self.op("dve", lambda e: e.reciprocal(out.ap, in_.ap), [in_], [out])

    def aselect(self, t, cmp, fill, base, pattern, cm):
        return self.op("pool", lambda g: g.affine_select(out=t.ap, in_=t.ap, compare_op=cmp, fill=fill, base=base,
                                                         pattern=pattern, channel_multiplier=cm), [t], [t])


def chunks(t0, t1, maxw=512):
    out = []
    t = t0
    while t < t1:
        w = min(maxw, t1 - t)
        out.append((t, w))
        t += w
    return out


class Builder:
    def __init__(self, NB, DEPTH, dbg=(), phases=None):
        self.phases = phases
        self.NB = NB
        self.Lp = NB * 128
        self.SEQ = self.Lp - 128
        self.DEPTH = DEPTH
        self.dbg = dbg
        self.nc = bass.Bass("TRN2", target_bir_lowering=False)

    def din(self, name, shape, dt=F32):
        return V(self.nc.dram_tensor(name, list(shape), dt, kind="ExternalInput").ap())

    def dscr(self, name, shape, dt=F32):
        kind = "ExternalOutput" if name in self.dbg else "Internal"
        return V(self.nc.dram_tensor(name, list(shape), dt, kind=kind).ap())

    def build(self):
        nc, Lp, DEPTH = self.nc, self.Lp, self.DEPTH
        self.hT0 = self.din("hT0", [1024, Lp])
        self.w_in = self.din("w_in", [DEPTH, 1024, 7088])
        self.w_uq = self.din("mla_w_uq", [DEPTH, 256, 384])
        self.w_ukv = self.din("mla_w_ukv", [DEPTH, 128, 512])
        self.rw_w2 = self.din("rw_w2", [DEPTH, 64, 256])
        self.rw_a2 = self.din("rw_a2", [DEPTH, 64, 256])
        self.rw_g2 = self.din("rw_g2", [DEPTH, 128, 256])
        self.gla_a2 = self.din("gla_a2", [DEPTH, 16, 128])
        self.w_branch = self.din("w_branch", [DEPTH, 4, 256, 1024])
        self.w_out = self.din("w_out", [DEPTH, 1024, 1024])
        self.w_ffn_in = self.din("w_ffn_in", [DEPTH, 1024, 5632])
        self.w_ffn_out = self.din("w_ffn_out", [DEPTH, 2816, 1024])
        self.vecs_d = self.din("vecs", [DEPTH, 128, NV])
        self.gvec_d = self.din("gvec", [128, 8])
        self.rope_d = self.din("rope", [64, Lp])
        self.outT = V(nc.dram_tensor("outT", [1024, self.SEQ], F32, kind="ExternalOutput").ap())
        self.hT = self.dscr("hT", [1024, Lp])
        self.nT = self.dscr("nT", [1024, Lp], BF16)
        self.zT = self.dscr("zT", [NMIX, Lp])
        self.yT = self.dscr("yT", [1024, Lp], BF16)
        nch = (Lp + 511) // 512
        self.hres = [[Res() for _ in range(nch)] for _ in range(8)]

        with ExitStack() as es:
            self.fw = fw = FW(nc, es)
            self.consts(es)
            for k in range(8):
                fw.dma(self.hT[k * 128:(k + 1) * 128, :].withres(tuple(self.hres[k])),
                       self.hT0[k * 128:(k + 1) * 128, :])
            fw.barrier()
            print("sbuf remaining after consts:", nc.sbuf_bytes_remaining)
            for l in range(DEPTH):
                self.layer_setup(l)
                for nm, fn in (("A", self.phase_A), ("mla", self.mla), ("sb", self.sbatt), ("gla", self.gla),
                               ("rwkv", self.rwkv), ("C1", self.phase_C1), ("C2", self.phase_C2)):
                    if self.phases is not None and nm not in self.phases:
                        continue
                    with ExitStack() as e2, nc.named_scope("L%d_%s" % (l, nm)):
                        fn(l, e2)
                        fw.barrier()
            with ExitStack() as e2:
                self.phase_F(e2)
            fw.barrier()
            print("instructions:", fw.nins, {k: e.n for k, e in fw.engs.items()})
        return nc

    def hcell(self, k, t0, w):
        c0, c1 = t0 // 512, (t0 + w - 1) // 512
        res = tuple(self.hres[k][c] for c in range(c0, c1 + 1))
        return self.hT[k * 128:(k + 1) * 128, t0:t0 + w].withres(res if len(res) > 1 else res[0])

    def hall(self, t0, w):
        c0, c1 = t0 // 512, (t0 + w - 1) // 512
        res = tuple(self.hres[k][c] for k in range(8) for c in range(c0, c1 + 1))
        return self.hT.rr("(k p) t -> p k t", p=128)[:, :, t0:t0 + w].withres(res)

    def consts(self, es):
        fw = self.fw
        Lp = self.Lp
        self.ident = fw.sb([128, 128], F32, es)
        fw.memset(self.ident, 0.0, eng="pool")
        fw.aselect(self.ident, ALU.not_equal, 1.0, 0, [[-1, 128]], 1)
        self.ident_bf = fw.sb([128, 128], BF16, es)
        fw.copy(self.ident_bf, self.ident)
        self.ones_bf = fw.sb([128, 128], BF16, es)
        fw.memset(self.ones_bf, 1.0)
        self.zeros_bf = fw.sb([128, 128], BF16, es)
        fw.memset(self.zeros_bf, 0.0)
        padcol = fw.sb([128, 1], F32, es)
        fw.memset(padcol, 1.0, eng="pool")
        fw.aselect(padcol, ALU.is_ge, 0.0, -PADC, [[0, 1]], 1)
        self.ones_pad = fw.sb([128, 128], BF16, es)
        fw.ts(self.ones_pad, self.ones_bf, padcol[:, 0:1], None, op0=ALU.mult)

        def tri(cmp, base, pat, cm):
            t32 = fw.sb([128, 128], F32, es)
            fw.memset(t32, 1.0, eng="pool")
            fw.aselect(t32, cmp, 0.0, base, pat, cm)
            tb = fw.sb([128, 128], BF16, es)
            fw.copy(tb, t32)
            return t32, tb
        self.tri_incl32, self.tri_incl = tri(ALU.is_ge, 0, [[1, 128]], -1)
        self.tri_strict32, self.tri_strict = tri(ALU.is_gt, 0, [[1, 128]], -1)
        self.tri_sl32, self.tri_sl = tri(ALU.is_gt, 0, [[-1, 128]], 1)
        self.tri_ge32, self.uincl = tri(ALU.is_ge, 0, [[-1, 128]], 1)
        self.uincl_pad = fw.sb([128, 128], BF16, es)
        fw.ts(self.uincl_pad, self.uincl, padcol[:, 0:1], None, op0=ALU.mult)
        self.tri_incl4 = fw.sb([128, 4, 128], F32, es)
        for h in range(4):
            fw.copy(self.tri_incl4[:, h, :], self.tri_incl32)
        self.rm = fw.sb([128, 512], F32, es)
        fw.memset(self.rm, 1.0)
        for k in range(4):
            fw.memset(self.rm[:, k * 128:k * 128 + 1], 0.0)
        self.headmask = fw.sb([128, 4], F32, es)
        fw.memset(self.headmask, 1.0, eng="pool")
        fw.aselect(self.headmask, ALU.is_ge, 0.0, 0, [[-32, 4]], 1)
        fw.aselect(self.headmask, ALU.is_ge, 0.0, 31, [[32, 4]], -1)
        self.bdmask = fw.sb([128, 256], F32, es)
        fw.memset(self.bdmask, 1.0, eng="pool")
        bd3 = self.bdmask.rr("p (h c) -> p h c", h=4)
        fw.aselect(bd3, ALU.is_ge, 0.0, 0, [[-32, 4], [0, 64]], 1)
        fw.aselect(bd3, ALU.is_ge, 0.0, 31, [[32, 4], [0, 64]], -1)
        self.vecs = fw.sb([128, NV], F32, es)
        self.gvec = fw.sb([128, 8], F32, es)
        fw.dma(self.gvec, self.gvec_d)
        self.dcols = fw.sb([128, 32], F32, es)
        self._eps = {}
        for ev in (EPS, 1.0, 1e-24, 64e-5):
            t = fw.sb([128, 1], F32, es)
            fw.memset(t, ev)
            self._eps[ev] = t
        banks = [fw.ps([128, 512], F32, es) for _ in range(8)]
        self.banks = banks
        self.psA = banks[0:3]
        self.psR = Rot(banks[3:8])
        self.wst = Rot([fw.sb([128, 1408], F32, es) for _ in range(4)])
        self.wbf = Rot([fw.sb([128, 2816], BF16, es) for _ in range(5)])
        self.t32 = Rot([fw.sb([128, 512], F32, es) for _ in range(6)])
        self.tbf = Rot([fw.sb([128, 512], BF16, es) for _ in range(6)])

    def vc(self, c, n=1, rows=128):
        return self.vecs[0:rows, c:c + n]

    def layer_setup(self, l):
        fw = self.fw
        fw.dma(self.vecs, self.vecs_d[l])
        d = self.dcols
        fw.ts(d[:, 0:15], self.vecs[:, C_MURKV:C_MURKV + 15], -1.0, 1.0, op0=ALU.mult, op1=ALU.add)
        fw.ts(d[:, 15:19], self.vecs[:, C_KA:C_KA + 4], -1.0, 1.0, op0=ALU.mult, op1=ALU.add)
        fw.ts(d[:, 19:20], self.vecs[:, C_GAB:C_GAB + 1], -1.0, None, op0=ALU.mult)

    def load_w(self, src2d, kc, ow, prows=128):
        fw = self.fw
        wb = self.wbf.next()[0:prows, 0:kc * ow].rr("p (k o) -> p k o", k=kc)
        srcv = src2d.rr("(k p) o -> p k o", p=prows)
        kmax = max(1, 1408 // ow)
        k0 = 0
        while k0 < kc:
            kn = min(kmax, kc - k0)
            st = self.wst.next()[0:prows, 0:kn * ow].rr("p (k o) -> p k o", k=kn)
            fw.dma(st, srcv[:, k0:k0 + kn, :])
            fw.copy(wb[:, k0:k0 + kn, :], st, eng="pool")
            k0 += kn
        return wb

    def rstd_from_ss(self, out, ss_ps, n, eps, rows=128):
        fw = self.fw
        fw.act(out, ss_ps, AF.Ln, bias=self.epscol(eps)[0:rows], scale=1.0 / n)
        fw.act(out, out, AF.Exp, scale=-0.5)

    def epscol(self, eps):
        return self._eps[eps]

    def rmsnorm_chunk(self, hc, w, gcol0, gsrc, out_fn):
        fw = self.fw
        sq = self.sq8.next()
        fw.act(sq[:, :, :w], hc[:, :, :w], AF.Square)
        ps = self.psR.next()
        for k in range(8):
            fw.mm(ps[:, :w], self.ones_bf, sq[:, k, :w], start=(k == 0), stop=(k == 7))
        rstd = self.t32.next()
        self.rstd_from_ss(rstd[:, :w], ps[:, :w], 1024.0, EPS)
        for k in range(8):
            fw.stt(out_fn(k), hc[:, k, :w], gsrc[:, gcol0 + k:gcol0 + k + 1], rstd[:, :w], ALU.mult, ALU.mult)

    def phase_A(self, l, es):
        fw, Lp = self.fw, self.Lp
        n = fw.sb([128, 8, Lp], BF16, es)
        self.h8 = Rot([fw.sb([128, 8, 512], F32, es) for _ in range(2)])
        self.sq8 = Rot([fw.sb([128, 8, 512], BF16, es) for _ in range(2)])
        nTv = self.nT.rr("(k p) t -> p k t", p=128)
        for (t0, w) in chunks(0, Lp):
            hc = self.h8.next()
            fw.dma(hc[:, :, :w], self.hall(t0, w))
            self.rmsnorm_chunk(hc, w, C_NMIX, self.vecs, lambda k: n[:, k, t0:t0 + w])
            fw.dma(nTv[:, :, t0:t0 + w].fresh(), n[:, :, t0:t0 + w], q="sp")
        ocs = [(o, min(128, NMIX - o)) for o in range(0, NMIX, 128)]
        wl = self.w_in[l]
        cur = self.load_w(wl[:, 0:ocs[0][1]], 8, ocs[0][1])
        for i, (o0, ow) in enumerate(ocs):
            nxt = None
            if i + 1 < len(ocs):
                o1, ow1 = ocs[i + 1]
                nxt = self.load_w(wl[:, o1:o1 + ow1], 8, ow1)
            for (t0, w) in chunks(0, Lp):
                ps = self.psR.next()
                for k in range(8):
                    fw.mm(ps[:ow, :w], cur[:, k, :], n[:, k, t0:t0 + w], start=(k == 0), stop=(k == 7))
                ot = self.t32.next()
                fw.evac(ot[:ow, :w], ps[:ow, :w])
                fw.dma(self.zT[o0:o0 + ow, t0:t0 + w].fresh(), ot[:ow, :w], q="sp")
            cur = nxt

    def mla(self, l, es):
        fw, Lp, NB = self.fw, self.Lp, self.NB
        scale = 96.0 ** -0.5
        Q = [fw.sb([96, Lp], BF16, es) for _ in range(4)]
        K = [fw.sb([96, Lp], BF16, es) for _ in range(4)]
        Vt = fw.sb([128, NB, 256], BF16, es)
        ropeb = Rot([fw.sb([96, 2, 512], F32, es) for _ in range(2)])
        wuq_t = self.load_w(self.w_uq[l], 2, 384)
        wuq = fw.sb([128, 2, 384], BF16, es)
        fw.copy(wuq, wuq_t, eng="pool")
        wuq_s = fw.sb([128, 2, 384], BF16, es)
        fw.copy(wuq_s, wuq_t, eng="pool")
        src = self.w_uq[l].rr("(k p) (h c) -> p k h c", p=128, h=4)
        st2 = self.wst.next()[:, 0:256].rr("p (k c) -> p k c", k=2)
        for k in range(2):
            fw.dma(st2[:, k, 0:64].rr("p (h c) -> p h c", h=4), src[:, k, :, 80:96])
            fw.dma(st2[:, k, 64:128].rr("p (h c) -> p h c", h=4), src[:, k, :, 64:80])
        wv = wuq_s.rr("p k (h c) -> p k h c", h=4)
        for k in range(2):
            fw.copy(wv[:, k, :, 64:80], st2[:, k, 0:64].rr("p (h c) -> p h c", h=4), eng="pool")
            fw.copy(wv[:, k, :, 80:96], st2[:, k, 64:128].rr("p (h c) -> p h c", h=4), eng="pool")
        wukv = self.load_w(self.w_ukv[l], 1, 512)
        wkn = fw.sb([128, 4, 64], BF16, es)
        wvv = fw.sb([128, 256], BF16, es)
        wk4 = wukv[:, 0, :].rr("p (h c) -> p h c", h=4)
        fw.copy(wkn, wk4[:, :, 0:64], eng="pool")
        fw.copy(wvv.rr("p (h c) -> p h c", h=4), wk4[:, :, 64:128], eng="pool")
        cq2 = Rot([fw.sb([128, 2, 512], F32, es) for _ in range(2)])
        sq2 = Rot([fw.sb([128, 2, 512], BF16, es) for _ in range(2)])
        cqn = Rot([fw.sb([128, 2, 512], BF16, es) for _ in range(2)])
        kr2 = Rot([fw.sb([96, 2, 512], F32, es) for _ in range(2)])
        krr = Rot([fw.sb([96, 512], BF16, es) for _ in range(2)])
        zcq = self.zT[0:256, :].rr("(k p) t -> p k t", p=128)
        for (t0, w) in chunks(0, Lp):
            c = cq2.next()
            fw.dma(c[:, :, :w], zcq[:, 0:2, t0:t0 + w].fresh())
            ck = self.t32.next()
            fw.dma(ck[:, :w], self.zT[OFF_CKV:OFF_CKV + 128, t0:t0 + w].fresh())
            rc = ropeb.next()
            fw.dma(rc[64:96, 0, :w], self.rope_d[0:32, t0:t0 + w])
            fw.dma(rc[64:96, 1, :w], self.rope_d[32:64, t0:t0 + w])
            kr = kr2.next()
            fw.dma(kr[64:96, 0, :w], self.zT[OFF_KR:OFF_KR + 32, t0:t0 + w].fresh())
            fw.dma(kr[64:80, 1, :w], self.zT[OFF_KR + 16:OFF_KR + 32, t0:t0 + w].fresh())
            fw.dma(kr[80:96, 1, :w], self.zT[OFF_KR:OFF_KR + 16, t0:t0 + w].fresh())
            s = sq2.next()
            fw.act(s[:, :, :w], c[:, :, :w], AF.Square)
            ps = self.psR.next()
            for k in range(2):
                fw.mm(ps[:, :w], self.ones_bf, s[:, k, :w], start=(k == 0), stop=(k == 1))
            rstd = self.t32.next()
            self.rstd_from_ss(rstd[:, :w], ps[:, :w], 256.0, EPS)
            cn = cqn.next()
            for k in range(2):
                fw.stt(cn[:, k, :w], c[:, k, :w], self.vc(C_QN + k), rstd[:, :w], ALU.mult, ALU.mult)
            s2 = self.tbf.next()
            fw.act(s2[:, :w], ck[:, :w], AF.Square)
            ps = self.psR.next()
            fw.mm(ps[:, :w], self.ones_bf, s2[:, :w])
            rstd2 = self.t32.next()
            self.rstd_from_ss(rstd2[:, :w], ps[:, :w], 128.0, EPS)
            ckn = self.tbf.next()
            fw.stt(ckn[:, :w], ck[:, :w], self.vc(C_KVN), rstd2[:, :w], ALU.mult, ALU.mult)
            t1 = self.t32.next()
            t2 = self.t32.next()
            fw.tt(t1[64:96, :w], kr[64:96, 0, :w], rc[64:96, 0, :w], ALU.mult)
            fw.tt(t2[64:96, :w], kr[64:96, 1, :w], rc[64:96, 1, :w], ALU.mult)
            kq = krr.next()
            fw.tt(kq[64:96, :w], t1[64:96, :w], t2[64:96, :w], ALU.add)
            for h in range(4):
                p1 = self.psR.next()
                p2 = self.psR.next()
                for k in range(2):
                    fw.mm(p1[0:96, :w], wuq[:, k, h * 96:(h + 1) * 96], cn[:, k, :w], start=(k == 0), stop=(k == 1))
                for k in range(2):
                    fw.mm(p2[0:96, :w], wuq_s[:, k, h * 96:(h + 1) * 96], cn[:, k, :w], start=(k == 0), stop=(k == 1))
                fw.evac(Q[h][0:64, t0:t0 + w], p1[0:64, :w])
                a1 = self.t32.next()
                a2 = self.t32.next()
                fw.tt(a1[64:96, :w], p1[64:96, :w], rc[64:96, 0, :w], ALU.mult)
                fw.tt(a2[64:96, :w], p2[64:96, :w], rc[64:96, 1, :w], ALU.mult)
                fw.tt(Q[h][64:96, t0:t0 + w], a1[64:96, :w], a2[64:96, :w], ALU.add)
                p3 = self.psR.next()
                fw.mm(p3[0:64, :w], wkn[:, h, :], ckn[:, :w])
                fw.evac(K[h][0:64, t0:t0 + w], p3[0:64, :w])
                fw.copy(K[h][64:96, t0:t0 + w], kq[64:96, :w], eng="pool")
            for b in range(w // 128):
                p4 = self.psR.next()
                fw.mm(p4[:, 0:256], ckn[:, b * 128:(b + 1) * 128], wvv)
                fw.evac(Vt[:, (t0 // 128) + b, :], p4[:, 0:256])
        acc = [(self.banks[0], self.banks[1]), (self.banks[2], self.banks[3])]
        psS = Rot(self.banks[4:8])
        its = []
        grp = 0
        for h in range(4):
            for (t0, w) in chunks(0, Lp):
                kbmax = (t0 + w) // 128 - 1
                for kb in range(kbmax + 1):
                    its.append((h, t0, w, kb, kbmax, grp))
                grp += 1

        def s_stage(it):
            h, t0, w, kb, kbmax, g = it
            c0 = max(0, kb - t0 // 128) * 128
            sp = psS.next()
            fw.mm(sp[:, c0:w], K[h][:, kb * 128:(kb + 1) * 128], Q[h][:, t0 + c0:t0 + w])
            return sp
        sp_next = s_stage(its[0])
        for idx, it in enumerate(its):
            h, t0, w, kb, kbmax, g = it
            Ops, Dps = acc[g % 2]
            sp = sp_next
            if idx + 1 < len(its):
                sp_next = s_stage(its[idx + 1])
            i = kb - t0 // 128
            c0 = max(0, i) * 128
            e = self.tbf.next()
            fw.act(e[:, c0:w], sp[:, c0:w], AF.Exp, scale=scale)
            if i >= 0:
                fw.tt(e[:, c0:c0 + 128], e[:, c0:c0 + 128], self.tri_incl, ALU.mult, eng="pool")
            last = (kb == kbmax)
            fw.mm(Ops[0:64, c0:w], Vt[:, kb, h * 64:(h + 1) * 64], e[:, c0:w], start=(kb == 0), stop=last)
            fw.mm(Dps[0:64, c0:w], (self.ones_pad if kb == 0 else self.ones_bf)[:, 0:64], e[:, c0:w],
                  start=(kb == 0), stop=last)
            if last:
                den = self.t32.next()
                fw.ts(den[0:64, :w], Dps[0:64, :w], 1e-30, None, op0=ALU.add)
                fw.recip(den[0:64, :w], den[0:64, :w])
                yb = self.tbf.next()
                fw.tt(yb[0:64, :w], Ops[0:64, :w], den[0:64, :w], ALU.mult)
                fw.dma(self.yT[h * 64:(h + 1) * 64, t0:t0 + w].fresh(), yb[0:64, :w], q="sp")

    def sbatt(self, l, es):
        fw, Lp, NB = self.fw, self.Lp, self.NB
        Q = fw.sb([64, 4, Lp], BF16, es)
        K = fw.sb([64, 4, Lp], BF16, es)
        Vt = fw.sb([128, NB, 256], BF16, es)
        ld = Rot([fw.sb([64, 4, 512], F32, es) for _ in range(1)])
        ldv = Rot([fw.sb([128, 2, 512], F32, es) for _ in range(2)])
        pacc = fw.sb([128, 512], BF16, es)
        zq = self.zT[OFF_SB:OFF_SB + 256, :].rr("(h d) t -> d h t", h=4)
        zk = self.zT[OFF_SB + 256:OFF_SB + 512, :].rr("(h d) t -> d h t", h=4)
        zv = self.zT[OFF_SB + 512:OFF_SB + 768, :].rr("(k p) t -> p k t", p=128)
        for (t0, w) in chunks(0, Lp):
            a = ld.next()
            fw.dma(a[:, :, :w], zq[:, :, t0:t0 + w].fresh())
            fw.copy(Q[:, :, t0:t0 + w], a[:, :, :w], eng="act")
            b = ld.next()
            fw.dma(b[:, :, :w], zk[:, :, t0:t0 + w].fresh())
            fw.copy(K[:, :, t0:t0 + w], b[:, :, :w], eng="dve")
            v = ldv.next()
            fw.dma(v[:, :, :w], zv[:, :, t0:t0 + w].fresh())
            for bb in range(w // 128):
                for k in range(2):
                    pt = self.psR.next()
                    fw.transpose(pt[:, 0:128], v[:, k, bb * 128:(bb + 1) * 128], self.ident)
                    fw.evac(Vt[:, t0 // 128 + bb, k * 128:(k + 1) * 128], pt[:, 0:128])
        L32 = Rot([fw.sb([128, 512], F32, es) for _ in range(8)])
        Lbf = Rot([fw.sb([128, 512], BF16, es) for _ in range(8)])
        OpsL = [self.banks[0], self.banks[1]]
        psZ = Rot(self.banks[2:8])
        its = []
        grp = 0
        for h in range(4):
            for (t0, w) in chunks(0, Lp):
                kbmax = (t0 + w) // 128 - 1
                for kb in range(kbmax, -1, -1):
                    its.append((h, t0, w, kb, kbmax, grp))
                grp += 1

        def stage1(it):
            h, t0, w, kb, kbmax, g = it
            first = (kb == kbmax)
            i = kb - t0 // 128
            c0 = max(0, i) * 128
            if first:
                fw.memset(pacc[:, :w], 0.0)
            zp = psZ.next()
            fw.mm(zp[:, c0:w], K[:, h, kb * 128:(kb + 1) * 128], Q[:, h, t0 + c0:t0 + w])
            e1 = L32.next()
            fw.act(e1[:, c0:w], zp[:, c0:w], AF.Exp, scale=0.125)
            P = Lbf.next()
            fw.act(P[:, c0:w], e1[:, c0:w], AF.Ln, bias=self.epscol(1.0))
            zs = L32.next()
            fw.ts(zs[:, c0:w], zp[:, c0:w], 0.125, None, op0=ALU.mult)
            if i >= 0:
                fw.tt(P[:, c0:c0 + 128], P[:, c0:c0 + 128], self.tri_strict, ALU.mult, eng="pool")
            cp = psZ.next()
            fw.mm(cp[:, c0:w], self.uincl_pad if kb == 0 else self.uincl, P[:, c0:w], start=True, stop=first)
            if not first:
                fw.mm(cp[:, c0:w], self.ones_bf, pacc[:, c0:w], start=False, stop=True)
            lt = L32.next()
            fw.tt(lt[:, c0:w], zs[:, c0:w], cp[:, c0:w], ALU.subtract)
            if kb > 0:
                fw.tt(pacc[:, c0:w], pacc[:, c0:w], P[:, c0:w], ALU.add)
            return lt

        def stage2(it, lt):
            h, t0, w, kb, kbmax, g = it
            Ops = OpsL[g % 2]
            first = (kb == kbmax)
            i = kb - t0 // 128
            c0 = max(0, i) * 128
            A = Lbf.next()
            fw.act(A[:, c0:w], lt[:, c0:w], AF.Exp)
            if i >= 0:
                fw.tt(A[:, c0:c0 + 128], A[:, c0:c0 + 128], self.tri_strict, ALU.mult, eng="pool")
            if first and c0 > 0:
                fw.memset(A[:, 0:c0], 0.0, eng="pool")
            cc = 0 if first else c0
            fw.mm(Ops[0:64, cc:w], Vt[:, kb, h * 64:(h + 1) * 64], A[:, cc:w], start=first, stop=(kb == 0))
            if kb == 0:
                yb = Lbf.next()
                fw.evac(yb[0:64, :w], Ops[0:64, :w])
                fw.dma(self.yT[512 + h * 64:512 + (h + 1) * 64, t0:t0 + w].fresh(), yb[0:64, :w], q="sp")

        lt_next = stage1(its[0])
        for idx, it in enumerate(its):
            lt = lt_next
            if idx + 1 < len(its):
                lt_next = stage1(its[idx + 1])
            stage2(it, lt)

    def gla(self, l, es):
        fw, Lp = self.fw, self.Lp
        a2 = self.load_w(self.gla_a2[l], 1, 128, prows=16)
        Sbd = fw.sb([128, 256], F32, es)
        Sbd_bf = fw.sb([128, 256], BF16, es)
        fw.memset(Sbd, 0.0)
        fw.memset(Sbd_bf, 0.0)
        ldr = Rot([fw.sb([64, 4, 512], F32, es) for _ in range(2)])
        ldv = Rot([fw.sb([128, 2, 512], F32, es) for _ in range(2)])
        s128 = Rot([fw.sb([128, 128], F32, es) for _ in range(3)])
        b128 = Rot([fw.sb([128, 128], BF16, es) for _ in range(12)])
        vtok = Rot([fw.sb([128, 256], BF16, es) for _ in range(2)])
        yst = Rot([fw.sb([64, 4, 128], BF16, es) for _ in range(2)])
        L32 = Rot([fw.sb([128, 512], F32, es) for _ in range(14)])
        Lbf = Rot([fw.sb([128, 512], BF16, es) for _ in range(6)])
        og = OFF_GLA
        zr = self.zT[og + 528:og + 784, :].rr("(h d) t -> d h t", h=4)
        zv = self.zT[og + 256:og + 512, :].rr("(k p) t -> p k t", p=128)
        yv = self.yT[768:1024, :].rr("(h d) t -> d h t", h=4)
        for (t0, w) in chunks(0, Lp):
            al = L32.next()
            fw.dma(al[0:16, :w], self.zT[og + 512:og + 528, t0:t0 + w].fresh())
            alb = Lbf.next()
            fw.copy(alb[0:16, :w], al[0:16, :w])
            xp = self.psR.next()
            fw.mm(xp[:, :w], a2[:, 0, :], alb[0:16, :w])
            e = L32.next()
            fw.act(e[:, :w], xp[:, :w], AF.Exp, bias=self.dcols[:, 19:20], scale=-1.0)
            fw.act(e[:, :w], e[:, :w], AF.Ln, bias=self.epscol(1.0))
            fw.ts(e[:, :w], e[:, :w], -1.0 / 16.0, None, op0=ALU.mult)
            gam = L32.next()
            fw.scan(gam[:, :w], self.rm[:, :w], e[:, :w], 0.0, ALU.mult, ALU.add)
            eg = L32.next()
            fw.act(eg[:, :w], gam[:, :w], AF.Exp)
            eng = L32.next()
            fw.act(eng[:, :w], gam[:, :w], AF.Exp, scale=-1.0)
            q = L32.next()
            fw.dma(q[:, :w], self.zT[og:og + 128, t0:t0 + w].fresh())
            k = L32.next()
            fw.dma(k[:, :w], self.zT[og + 128:og + 256, t0:t0 + w].fresh())
            qt = Lbf.next()
            fw.stt(qt[:, :w], q[:, :w], 32.0 ** -0.5, eg[:, :w], ALU.mult, ALU.mult)
            kt = k
            fw.tt(kt[:, :w], k[:, :w], eng[:, :w], ALU.mult)
            ktb = Lbf.next()
            fw.copy(ktb[:, :w], kt[:, :w], eng="pool")
            v = ldv.next()
            fw.dma(v[:, :, :w], zv[:, :, t0:t0 + w].fresh())
            r = ldr.next()
            fw.dma(r[:, :, :w], zr[:, :, t0:t0 + w].fresh())
            fw.act(r[:, :, :w], r[:, :, :w], AF.Silu)
            for c in range(w // 128):
                cs = slice(c * 128, (c + 1) * 128)
                gend = eg[:, c * 128 + 127:c * 128 + 128]
                vt = vtok.next()
                for kk in range(2):
                    pt = self.psR.next()
                    fw.transpose(pt[:, 0:128], v[:, kk, cs], self.ident)
                    fw.evac(vt[:, kk * 128:(kk + 1) * 128], pt[:, 0:128])
                kh = s128.next()
                fw.ts(kh, kt[:, cs], gend, None, op0=ALU.mult)
                pt = self.psR.next()
                fw.transpose(pt[:, 0:128], kh, self.ident)
                kht = b128.next()
                fw.evac(kht, pt[:, 0:128])
                scp = self.psR.next()
                for h in range(4):
                    khh = b128.next()
                    fw.ts(khh, ktb[:, cs], self.headmask[:, h:h + 1], None, op0=ALU.mult, eng="pool")
                    fw.mm(scp[:, h * 128:(h + 1) * 128], khh, qt[:, cs])
                sc = self.tbf.next()
                fw.tt(sc, scp, self.tri_incl4.rr("p h t -> p (h t)"), ALU.mult)
                op_ = self.psR.next()
                for h in range(4):
                    fw.mm(op_[0:64, h * 128:(h + 1) * 128], Sbd_bf[:, h * 64:(h + 1) * 64], qt[:, cs],
                          start=True, stop=False)
                    fw.mm(op_[0:64, h * 128:(h + 1) * 128], vt[:, h * 64:(h + 1) * 64], sc[:, h * 128:(h + 1) * 128],
                          start=False, stop=True)
                kvp = self.psR.next()
                fw.mm(kvp[:, 0:256], kht, vt)
                tmp = self.t32.next()
                fw.tt(tmp[:, 0:256], kvp[:, 0:256], self.bdmask, ALU.mult)
                fw.stt(Sbd, Sbd, gend, tmp[:, 0:256], ALU.mult, ALU.add)
                fw.copy(Sbd_bf, Sbd, eng="pool")
                osb = self.t32.next()
                fw.evac(osb[0:64, :], op_[0:64, :])
                sq = self.tbf.next()
                fw.act(sq[0:64, :], op_[0:64, :], AF.Square)
                ssp = self.psR.next()
                fw.mm(ssp[0:64, :], self.ones_bf[0:64, 0:64], sq[0:64, :])
                rs = self.t32.next()
                self.rstd_from_ss(rs[0:64, :], ssp[0:64, :], 64.0, EPS, rows=64)
                fw.tt(osb[0:64, :], osb[0:64, :], rs[0:64, :], ALU.mult)
                yo = yst.next()
                for h in range(4):
                    fw.stt(yo[:, h, :], osb[0:64, h * 128:(h + 1) * 128], self.vc(C_GNORM + h, rows=64),
                           r[:, h, cs], ALU.mult, ALU.mult)
                fw.dma(yv[:, :, t0 + c * 128:t0 + (c + 1) * 128].fresh(), yo, q="sp")

    def rwkv(self, l, es):
        fw, Lp = self.fw, self.Lp
        w2 = fw.sb([64, 256], BF16, es)
        a2 = fw.sb([64, 256], BF16, es)
        g2 = fw.sb([128, 256], BF16, es)
        t = self.load_w(self.rw_w2[l], 1, 256, prows=64)
        fw.copy(w2, t[:, 0, :], eng="pool")
        t = self.load_w(self.rw_a2[l], 1, 256, prows=64)
        fw.copy(a2, t[:, 0, :], eng="pool")
        t = self.load_w(self.rw_g2[l], 1, 256, prows=128)
        fw.copy(g2, t[:, 0, :], eng="pool")
        T32 = [fw.sb([64, 64], F32, es) for _ in range(4)]
        Tbf = [fw.sb([64, 64], BF16, es) for _ in range(4)]
        for h in range(4):
            fw.memset(T32[h], 0.0)
            fw.memset(Tbf[h], 0.0)
        halo = Rot([fw.sb([128, 513], F32, es) for _ in range(4)])
        f64 = Rot([fw.sb([64, 512], F32, es) for _ in range(28)])
        h64 = Rot([fw.sb([64, 512], BF16, es) for _ in range(12)])
        s128 = Rot([fw.sb([128, 128], F32, es) for _ in range(10)])
        b128 = Rot([fw.sb([128, 128], BF16, es) for _ in range(104)])
        b64 = Rot([fw.sb([128, 64], BF16, es) for _ in range(32)])
        oz = OFF_RW
        ps8 = Rot(self.banks)
        c_decay = -math.exp(-0.5)
        wlb_r = Rot([fw.sb([64, 512], BF16, es) for _ in range(2)])
        alb_r = Rot([fw.sb([64, 512], BF16, es) for _ in range(2)])
        glb_r = Rot([fw.sb([128, 512], BF16, es) for _ in range(2)])

        def load_shift(rows0, nrows, t0, w, mucol, omucol, out):
            hl = halo.next()
            if t0 == 0:
                fw.memset(hl[0:nrows, 0:1], 0.0)
                fw.dma(hl[0:nrows, 1:1 + w], self.zT[rows0:rows0 + nrows, 0:w].fresh())
            else:
                fw.dma(hl[0:nrows, 0:1 + w], self.zT[rows0:rows0 + nrows, t0 - 1:t0 + w].fresh())
            tmp = f64.next() if nrows <= 64 else self.t32.next()
            fw.ts(tmp[0:nrows, :w], hl[0:nrows, 0:w], mucol, None, op0=ALU.mult)
            fw.stt(out, hl[0:nrows, 1:1 + w], omucol, tmp[0:nrows, :w], ALU.mult, ALU.add)

        for (t0, w) in chunks(0, Lp):
            nck = w // 128
            wl = f64.next()
            load_shift(oz + 768, 64, t0, w, self.vc(C_MUWL, rows=64), self.dcols[0:64, 12:13], wl[:, :w])
            wlb = wlb_r.next()
            fw.act(wlb[:, :w], wl[:, :w], AF.Tanh)
            al = f64.next()
            load_shift(oz + 832, 64, t0, w, self.vc(C_MUAL, rows=64), self.dcols[0:64, 13:14], al[:, :w])
            alb = alb_r.next()
            fw.copy(alb[:, :w], al[:, :w])
            gl = self.t32.next()
            load_shift(oz + 896, 128, t0, w, self.vc(C_MUGL), self.dcols[:, 14:15], gl[:, :w])
            glb = glb_r.next()
            fw.act(glb[:, :w], gl[:, :w], AF.Sigmoid)
            for h in range(4):
                r32, k32, v32 = f64.next(), f64.next(), f64.next()
                load_shift(oz + h * 64, 64, t0, w, self.vc(C_MURKV + h, rows=64), self.dcols[0:64, h:h + 1], r32[:, :w])
                load_shift(oz + 256 + h * 64, 64, t0, w, self.vc(C_MURKV + 4 + h, rows=64),
                           self.dcols[0:64, 4 + h:5 + h], k32[:, :w])
                load_shift(oz + 512 + h * 64, 64, t0, w, self.vc(C_MURKV + 8 + h, rows=64),
                           self.dcols[0:64, 8 + h:9 + h], v32[:, :w])
                pw = ps8.next()
                fw.mm(pw[0:64, :w], w2[:, h * 64:(h + 1) * 64], wlb[:, :w])
                logw = f64.next()
                fw.act(logw[:, :w], pw[0:64, :w], AF.Sigmoid, bias=self.vc(C_W0 + h, rows=64))
                fw.ts(logw[:, :w], logw[:, :w], c_decay, None, op0=ALU.mult)
                pa = ps8.next()
                fw.mm(pa[0:64, :w], a2[:, h * 64:(h + 1) * 64], alb[:, :w])
                alpha = f64.next()
                fw.act(alpha[:, :w], pa[0:64, :w], AF.Sigmoid, bias=self.vc(C_A0 + h, rows=64))
                pg = ps8.next()
                fw.mm(pg[0:64, :w], g2[:, h * 64:(h + 1) * 64], glb[:, :w])
                g32 = f64.next()
                fw.evac(g32[:, :w], pg[0:64, :w])
                kkr = f64.next()
                fw.ts(kkr[:, :w], k32[:, :w], self.vc(C_KK + h, rows=64), None, op0=ALU.mult)
                sqk = h64.next()
                fw.act(sqk[:, :w], kkr[:, :w], AF.Square)
                pss = ps8.next()
                fw.mm(pss[0:64, :w], self.ones_bf[0:64, 0:64], sqk[:, :w])
                rn = f64.next()
                fw.act(rn[:, :w], pss[0:64, :w], AF.Ln, bias=self.epscol(1e-24)[0:64])
                fw.act(rn[:, :w], rn[:, :w], AF.Exp, scale=-0.5)
                kk = kkr
                fw.tt(kk[:, :w], kkr[:, :w], rn[:, :w], ALU.mult)
                kmod = f64.next()
                fw.ts(kmod[:, :w], alpha[:, :w], self.vc(C_KA + h, rows=64), self.dcols[0:64, 15 + h:16 + h],
                      op0=ALU.mult, op1=ALU.add)
                fw.tt(kmod[:, :w], kmod[:, :w], k32[:, :w], ALU.mult)
                gam = f64.next()
                fw.scan(gam[:, :w], self.rm[0:64, :w], logw[:, :w], 0.0, ALU.mult, ALU.add)
                eg = f64.next()
                fw.act(eg[:, :w], gam[:, :w], AF.Exp)
                eng = f64.next()
                fw.act(eng[:, :w], gam[:, :w], AF.Exp, scale=-1.0)
                egm = f64.next()
                fw.tt(egm[:, :w], gam[:, :w], logw[:, :w], ALU.subtract)
                fw.act(egm[:, :w], egm[:, :w], AF.Exp)
                rt = h64.next()
                fw.tt(rt[:, :w], r32[:, :w], eg[:, :w], ALU.mult)
                kt32 = f64.next()
                fw.tt(kt32[:, :w], kmod[:, :w], eng[:, :w], ALU.mult)
                ktb = h64.next()
                fw.copy(ktb[:, :w], kt32[:, :w], eng="pool")
                bt32 = f64.next()
                fw.tt(bt32[:, :w], kk[:, :w], alpha[:, :w], ALU.mult)
                fw.tt(bt32[:, :w], bt32[:, :w], eng[:, :w], ALU.mult)
                btb = h64.next()
                fw.copy(btb[:, :w], bt32[:, :w], eng="pool")
                atb = h64.next()
                fw.stt(atb[:, :w], kk[:, :w], -1.0, egm[:, :w], ALU.mult, ALU.mult)
                rk = h64.next()
                fw.stt(rk[:, :w], r32[:, :w], self.vc(C_RK + h, rows=64), kmod[:, :w], ALU.mult, ALU.mult)
                pb = ps8.next()
                fw.mm(pb[0:64, :w], self.ones_bf[0:64, 0:64], rk[:, :w])
                bon = f64.next()
                fw.tt(bon[:, :w], pb[0:64, :w], v32[:, :w], ALU.mult)
                y32 = f64.next()
                def indep(c, o):
                    cs = slice(c * 128, (c + 1) * 128)
                    gend = eg[:, c * 128 + 127:c * 128 + 128]
                    o["cs"], o["gend"] = cs, gend
                    pt = ps8.next()
                    fw.transpose(pt[:, 0:64], v32[:, cs], self.ident[0:64, 0:64])
                    kh = s128.next()
                    fw.ts(kh[0:64, :], kt32[:, cs], gend, None, op0=ALU.mult)
                    bh = s128.next()
                    fw.ts(bh[0:64, :], bt32[:, cs], gend, None, op0=ALU.mult)
                    yield
                    vt = b64.next()
                    fw.evac(vt, pt[:, 0:64])
                    pt2 = ps8.next()
                    fw.transpose(pt2[:, 0:64], kh[0:64, :], self.ident[0:64, 0:64])
                    fw.transpose(pt2[:, 64:128], bh[0:64, :], self.ident[0:64, 0:64])
                    yield
                    kht = b64.next()
                    fw.evac(kht, pt2[:, 0:64])
                    bht = b64.next()
                    fw.evac(bht, pt2[:, 64:128])
                    o["vt"], o["kht"], o["bht"] = vt, kht, bht
                    pn = ps8.next()
                    fw.mm(pn[:, 0:128], btb[:, cs], atb[:, cs])
                    fw.mm(pn[:, 128:256], atb[:, cs], btb[:, cs])
                    fw.mm(pn[:, 256:384], ktb[:, cs], atb[:, cs])
                    pr = ps8.next()
                    fw.mm(pr[:, 0:128], ktb[:, cs], rt[:, cs])
                    fw.mm(pr[:, 128:256], btb[:, cs], rt[:, cs])
                    yield
                    N = b128.next()
                    fw.tt(N, pn[:, 0:128], self.tri_strict32, ALU.mult)
                    NT = b128.next()
                    fw.tt(NT, pn[:, 128:256], self.tri_sl32, ALU.mult)
                    AakT = b128.next()
                    fw.tt(AakT, pn[:, 256:384], self.tri_strict32, ALU.mult)
                    ArkT = b128.next()
                    fw.tt(ArkT, pr[:, 0:128], self.tri_incl32, ALU.mult)
                    ArbT = b128.next()
                    fw.tt(ArbT, pr[:, 128:256], self.tri_incl32, ALU.mult)
                    o["AakT"], o["ArkT"], o["ArbT"] = AakT, ArkT, ArbT
                    P = b128.next()
                    fw.tt(P, N, self.ident_bf, ALU.add)
                    yield
                    for lev in range(6):
                        pq = ps8.next()
                        fw.mm(pq[:, 128:256], N, NT)
                        if lev < 5:
                            fw.mm(pq[:, 0:128], NT, N)
                        yield
                        if lev < 5:
                            N2 = b128.next()
                            fw.evac(N2, pq[:, 0:128])
                        NT2 = b128.next()
                        fw.evac(NT2, pq[:, 128:256])
                        yield
                        pp = ps8.next()
                        fw.mm(pp[:, 0:128], NT2, P)
                        yield
                        P2 = b128.next()
                        fw.tt(P2, P, pp[:, 0:128], ALU.add)
                        P = P2
                        NT = NT2
                        if lev < 5:
                            N = N2
                    o["MT"] = P

                outs = [dict() for _ in range(nck)]
                alive = [indep(c, outs[c]) for c in range(nck)]
                while alive:
                    for gnr in list(alive):
                        try:
                            next(gnr)
                        except StopIteration:
                            alive.remove(gnr)
                for c in range(nck):
                    o = outs[c]
                    cs, gend, vt, kht, bht = o["cs"], o["gend"], o["vt"], o["kht"], o["bht"]
                    AakT, ArkT, ArbT, MT = o["AakT"], o["ArkT"], o["ArbT"], o["MT"]
                    p0 = ps8.next()
                    fw.mm(p0[:, 0:64], atb[:, cs], Tbf[h], start=True, stop=False)
                    fw.mm(p0[:, 0:64], AakT, vt, start=False, stop=True)
                    rhs0 = b64.next()
                    fw.evac(rhs0, p0[:, 0:64])
                    pu = ps8.next()
                    fw.mm(pu[:, 0:64], MT, rhs0)
                    U = b64.next()
                    fw.evac(U, pu[:, 0:64])
                    py = ps8.next()
                    fw.mm(py[0:64, 0:128], Tbf[h], rt[:, cs], start=True, stop=False)
                    fw.mm(py[0:64, 0:128], vt, ArkT, start=False, stop=False)
                    fw.mm(py[0:64, 0:128], U, ArbT, start=False, stop=True)
                    fw.evac(y32[:, cs], py[0:64, 0:128])
                    pT = ps8.next()
                    fw.mm(pT[0:64, 0:64], kht, vt, start=True, stop=False)
                    fw.mm(pT[0:64, 0:64], bht, U, start=False, stop=True)
                    fw.stt(T32[h], T32[h], gend, pT[0:64, 0:64], ALU.mult, ALU.add)
                    fw.copy(Tbf[h], T32[h], eng="pool")
                ybf = h64.next()
                fw.copy(ybf[:, :w], y32[:, :w], eng="pool")
                pm = ps8.next()
                fw.mm(pm[0:64, :w], self.ones_bf[0:64, 0:64], ybf[:, :w])
                yc = f64.next()
                fw.stt(yc[:, :w], pm[0:64, :w], -1.0 / 64.0, y32[:, :w], ALU.mult, ALU.add)
                sq = h64.next()
                fw.act(sq[:, :w], yc[:, :w], AF.Square)
                pv = ps8.next()
                fw.mm(pv[0:64, :w], self.ones_bf[0:64, 0:64], sq[:, :w])
                rs = f64.next()
                self.rstd_from_ss(rs[:, :w], pv[0:64, :w], 64.0, 64e-5, rows=64)
                fw.tt(yc[:, :w], yc[:, :w], rs[:, :w], ALU.mult)
                fw.ts(yc[:, :w], yc[:, :w], self.vc(C_LNW + h, rows=64), self.vc(C_LNB + h, rows=64),
                      op0=ALU.mult, op1=ALU.add)
                fw.tt(yc[:, :w], yc[:, :w], bon[:, :w], ALU.add)
                yo = h64.next()
                fw.tt(yo[:, :w], yc[:, :w], g32[:, :w], ALU.mult)
                fw.dma(self.yT[256 + h * 64:256 + (h + 1) * 64, t0:t0 + w].fresh(), yo[:, :w], q="sp")

    def phase_C1(self, l, es):
        fw, Lp = self.fw, self.Lp
        TS = 1536
        n = fw.sb([128, 8, TS], BF16, es)
        y = fw.sb([128, 8, TS], BF16, es)
        mg = fw.sb([128, 8, TS], BF16, es)
        acc = fw.sb([128, TS], F32, es)
        nTv = self.nT.rr("(k p) t -> p k t", p=128)
        yTv = self.yT.rr("(k p) t -> p k t", p=128)
        wl = self.w_in[l]
        for (s0, sw) in chunks(0, Lp, TS):
            for (t0, w) in chunks(0, sw):
                fw.dma(n[:, :, t0:t0 + w], nTv[:, :, s0 + t0:s0 + t0 + w].fresh())
                fw.dma(y[:, :, t0:t0 + w], yTv[:, :, s0 + t0:s0 + t0 + w].fresh())
            jobs = [(d, m) for d in range(8) for m in range(4)]

            def loadj(j):
                d, m = j
                c0 = OFF_GATE + m * 1024 + d * 128
                return (self.load_w(wl[:, c0:c0 + 128], 8, 128),
                        self.load_w(self.w_branch[l][m][:, d * 128:(d + 1) * 128], 2, 128))
            curw = loadj(jobs[0])
            for ji, (d, m) in enumerate(jobs):
                    nxtw = loadj(jobs[ji + 1]) if ji + 1 < len(jobs) else None
                    wg, wb = curw
                    curw = nxtw
                    for (t0, w) in chunks(0, sw):
                        pg = self.psR.next()
                        for k in range(8):
                            fw.mm(pg[:, :w], wg[:, k, :], n[:, k, t0:t0 + w], start=(k == 0), stop=(k == 7))
                        pb = self.psR.next()
                        for k in range(2):
                            fw.mm(pb[:, :w], wb[:, k, :], y[:, 2 * m + k, t0:t0 + w], start=(k == 0), stop=(k == 1))
                        gt = self.t32.next()
                        fw.act(gt[:, :w], pg[:, :w], AF.Sigmoid, bias=self.vc(C_GATEB + m * 8 + d))
                        if m == 0:
                            fw.tt(acc[:, t0:t0 + w], gt[:, :w], pb[:, :w], ALU.mult)
                        else:
                            fw.tt(gt[:, :w], gt[:, :w], pb[:, :w], ALU.mult)
                            if m < 3:
                                fw.tt(acc[:, t0:t0 + w], acc[:, t0:t0 + w], gt[:, :w], ALU.add, eng="pool")
                            else:
                                fw.tt(mg[:, d, t0:t0 + w], acc[:, t0:t0 + w], gt[:, :w], ALU.add, eng="pool")
            curo = self.load_w(self.w_out[l][:, 0:128], 8, 128)
            for d in range(8):
                wo = curo
                if d + 1 < 8:
                    curo = self.load_w(self.w_out[l][:, (d + 1) * 128:(d + 2) * 128], 8, 128)
                for (t0, w) in chunks(0, sw):
                    g0 = s0 + t0
                    po = self.psR.next()
                    for k in range(8):
                        fw.mm(po[:, :w], wo[:, k, :], mg[:, k, t0:t0 + w], start=(k == 0), stop=(k == 7))
                    lo = PADC if g0 == 0 else 0
                    hc = self.t32.next()
                    cell = self.hcell(d, g0 + lo, w - lo)
                    fw.dma(hc[:, lo:w], cell)
                    fw.tt(hc[:, lo:w], hc[:, lo:w], po[:, lo:w], ALU.add)
                    fw.dma(cell, hc[:, lo:w], q="sp")

    def phase_C2(self, l, es):
        fw, Lp = self.fw, self.Lp
        TS = 1536
        n2 = fw.sb([128, 8, TS], BF16, es)
        g = fw.sb([128, 22, TS], BF16, es)
        carry = fw.sb([128, 22, 2], F32, es)
        fw.memset(carry, 0.0)
        self.h8 = Rot([fw.sb([128, 8, 512], F32, es) for _ in range(1)])
        self.sq8 = Rot([fw.sb([128, 8, 512], BF16, es) for _ in range(1)])
        asb = Rot([fw.sb([128, 514], F32, es) for _ in range(3)])
        wfi = self.w_ffn_in[l]
        for (s0, sw) in chunks(0, Lp, TS):
            for (t0, w) in chunks(0, sw):
                hc = self.h8.next()
                fw.dma(hc[:, :, :w], self.hall(s0 + t0, w))
                self.rmsnorm_chunk(hc, w, C_NFFN, self.vecs, lambda k: n2[:, k, t0:t0 + w])
            def loadf(fc):
                return (self.load_w(wfi[:, fc * 128:(fc + 1) * 128], 8, 128),
                        self.load_w(wfi[:, DFF + fc * 128:DFF + (fc + 1) * 128], 8, 128))
            curw = loadf(0)
            for fc in range(22):
                wa, wu = curw
                if fc + 1 < 22:
                    curw = loadf(fc + 1)
                for (t0, w) in chunks(0, sw):
                    pa = self.psR.next()
                    for k in range(8):
                        fw.mm(pa[:, :w], wa[:, k, :], n2[:, k, t0:t0 + w], start=(k == 0), stop=(k == 7))
                    pu = self.psR.next()
                    for k in range(8):
                        fw.mm(pu[:, :w], wu[:, k, :], n2[:, k, t0:t0 + w], start=(k == 0), stop=(k == 7))
                    a = asb.next()
                    fw.copy(a[:, 0:2], carry[:, fc, :], eng="pool")
                    fw.copy(a[:, 2:2 + w], pa[:, :w], eng="act")
                    fw.copy(carry[:, fc, :], a[:, w:w + 2], eng="pool")
                    c = self.t32.next()
                    fw.ts(c[:, :w], a[:, 0:w], self.vc(C_CONVW + fc), self.vc(C_CONVB + fc), op0=ALU.mult, op1=ALU.add)
                    fw.stt(c[:, :w], a[:, 1:1 + w], self.vc(C_CONVW + 22 + fc), c[:, :w], ALU.mult, ALU.add)
                    fw.stt(c[:, :w], a[:, 2:2 + w], self.vc(C_CONVW + 44 + fc), c[:, :w], ALU.mult, ALU.add)
                    fw.act(c[:, :w], c[:, :w], AF.Silu)
                    fw.tt(g[:, fc, t0:t0 + w], c[:, :w], pu[:, :w], ALU.mult)
            curo = self.load_w(self.w_ffn_out[l][:, 0:128], 22, 128)
            for d in range(8):
                wo = curo
                if d + 1 < 8:
                    curo = self.load_w(self.w_ffn_out[l][:, (d + 1) * 128:(d + 2) * 128], 22, 128)
                for (t0, w) in chunks(0, sw):
                    g0 = s0 + t0
                    po = self.psR.next()
                    for k in range(22):
                        fw.mm(po[:, :w], wo[:, k, :], g[:, k, t0:t0 + w], start=(k == 0), stop=(k == 21))
                    lo = PADC if g0 == 0 else 0
                    hc = self.t32.next()
                    cell = self.hcell(d, g0 + lo, w - lo)
                    fw.dma(hc[:, lo:w], cell)
                    fw.tt(hc[:, lo:w], hc[:, lo:w], po[:, lo:w], ALU.add)
                    fw.dma(cell, hc[:, lo:w], q="sp")

    def phase_F(self, es):
        fw, Lp = self.fw, self.Lp
        self.h8 = Rot([fw.sb([128, 8, 512], F32, es) for _ in range(2)])
        self.sq8 = Rot([fw.sb([128, 8, 512], BF16, es) for _ in range(2)])
        o8 = Rot([fw.sb([128, 8, 512], F32, es) for _ in range(2)])
        ov = self.outT.rr("(k p) t -> p k t", p=128)
        for (t0, w) in chunks(0, Lp):
            hc = self.h8.next()
            fw.dma(hc[:, :, :w], self.hall(t0, w))
            o = o8.next()
            self.rmsnorm_chunk(hc, w, 0, self.gvec, lambda k: o[:, k, :w])
            lo = 128 if t0 == 0 else 0
            if w - lo > 0:
                fw.dma(ov[:, :, t0 + lo - 128:t0 + w - 128].fresh(), o[:, :, lo:w], q="sp")


def _cols(v, p=128):
    a = np.asarray(v, np.float32).reshape(-1, p).T
    if p < 128:
        a = np.concatenate([a, np.zeros((128 - p, a.shape[1]), np.float32)], 0)
    return a


def pack_vecs(inp, l):
    out = np.zeros((128, NV), np.float32)

    def put(c, a):
        out[:, c:c + a.shape[1]] = a
    put(C_NMIX, _cols(inp["norm_mix"][l]))
    put(C_NFFN, _cols(inp["norm_ffn"][l]))
    put(C_GATEB, _cols(inp["gate_b"][l].reshape(-1)))
    put(C_CONVW, _cols(inp["ffn_conv_w"][l].reshape(-1)))
    put(C_CONVB, _cols(inp["ffn_conv_b"][l]))
    put(C_QN, _cols(inp["mla_q_norm"][l]))
    put(C_KVN, _cols(inp["mla_kv_norm"][l]))
    mu = inp["rw_mu"][l]
    put(C_MURKV, _cols(mu[0:768], 64))
    put(C_MUWL, _cols(mu[768:832], 64))
    put(C_MUAL, _cols(mu[832:896], 64))
    put(C_MUGL, _cols(mu[896:1024]))
    put(C_W0, _cols(inp["rw_w0"][l], 64))
    put(C_A0, _cols(inp["rw_a0"][l], 64))
    put(C_KK, _cols(inp["rw_k_k"][l], 64))
    put(C_KA, _cols(inp["rw_k_a"][l], 64))
    put(C_RK, _cols(inp["rw_r_k"][l].reshape(-1), 64))
    put(C_LNW, _cols(inp["rw_ln_w"][l], 64))
    put(C_LNB, _cols(inp["rw_ln_b"][l], 64))
    put(C_GAB, _cols(inp["gla_a_b"][l]))
    put(C_GNORM, _cols(inp["gla_norm"][l], 64))
    return out


def rope_table(Lp):
    half = 16
    freqs = (np.float32(10000.0) ** (-np.arange(half, dtype=np.float32) / np.float32(half))).astype(np.float32)
    pos = (np.arange(Lp) - PADC).astype(np.float32)
    ang = (pos[None, :] * freqs[:, None]).astype(np.float32)
    c, s = np.cos(ang).astype(np.float32), np.sin(ang).astype(np.float32)
    return np.concatenate([c, c, -s, s], 0).astype(np.float32)


_CACHE = {}


def run(inputs, NB, DEPTH, dbg=(), n_cores=8, phases=None):
    key = (NB, DEPTH, tuple(dbg), phases)
    if key not in _CACHE:
        _CACHE[key] = Builder(NB, DEPTH, dbg, phases).build()
    nc = _CACHE[key]
    Lp = NB * 128
    x = np.asarray(inputs["x"], np.float32)
    B = x.shape[0]
    meta = np.asarray(inputs["meta_tokens"], np.float32)
    shared = {
        "w_in": np.ascontiguousarray(inputs["w_in"][:DEPTH], np.float32),
        "mla_w_uq": np.ascontiguousarray(inputs["mla_w_uq"][:DEPTH], np.float32),
        "mla_w_ukv": np.ascontiguousarray(inputs["mla_w_ukv"][:DEPTH], np.float32),
        "rw_w2": np.ascontiguousarray(inputs["rw_w2"][:DEPTH], np.float32),
        "rw_a2": np.ascontiguousarray(inputs["rw_a2"][:DEPTH], np.float32),
        "rw_g2": np.ascontiguousarray(inputs["rw_g2"][:DEPTH], np.float32),
        "gla_a2": np.ascontiguousarray(inputs["gla_a2"][:DEPTH], np.float32),
        "w_branch": np.ascontiguousarray(inputs["w_branch"][:DEPTH], np.float32),
        "w_out": np.ascontiguousarray(inputs["w_out"][:DEPTH], np.float32),
        "w_ffn_in": np.ascontiguousarray(inputs["w_ffn_in"][:DEPTH], np.float32),
        "w_ffn_out": np.ascontiguousarray(inputs["w_ffn_out"][:DEPTH], np.float32),
        "vecs": np.stack([pack_vecs(inputs, l) for l in range(DEPTH)], 0),
        "gvec": _cols(inputs["norm_final"]),
        "rope": rope_table(Lp),
    }
    in_maps = []
    for c in range(n_cores):
        b = c % B
        hT0 = np.zeros((1024, Lp), np.float32)
        hT0[:, PADC:PADC + 16] = meta.T
        hT0[:, 128:] = x[b].T
        m = dict(shared)
        m["hT0"] = hT0
        in_maps.append(m)
    res = run_bass_kernel_spmd(nc, in_maps, core_ids=list(range(n_cores)))
    return res.results


def kernel(**inputs):
    x = np.asarray(inputs["x"])
    B, SEQ, D = x.shape
    NB = (SEQ + 128) // 128
    results = run(inputs, NB, 4)
    out = np.stack([np.ascontiguousarray(results[b]["outT"].T) for b in range(B)], 0)
    return out.astype(np.float32)
```

```python
import math
import numpy as np
from contextlib import ExitStack
import concourse.bass as bass
import concourse.mybir as mybir
from concourse.bass_utils import run_bass_kernel_spmd

F32 = mybir.dt.float32
BF16 = mybir.dt.bfloat16
ALU = mybir.AluOpType
AF = mybir.ActivationFunctionType
AX = mybir.AxisListType

NDMA = 24
EPS = 1e-6
PADC = 112
NMIX = 2992
OFF_CQ, OFF_CKV, OFF_KR, OFF_RW, OFF_SB, OFF_GLA, OFF_GATE = 0, 256, 384, 416, 1440, 2208, 2992
DFF = 2816
NV = 192
(C_NMIX, C_NFFN, C_GATEB, C_CONVW, C_CONVB, C_QN, C_KVN, C_MURKV, C_MUWL, C_MUAL, C_MUGL,
 C_W0, C_A0, C_KK, C_KA, C_RK, C_LNW, C_LNB, C_GAB, C_GNORM) = (
    0, 8, 16, 48, 114, 136, 138, 139, 151, 152, 153, 154, 158, 162, 166, 170, 174, 178, 182, 183)


class Res:
    __slots__ = ("w", "r", "excl")

    def __init__(self, excl=False):
        self.w = {}
        self.r = {}
        self.excl = excl


def _rl(v):
    r = v.res
    return r if isinstance(r, tuple) else (r,)


class V:
    __slots__ = ("ap", "res")

    def __init__(self, ap, res=None):
        self.ap = ap
        self.res = res if res is not None else Res()

    def __getitem__(self, idx):
        return V(self.ap[idx], self.res)

    def rr(self, pat, **kw):
        return V(self.ap.rearrange(pat, **kw), self.res)

    def fresh(self):
        return V(self.ap, Res())

    def withres(self, res):
        return V(self.ap, res)


class _Eng:
    def __init__(self, name, obj, sem):
        self.name, self.obj, self.sem = name, obj, sem
        self.n = 0
        self.waited = {}
        self.pending = False


class Rot:
    def __init__(self, items):
        self.items = items
        self.i = 0

    def next(self):
        x = self.items[self.i]
        self.i = (self.i + 1) % len(self.items)
        return x


class FW:
    def __init__(self, nc, es):
        self.nc = nc
        self.es = es
        self.engs = {}
        for name, obj in (("pe", nc.tensor), ("act", nc.scalar), ("dve", nc.vector),
                          ("pool", nc.gpsimd), ("sp", nc.sync)):
            sem = es.enter_context(nc.semaphore("s_" + name))
            self.engs[name] = _Eng(name, obj, sem)
        self.dma_sems = [es.enter_context(nc.semaphore("d%d" % i)) for i in range(NDMA)]
        self.dma_cnt = [0] * NDMA
        self.dma_next = 0
        self.nins = 0
        self._uid = 0
        self._ev = 0

    def sb(self, shape, dt=F32, es=None):
        self._uid += 1
        t = (es or self.es).enter_context(self.nc.sbuf_tensor("sb%d" % self._uid, list(shape), dt))
        return V(t[:], Res())

    def ps(self, shape, dt=F32, es=None):
        self._uid += 1
        t = (es or self.es).enter_context(self.nc.psum_tensor("ps%d" % self._uid, list(shape), dt))
        return V(t[:], Res(excl=True))

    def _wait(self, eng, tok):
        key, sem, val, src = tok
        if src == "pe" and eng.name == "pe":
            return
        if eng.waited.get(key, 0) >= val:
            return
        eng.obj.wait_ge(sem, val)
        eng.waited[key] = val
        self.nins += 1

    def _deps(self, reads, writes):
        toks = []
        for v in reads:
            for r in _rl(v):
                toks.extend(r.w.values())
                if r.excl:
                    toks.extend(t for t in r.r.values() if t[3] != self._cur)
        for v in writes:
            for r in _rl(v):
                toks.extend(r.w.values())
                toks.extend(r.r.values())
        return toks

    def _mark(self, key, tok, reads, writes):
        wres = []
        for v in writes:
            for r in _rl(v):
                r.w = {key: tok}
                r.r = {}
                wres.append(r)
        for v in reads:
            for r in _rl(v):
                if r not in wres:
                    r.r[key] = tok

    def op(self, engname, fn, reads, writes, inc=True):
        eng = self.engs[engname]
        self._cur = engname
        for t in self._deps(reads, writes):
            self._wait(eng, t)
        ins = fn(eng.obj)
        self.nins += 1
        if inc:
            eng.n += 1
            ins.then_inc(eng.sem, 1)
            tok = (engname, eng.sem, eng.n, engname)
            eng.pending = False
        else:
            tok = (engname, eng.sem, eng.n + 1, engname)
            eng.pending = True
        self._mark(engname, tok, reads, writes)
        return ins

    def dma(self, out, in_, q="sp"):
        eng = self.engs[q]
        self._cur = q
        for t in self._deps([in_], [out]):
            self._wait(eng, t)
        i = self.dma_next
        self.dma_next = (i + 1) % NDMA
        key = ("dma", i)
        if self.dma_cnt[i] > 0:
            self._wait(eng, (key, self.dma_sems[i], self.dma_cnt[i], None))
        self.dma_cnt[i] += 16
        eng.obj.dma_start(out=out.ap, in_=in_.ap).then_inc(self.dma_sems[i], 16)
        self.nins += 1
        tok = (key, self.dma_sems[i], self.dma_cnt[i], None)
        self._mark(key, tok, [in_], [out])

    def barrier(self, engines=("pe", "act", "dve", "pool", "sp")):
        for en in engines:
            eng = self.engs[en]
            for i in range(NDMA):
                if self.dma_cnt[i] > 0:
                    self._wait(eng, (("dma", i), self.dma_sems[i], self.dma_cnt[i], None))
            for name, e in self.engs.items():
                assert not e.pending, name
                if e.n > 0 and name != en:
                    self._wait(eng, (name, e.sem, e.n, None))

    def mm(self, out, lhsT, rhs, start=True, stop=True):
        return self.op("pe", lambda e: e.matmul(out.ap, lhsT.ap, rhs.ap, start=start, stop=stop),
                       [lhsT, rhs], [out], inc=stop)

    def transpose(self, out, in_, ident):
        return self.op("pe", lambda e: e.transpose(out.ap, in_.ap, ident.ap), [in_, ident], [out])

    def act(self, out, in_, func, bias=None, scale=None):
        reads = [in_]
        kw = {}
        if bias is not None:
            if isinstance(bias, V):
                reads.append(bias)
                kw["bias"] = bias.ap
            else:
                kw["bias"] = bias
        if scale is not None:
            if isinstance(scale, V):
                reads.append(scale)
                kw["scale"] = scale.ap
            else:
                kw["scale"] = scale
        return self.op("act", lambda e: e.activation(out.ap, in_.ap, func, **kw), reads, [out])

    def tt(self, out, in0, in1, op, eng="dve"):
        return self.op(eng, lambda e: e.tensor_tensor(out.ap, in0.ap, in1.ap, op), [in0, in1], [out])

    def ts(self, out, in0, s1, s2=None, op0=ALU.mult, op1=None, eng="dve"):
        reads = [in0]
        a1 = s1
        if isinstance(s1, V):
            reads.append(s1)
            a1 = s1.ap
        a2 = s2
        if isinstance(s2, V):
            reads.append(s2)
            a2 = s2.ap
        kw = {}
        if op1 is not None:
            kw["op1"] = op1
        return self.op(eng, lambda e: e.tensor_scalar(out.ap, in0.ap, a1, a2, op0, **kw), reads, [out])

    def stt(self, out, in0, scalar, in1, op0, op1, eng="dve"):
        reads = [in0, in1]
        a = scalar
        if isinstance(scalar, V):
            reads.append(scalar)
            a = scalar.ap
        return self.op(eng, lambda e: e.scalar_tensor_tensor(out.ap, in0.ap, a, in1.ap, op0, op1), reads, [out])

    def copy(self, out, in_, eng="dve"):
        if eng == "act":
            return self.act(out, in_, AF.Copy)
        return self.op(eng, lambda e: e.tensor_copy(out.ap, in_.ap), [in_], [out])

    def evac(self, out, in_):
        self._ev ^= 1
        return self.copy(out, in_, eng="act" if self._ev else "dve")

    def memset(self, out, val, eng="dve"):
        return self.op(eng, lambda e: e.memset(out.ap, val), [], [out])

    def scan(self, out, d0, d1, init, op0, op1):
        return self.op("dve", lambda e: e.tensor_tensor_scan(out.ap, d0.ap, d1.ap, init, op0, op1), [d0, d1], [out])

    def recip(self, out, in_):
        return self.op("dve", lambda e: e.reciprocal(out.ap, in_.ap), [in_], [out])

    def aselect(self, t, cmp, fill, base, pattern, cm):
        return self.op("pool", lambda g: g.affine_select(out=t.ap, in_=t.ap, compare_op=cmp, fill=fill, base=base,
                                                         pattern=pattern, channel_multiplier=cm), [t], [t])


def chunks(t0, t1, maxw=512):
    out = []
    t = t0
    while t < t1:
        w = min(maxw, t1 - t)
        out.append((t, w))
        t += w
    return out


class Builder:
    def __init__(self, NB, DEPTH, dbg=(), phases=None):
        self.phases = phases
        self.NB = NB
        self.Lp = NB * 128
        self.SEQ = self.Lp - 128
        self.DEPTH = DEPTH
        self.dbg = dbg
        self.nc = bass.Bass("TRN2", target_bir_lowering=False)

    def din(self, name, shape, dt=F32):
        return V(self.nc.dram_tensor(name, list(shape), dt, kind="ExternalInput").ap())

    def dscr(self, name, shape, dt=F32):
        kind = "ExternalOutput" if name in self.dbg else "Internal"
        return V(self.nc.dram_tensor(name, list(shape), dt, kind=kind).ap())

    def build(self):
        nc, Lp, DEPTH = self.nc, self.Lp, self.DEPTH
        self.hT0 = self.din("hT0", [1024, Lp])
        self.w_in = self.din("w_in", [DEPTH, 1024, 7088])
        self.w_uq = self.din("mla_w_uq", [DEPTH, 256, 384])
        self.w_ukv = self.din("mla_w_ukv", [DEPTH, 128, 512])
        self.rw_w2 = self.din("rw_w2", [DEPTH, 64, 256])
        self.rw_a2 = self.din("rw_a2", [DEPTH, 64, 256])
        self.rw_g2 = self.din("rw_g2", [DEPTH, 128, 256])
        self.gla_a2 = self.din("gla_a2", [DEPTH, 16, 128])
        self.w_branch = self.din("w_branch", [DEPTH, 4, 256, 1024])
        self.w_out = self.din("w_out", [DEPTH, 1024, 1024])
        self.w_ffn_in = self.din("w_ffn_in", [DEPTH, 1024, 5632])
        self.w_ffn_out = self.din("w_ffn_out", [DEPTH, 2816, 1024])
        self.vecs_d = self.din("vecs", [DEPTH, 128, NV])
        self.gvec_d = self.din("gvec", [128, 8])
        self.rope_d = self.din("rope", [64, Lp])
        self.outT = V(nc.dram_tensor("outT", [1024, self.SEQ], F32, kind="ExternalOutput").ap())
        self.hT = self.dscr("hT", [1024, Lp])
        self.nT = self.dscr("nT", [1024, Lp], BF16)
        self.zT = self.dscr("zT", [NMIX, Lp])
        self.yT = self.dscr("yT", [1024, Lp], BF16)
        nch = (Lp + 511) // 512
        self.hres = [[Res() for _ in range(nch)] for _ in range(8)]

        with ExitStack() as es:
            self.fw = fw = FW(nc, es)
            self.consts(es)
            for k in range(8):
                fw.dma(self.hT[k * 128:(k + 1) * 128, :].withres(tuple(self.hres[k])),
                       self.hT0[k * 128:(k + 1) * 128, :])
            fw.barrier()
            print("sbuf remaining after consts:", nc.sbuf_bytes_remaining)
            for l in range(DEPTH):
                self.layer_setup(l)
                for nm, fn in (("A", self.phase_A), ("mla", self.mla), ("sb", self.sbatt), ("gla", self.gla),
                               ("rwkv", self.rwkv), ("C1", self.phase_C1), ("C2", self.phase_C2)):
                    if self.phases is not None and nm not in self.phases:
                        continue
                    with ExitStack() as e2, nc.named_scope("L%d_%s" % (l, nm)):
                        fn(l, e2)
                        fw.barrier()
            with ExitStack() as e2:
                self.phase_F(e2)
            fw.barrier()
            print("instructions:", fw.nins, {k: e.n for k, e in fw.engs.items()})
        return nc

    def hcell(self, k, t0, w):
        c0, c1 = t0 // 512, (t0 + w - 1) // 512
        res = tuple(self.hres[k][c] for c in range(c0, c1 + 1))
        return self.hT[k * 128:(k + 1) * 128, t0:t0 + w].withres(res if len(res) > 1 else res[0])

    def hall(self, t0, w):
        c0, c1 = t0 // 512, (t0 + w - 1) // 512
        res = tuple(self.hres[k][c] for k in range(8) for c in range(c0, c1 + 1))
        return self.hT.rr("(k p) t -> p k t", p=128)[:, :, t0:t0 + w].withres(res)

    def consts(self, es):
        fw = self.fw
        Lp = self.Lp
        self.ident = fw.sb([128, 128], F32, es)
        fw.memset(self.ident, 0.0, eng="pool")
        fw.aselect(self.ident, ALU.not_equal, 1.0, 0, [[-1, 128]], 1)
        self.ident_bf = fw.sb([128, 128], BF16, es)
        fw.copy(self.ident_bf, self.ident)
        self.ones_bf = fw.sb([128, 128], BF16, es)
        fw.memset(self.ones_bf, 1.0)
        self.zeros_bf = fw.sb([128, 128], BF16, es)
        fw.memset(self.zeros_bf, 0.0)
        padcol = fw.sb([128, 1], F32, es)
        fw.memset(padcol, 1.0, eng="pool")
        fw.aselect(padcol, ALU.is_ge, 0.0, -PADC, [[0, 1]], 1)
        self.ones_pad = fw.sb([128, 128], BF16, es)
        fw.ts(self.ones_pad, self.ones_bf, padcol[:, 0:1], None, op0=ALU.mult)

        def tri(cmp, base, pat, cm):
            t32 = fw.sb([128, 128], F32, es)
            fw.memset(t32, 1.0, eng="pool")
            fw.aselect(t32, cmp, 0.0, base, pat, cm)
            tb = fw.sb([128, 128], BF16, es)
            fw.copy(tb, t32)
            return t32, tb
        self.tri_incl32, self.tri_incl = tri(ALU.is_ge, 0, [[1, 128]], -1)
        self.tri_strict32, self.tri_strict = tri(ALU.is_gt, 0, [[1, 128]], -1)
        self.tri_sl32, self.tri_sl = tri(ALU.is_gt, 0, [[-1, 128]], 1)
        self.tri_ge32, self.uincl = tri(ALU.is_ge, 0, [[-1, 128]], 1)
        self.uincl_pad = fw.sb([128, 128], BF16, es)
        fw.ts(self.uincl_pad, self.uincl, padcol[:, 0:1], None, op0=ALU.mult)
        self.tri_incl4 = fw.sb([128, 4, 128], F32, es)
        for h in range(4):
            fw.copy(self.tri_incl4[:, h, :], self.tri_incl32)
        self.rm = fw.sb([128, 512], F32, es)
        fw.memset(self.rm, 1.0)
        for k in range(4):
            fw.memset(self.rm[:, k * 128:k * 128 + 1], 0.0)
        self.headmask = fw.sb([128, 4], F32, es)
        fw.memset(self.headmask, 1.0, eng="pool")
        fw.aselect(self.headmask, ALU.is_ge, 0.0, 0, [[-32, 4]], 1)
        fw.aselect(self.headmask, ALU.is_ge, 0.0, 31, [[32, 4]], -1)
        self.bdmask = fw.sb([128, 256], F32, es)
        fw.memset(self.bdmask, 1.0, eng="pool")
        bd3 = self.bdmask.rr("p (h c) -> p h c", h=4)
        fw.aselect(bd3, ALU.is_ge, 0.0, 0, [[-32, 4], [0, 64]], 1)
        fw.aselect(bd3, ALU.is_ge, 0.0, 31, [[32, 4], [0, 64]], -1)
        self.vecs = fw.sb([128, NV], F32, es)
        self.gvec = fw.sb([128, 8], F32, es)
        fw.dma(self.gvec, self.gvec_d)
        self.dcols = fw.sb([128, 32], F32, es)
        self._eps = {}
        for ev in (EPS, 1.0, 1e-24, 64e-5):
            t = fw.sb([128, 1], F32, es)
            fw.memset(t, ev)
            self._eps[ev] = t
        banks = [fw.ps([128, 512], F32, es) for _ in range(8)]
        self.banks = banks
        self.psA = banks[0:3]
        self.psR = Rot(banks[3:8])
        self.wst = Rot([fw.sb([128, 1408], F32, es) for _ in range(4)])
        self.wbf = Rot([fw.sb([128, 2816], BF16, es) for _ in range(5)])
        self.t32 = Rot([fw.sb([128, 512], F32, es) for _ in range(6)])
        self.tbf = Rot([fw.sb([128, 512], BF16, es) for _ in range(6)])

    def vc(self, c, n=1, rows=128):
        return self.vecs[0:rows, c:c + n]

    def layer_setup(self, l):
        fw = self.fw
        fw.dma(self.vecs, self.vecs_d[l])
        d = self.dcols
        fw.ts(d[:, 0:15], self.vecs[:, C_MURKV:C_MURKV + 15], -1.0, 1.0, op0=ALU.mult, op1=ALU.add)
        fw.ts(d[:, 15:19], self.vecs[:, C_KA:C_KA + 4], -1.0, 1.0, op0=ALU.mult, op1=ALU.add)
        fw.ts(d[:, 19:20], self.vecs[:, C_GAB:C_GAB + 1], -1.0, None, op0=ALU.mult)

    def load_w(self, src2d, kc, ow, prows=128):
        fw = self.fw
        wb = self.wbf.next()[0:prows, 0:kc * ow].rr("p (k o) -> p k o", k=kc)
        srcv = src2d.rr("(k p) o -> p k o", p=prows)
        kmax = max(1, 1408 // ow)
        k0 = 0
        while k0 < kc:
            kn = min(kmax, kc - k0)
            st = self.wst.next()[0:prows, 0:kn * ow].rr("p (k o) -> p k o", k=kn)
            fw.dma(st, srcv[:, k0:k0 + kn, :])
            fw.copy(wb[:, k0:k0 + kn, :], st, eng="pool")
            k0 += kn
        return wb

    def rstd_from_ss(self, out, ss_ps, n, eps, rows=128):
        fw = self.fw
        fw.act(out, ss_ps, AF.Ln, bias=self.epscol(eps)[0:rows], scale=1.0 / n)
        fw.act(out, out, AF.Exp, scale=-0.5)

    def epscol(self, eps):
        return self._eps[eps]

    def rmsnorm_chunk(self, hc, w, gcol0, gsrc, out_fn):
        fw = self.fw
        sq = self.sq8.next()
        fw.act(sq[:, :, :w], hc[:, :, :w], AF.Square)
        ps = self.psR.next()
        for k in range(8):
            fw.mm(ps[:, :w], self.ones_bf, sq[:, k, :w], start=(k == 0), stop=(k == 7))
        rstd = self.t32.next()
        self.rstd_from_ss(rstd[:, :w], ps[:, :w], 1024.0, EPS)
        for k in range(8):
            fw.stt(out_fn(k), hc[:, k, :w], gsrc[:, gcol0 + k:gcol0 + k + 1], rstd[:, :w], ALU.mult, ALU.mult)

    def phase_A(self, l, es):
        fw, Lp = self.fw, self.Lp
        n = fw.sb([128, 8, Lp], BF16, es)
        self.h8 = Rot([fw.sb([128, 8, 512], F32, es) for _ in range(2)])
        self.sq8 = Rot([fw.sb([128, 8, 512], BF16, es) for _ in range(2)])
        nTv = self.nT.rr("(k p) t -> p k t", p=128)
        for (t0, w) in chunks(0, Lp):
            hc = self.h8.next()
            fw.dma(hc[:, :, :w], self.hall(t0, w))
            self.rmsnorm_chunk(hc, w, C_NMIX, self.vecs, lambda k: n[:, k, t0:t0 + w])
            fw.dma(nTv[:, :, t0:t0 + w].fresh(), n[:, :, t0:t0 + w], q="sp")
        ocs = [(o, min(128, NMIX - o)) for o in range(0, NMIX, 128)]
        wl = self.w_in[l]
        cur = self.load_w(wl[:, 0:ocs[0][1]], 8, ocs[0][1])
        for i, (o0, ow) in enumerate(ocs):
            nxt = None
            if i + 1 < len(ocs):
                o1, ow1 = ocs[i + 1]
                nxt = self.load_w(wl[:, o1:o1 + ow1], 8, ow1)
            for (t0, w) in chunks(0, Lp):
                ps = self.psR.next()
                for k in range(8):
                    fw.mm(ps[:ow, :w], cur[:, k, :], n[:, k, t0:t0 + w], start=(k == 0), stop=(k == 7))
                ot = self.t32.next()
                fw.evac(ot[:ow, :w], ps[:ow, :w])
                fw.dma(self.zT[o0:o0 + ow, t0:t0 + w].fresh(), ot[:ow, :w], q="sp")
            cur = nxt

    def mla(self, l, es):
        fw, Lp, NB = self.fw, self.Lp, self.NB
        scale = 96.0 ** -0.5
        Q = [fw.sb([96, Lp], BF16, es) for _ in range(4)]
        K = [fw.sb([96, Lp], BF16, es) for _ in range(4)]
        Vt = fw.sb([128, NB, 256], BF16, es)
        ropeb = Rot([fw.sb([96, 2, 512], F32, es) for _ in range(2)])
        wuq_t = self.load_w(self.w_uq[l], 2, 384)
        wuq = fw.sb([128, 2, 384], BF16, es)
        fw.copy(wuq, wuq_t, eng="pool")
        wuq_s = fw.sb([128, 2, 384], BF16, es)
        fw.copy(wuq_s, wuq_t, eng="pool")
        src = self.w_uq[l].rr("(k p) (h c) -> p k h c", p=128, h=4)
        st2 = self.wst.next()[:, 0:256].rr("p (k c) -> p k c", k=2)
        for k in range(2):
            fw.dma(st2[:, k, 0:64].rr("p (h c) -> p h c", h=4), src[:, k, :, 80:96])
            fw.dma(st2[:, k, 64:128].rr("p (h c) -> p h c", h=4), src[:, k, :, 64:80])
        wv = wuq_s.rr("p k (h c) -> p k h c", h=4)
        for k in range(2):
            fw.copy(wv[:, k, :, 64:80], st2[:, k, 0:64].rr("p (h c) -> p h c", h=4), eng="pool")
            fw.copy(wv[:, k, :, 80:96], st2[:, k, 64:128].rr("p (h c) -> p h c", h=4), eng="pool")
        wukv = self.load_w(self.w_ukv[l], 1, 512)
        wkn = fw.sb([128, 4, 64], BF16, es)
        wvv = fw.sb([128, 256], BF16, es)
        wk4 = wukv[:, 0, :].rr("p (h c) -> p h c", h=4)
        fw.copy(wkn, wk4[:, :, 0:64], eng="pool")
        fw.copy(wvv.rr("p (h c) -> p h c", h=4), wk4[:, :, 64:128], eng="pool")
        cq2 = Rot([fw.sb([128, 2, 512], F32, es) for _ in range(2)])
        sq2 = Rot([fw.sb([128, 2, 512], BF16, es) for _ in range(2)])
        cqn = Rot([fw.sb([128, 2, 512], BF16, es) for _ in range(2)])
        kr2 = Rot([fw.sb([96, 2, 512], F32, es) for _ in range(2)])
        krr = Rot([fw.sb([96, 512], BF16, es) for _ in range(2)])
        zcq = self.zT[0:256, :].rr("(k p) t -> p k t", p=128)
        for (t0, w) in chunks(0, Lp):
            c = cq2.next()
            fw.dma(c[:, :, :w], zcq[:, 0:2, t0:t0 + w].fresh())
            ck = self.t32.next()
            fw.dma(ck[:, :w], self.zT[OFF_CKV:OFF_CKV + 128, t0:t0 + w].fresh())
            rc = ropeb.next()
            fw.dma(rc[64:96, 0, :w], self.rope_d[0:32, t0:t0 + w])
            fw.dma(rc[64:96, 1, :w], self.rope_d[32:64, t0:t0 + w])
            kr = kr2.next()
            fw.dma(kr[64:96, 0, :w], self.zT[OFF_KR:OFF_KR + 32, t0:t0 + w].fresh())
            fw.dma(kr[64:80, 1, :w], self.zT[OFF_KR + 16:OFF_KR + 32, t0:t0 + w].fresh())
            fw.dma(kr[80:96, 1, :w], self.zT[OFF_KR:OFF_KR + 16, t0:t0 + w].fresh())
            s = sq2.next()
            fw.act(s[:, :, :w], c[:, :, :w], AF.Square)
            ps = self.psR.next()
            for k in range(2):
                fw.mm(ps[:, :w], self.ones_bf, s[:, k, :w], start=(k == 0), stop=(k == 1))
            rstd = self.t32.next()
            self.rstd_from_ss(rstd[:, :w], ps[:, :w], 256.0, EPS)
            cn = cqn.next()
            for k in range(2):
                fw.stt(cn[:, k, :w], c[:, k, :w], self.vc(C_QN + k), rstd[:, :w], ALU.mult, ALU.mult)
            s2 = self.tbf.next()
            fw.act(s2[:, :w], ck[:, :w], AF.Square)
            ps = self.psR.next()
            fw.mm(ps[:, :w], self.ones_bf, s2[:, :w])
            rstd2 = self.t32.next()
            self.rstd_from_ss(rstd2[:, :w], ps[:, :w], 128.0, EPS)
            ckn = self.tbf.next()
            fw.stt(ckn[:, :w], ck[:, :w], self.vc(C_KVN), rstd2[:, :w], ALU.mult, ALU.mult)
            t1 = self.t32.next()
            t2 = self.t32.next()
            fw.tt(t1[64:96, :w], kr[64:96, 0, :w], rc[64:96, 0, :w], ALU.mult)
            fw.tt(t2[64:96, :w], kr[64:96, 1, :w], rc[64:96, 1, :w], ALU.mult)
            kq = krr.next()
            fw.tt(kq[64:96, :w], t1[64:96, :w], t2[64:96, :w], ALU.add)
            for h in range(4):
                p1 = self.psR.next()
                p2 = self.psR.next()
                for k in range(2):
                    fw.mm(p1[0:96, :w], wuq[:, k, h * 96:(h + 1) * 96], cn[:, k, :w], start=(k == 0), stop=(k == 1))
                for k in range(2):
                    fw.mm(p2[0:96, :w], wuq_s[:, k, h * 96:(h + 1) * 96], cn[:, k, :w], start=(k == 0), stop=(k == 1))
                fw.evac(Q[h][0:64, t0:t0 + w], p1[0:64, :w])
                a1 = self.t32.next()
                a2 = self.t32.next()
                fw.tt(a1[64:96, :w], p1[64:96, :w], rc[64:96, 0, :w], ALU.mult)
                fw.tt(a2[64:96, :w], p2[64:96, :w], rc[64:96, 1, :w], ALU.mult)
                fw.tt(Q[h][64:96, t0:t0 + w], a1[64:96, :w], a2[64:96, :w], ALU.add)
                p3 = self.psR.next()
                fw.mm(p3[0:64, :w], wkn[:, h, :], ckn[:, :w])
                fw.evac(K[h][0:64, t0:t0 + w], p3[0:64, :w])
                fw.copy(K[h][64:96, t0:t0 + w], kq[64:96, :w], eng="pool")
            for b in range(w // 128):
                p4 = self.psR.next()
                fw.mm(p4[:, 0:256], ckn[:, b * 128:(b + 1) * 128], wvv)
                fw.evac(Vt[:, (t0 // 128) + b, :], p4[:, 0:256])
        acc = [(self.banks[0], self.banks[1]), (self.banks[2], self.banks[3])]
        psS = Rot(self.banks[4:8])
        its = []
        grp = 0
        for h in range(4):
            for (t0, w) in chunks(0, Lp):
                kbmax = (t0 + w) // 128 - 1
                for kb in range(kbmax + 1):
                    its.append((h, t0, w, kb, kbmax, grp))
                grp += 1

        def s_stage(it):
            h, t0, w, kb, kbmax, g = it
            c0 = max(0, kb - t0 // 128) * 128
            sp = psS.next()
            fw.mm(sp[:, c0:w], K[h][:, kb * 128:(kb + 1) * 128], Q[h][:, t0 + c0:t0 + w])
            return sp
        sp_next = s_stage(its[0])
        for idx, it in enumerate(its):
            h, t0, w, kb, kbmax, g = it
            Ops, Dps = acc[g % 2]
            sp = sp_next
            if idx + 1 < len(its):
                sp_next = s_stage(its[idx + 1])
            i = kb - t0 // 128
            c0 = max(0, i) * 128
            e = self.tbf.next()
            fw.act(e[:, c0:w], sp[:, c0:w], AF.Exp, scale=scale)
            if i >= 0:
                fw.tt(e[:, c0:c0 + 128], e[:, c0:c0 + 128], self.tri_incl, ALU.mult, eng="pool")
            last = (kb == kbmax)
            fw.mm(Ops[0:64, c0:w], Vt[:, kb, h * 64:(h + 1) * 64], e[:, c0:w], start=(kb == 0), stop=last)
            fw.mm(Dps[0:64, c0:w], (self.ones_pad if kb == 0 else self.ones_bf)[:, 0:64], e[:, c0:w],
                  start=(kb == 0), stop=last)
            if last:
                den = self.t32.next()
                fw.ts(den[0:64, :w], Dps[0:64, :w], 1e-30, None, op0=ALU.add)
                fw.recip(den[0:64, :w], den[0:64, :w])
                yb = self.tbf.next()
                fw.tt(yb[0:64, :w], Ops[0:64, :w], den[0:64, :w], ALU.mult)
                fw.dma(self.yT[h * 64:(h + 1) * 64, t0:t0 + w].fresh(), yb[0:64, :w], q="sp")

    def sbatt(self, l, es):
        fw, Lp, NB = self.fw, self.Lp, self.NB
        Q = fw.sb([64, 4, Lp], BF16, es)
        K = fw.sb([64, 4, Lp], BF16, es)
        Vt = fw.sb([128, NB, 256], BF16, es)
        ld = Rot([fw.sb([64, 4, 512], F32, es) for _ in range(1)])
        ldv = Rot([fw.sb([128, 2, 512], F32, es) for _ in range(2)])
        pacc = fw.sb([128, 512], BF16, es)
        zq = self.zT[OFF_SB:OFF_SB + 256, :].rr("(h d) t -> d h t", h=4)
        zk = self.zT[OFF_SB + 256:OFF_SB + 512, :].rr("(h d) t -> d h t", h=4)
        zv = self.zT[OFF_SB + 512:OFF_SB + 768, :].rr("(k p) t -> p k t", p=128)
        for (t0, w) in chunks(0, Lp):
            a = ld.next()
            fw.dma(a[:, :, :w], zq[:, :, t0:t0 + w].fresh())
            fw.copy(Q[:, :, t0:t0 + w], a[:, :, :w], eng="act")
            b = ld.next()
            fw.dma(b[:, :, :w], zk[:, :, t0:t0 + w].fresh())
            fw.copy(K[:, :, t0:t0 + w], b[:, :, :w], eng="dve")
            v = ldv.next()
            fw.dma(v[:, :, :w], zv[:, :, t0:t0 + w].fresh())
            for bb in range(w // 128):
                for k in range(2):
                    pt = self.psR.next()
                    fw.transpose(pt[:, 0:128], v[:, k, bb * 128:(bb + 1) * 128], self.ident)
                    fw.evac(Vt[:, t0 // 128 + bb, k * 128:(k + 1) * 128], pt[:, 0:128])
        L32 = Rot([fw.sb([128, 512], F32, es) for _ in range(8)])
        Lbf = Rot([fw.sb([128, 512], BF16, es) for _ in range(8)])
        OpsL = [self.banks[0], self.banks[1]]
        psZ = Rot(self.banks[2:8])
        its = []
        grp = 0
        for h in range(4):
            for (t0, w) in chunks(0, Lp):
                kbmax = (t0 + w) // 128 - 1
                for kb in range(kbmax, -1, -1):
                    its.append((h, t0, w, kb, kbmax, grp))
                grp += 1

        def stage1(it):
            h, t0, w, kb, kbmax, g = it
            first = (kb == kbmax)
            i = kb - t0 // 128
            c0 = max(0, i) * 128
            if first:
                fw.memset(pacc[:, :w], 0.0)
            zp = psZ.next()
            fw.mm(zp[:, c0:w], K[:, h, kb * 128:(kb + 1) * 128], Q[:, h, t0 + c0:t0 + w])
            e1 = L32.next()
            fw.act(e1[:, c0:w], zp[:, c0:w], AF.Exp, scale=0.125)
            P = Lbf.next()
            fw.act(P[:, c0:w], e1[:, c0:w], AF.Ln, bias=self.epscol(1.0))
            zs = L32.next()
            fw.ts(zs[:, c0:w], zp[:, c0:w], 0.125, None, op0=ALU.mult)
            if i >= 0:
                fw.tt(P[:, c0:c0 + 128], P[:, c0:c0 + 128], self.tri_strict, ALU.mult, eng="pool")
            cp = psZ.next()
            fw.mm(cp[:, c0:w], self.uincl_pad if kb == 0 else self.uincl, P[:, c0:w], start=True, stop=first)
            if not first:
                fw.mm(cp[:, c0:w], self.ones_bf, pacc[:, c0:w], start=False, stop=True)
            lt = L32.next()
            fw.tt(lt[:, c0:w], zs[:, c0:w], cp[:, c0:w], ALU.subtract)
            if kb > 0:
                fw.tt(pacc[:, c0:w], pacc[:, c0:w], P[:, c0:w], ALU.add)
            return lt

        def stage2(it, lt):
            h, t0, w, kb, kbmax, g = it
            Ops = OpsL[g % 2]
            first = (kb == kbmax)
            i = kb - t0 // 128
            c0 = max(0, i) * 128
            A = Lbf.next()
            fw.act(A[:, c0:w], lt[:, c0:w], AF.Exp)
            if i >= 0:
                fw.tt(A[:, c0:c0 + 128], A[:, c0:c0 + 128], self.tri_strict, ALU.mult, eng="pool")
            if first and c0 > 0:
                fw.memset(A[:, 0:c0], 0.0, eng="pool")
            cc = 0 if first else c0
            fw.mm(Ops[0:64, cc:w], Vt[:, kb, h * 64:(h + 1) * 64], A[:, cc:w], start=first, stop=(kb == 0))
            if kb == 0:
                yb = Lbf.next()
                fw.evac(yb[0:64, :w], Ops[0:64, :w])
                fw.dma(self.yT[512 + h * 64:512 + (h + 1) * 64, t0:t0 + w].fresh(), yb[0:64, :w], q="sp")

        lt_next = stage1(its[0])
        for idx, it in enumerate(its):
            lt = lt_next
            if idx + 1 < len(its):
                lt_next = stage1(its[idx + 1])
            stage2(it, lt)

    def gla(self, l, es):
        fw, Lp = self.fw, self.Lp
        a2 = self.load_w(self.gla_a2[l], 1, 128, prows=16)
        Sbd = fw.sb([128, 256], F32, es)
        Sbd_bf = fw.sb([128, 256], BF16, es)
        fw.memset(Sbd, 0.0)
        fw.memset(Sbd_bf, 0.0)
        ldr = Rot([fw.sb([64, 4, 512], F32, es) for _ in range(2)])
        ldv = Rot([fw.sb([128, 2, 512], F32, es) for _ in range(2)])
        s128 = Rot([fw.sb([128, 128], F32, es) for _ in range(3)])
        b128 = Rot([fw.sb([128, 128], BF16, es) for _ in range(12)])
        vtok = Rot([fw.sb([128, 256], BF16, es) for _ in range(2)])
        yst = Rot([fw.sb([64, 4, 128], BF16, es) for _ in range(2)])
        L32 = Rot([fw.sb([128, 512], F32, es) for _ in range(14)])
        Lbf = Rot([fw.sb([128, 512], BF16, es) for _ in range(6)])
        og = OFF_GLA
        zr = self.zT[og + 528:og + 784, :].rr("(h d) t -> d h t", h=4)
        zv = self.zT[og + 256:og + 512, :].rr("(k p) t -> p k t", p=128)
        yv = self.yT[768:1024, :].rr("(h d) t -> d h t", h=4)
        for (t0, w) in chunks(0, Lp):
            al = L32.next()
            fw.dma(al[0:16, :w], self.zT[og + 512:og + 528, t0:t0 + w].fresh())
            alb = Lbf.next()
            fw.copy(alb[0:16, :w], al[0:16, :w])
            xp = self.psR.next()
            fw.mm(xp[:, :w], a2[:, 0, :], alb[0:16, :w])
            e = L32.next()
            fw.act(e[:, :w], xp[:, :w], AF.Exp, bias=self.dcols[:, 19:20], scale=-1.0)
            fw.act(e[:, :w], e[:, :w], AF.Ln, bias=self.epscol(1.0))
            fw.ts(e[:, :w], e[:, :w], -1.0 / 16.0, None, op0=ALU.mult)
            gam = L32.next()
            fw.scan(gam[:, :w], self.rm[:, :w], e[:, :w], 0.0, ALU.mult, ALU.add)
            eg = L32.next()
            fw.act(eg[:, :w], gam[:, :w], AF.Exp)
            eng = L32.next()
            fw.act(eng[:, :w], gam[:, :w], AF.Exp, scale=-1.0)
            q = L32.next()
            fw.dma(q[:, :w], self.zT[og:og + 128, t0:t0 + w].fresh())
            k = L32.next()
            fw.dma(k[:, :w], self.zT[og + 128:og + 256, t0:t0 + w].fresh())
            qt = Lbf.next()
            fw.stt(qt[:, :w], q[:, :w], 32.0 ** -0.5, eg[:, :w], ALU.mult, ALU.mult)
            kt = k
            fw.tt(kt[:, :w], k[:, :w], eng[:, :w], ALU.mult)
            ktb = Lbf.next()
            fw.copy(ktb[:, :w], kt[:, :w], eng="pool")
            v = ldv.next()
            fw.dma(v[:, :, :w], zv[:, :, t0:t0 + w].fresh())
            r = ldr.next()
            fw.dma(r[:, :, :w], zr[:, :, t0:t0 + w].fresh())
            fw.act(r[:, :, :w], r[:, :, :w], AF.Silu)
            for c in range(w // 128):
                cs = slice(c * 128, (c + 1) * 128)
                gend = eg[:, c * 128 + 127:c * 128 + 128]
                vt = vtok.next()
                for kk in range(2):
                    pt = self.psR.next()
                    fw.transpose(pt[:, 0:128], v[:, kk, cs], self.ident)
                    fw.evac(vt[:, kk * 128:(kk + 1) * 128], pt[:, 0:128])
                kh = s128.next()
                fw.ts(kh, kt[:, cs], gend, None, op0=ALU.mult)
                pt = self.psR.next()
                fw.transpose(pt[:, 0:128], kh, self.ident)
                kht = b128.next()
                fw.evac(kht, pt[:, 0:128])
                scp = self.psR.next()
                for h in range(4):
                    khh = b128.next()
                    fw.ts(khh, ktb[:, cs], self.headmask[:, h:h + 1], None, op0=ALU.mult, eng="pool")
                    fw.mm(scp[:, h * 128:(h + 1) * 128], khh, qt[:, cs])
                sc = self.tbf.next()
                fw.tt(sc, scp, self.tri_incl4.rr("p h t -> p (h t)"), ALU.mult)
                op_ = self.psR.next()
                for h in range(4):
                    fw.mm(op_[0:64, h * 128:(h + 1) * 128], Sbd_bf[:, h * 64:(h + 1) * 64], qt[:, cs],
                          start=True, stop=False)
                    fw.mm(op_[0:64, h * 128:(h + 1) * 128], vt[:, h * 64:(h + 1) * 64], sc[:, h * 128:(h + 1) * 128],
                          start=False, stop=True)
                kvp = self.psR.next()
                fw.mm(kvp[:, 0:256], kht, vt)
                tmp = self.t32.next()
                fw.tt(tmp[:, 0:256], kvp[:, 0:256], self.bdmask, ALU.mult)
                fw.stt(Sbd, Sbd, gend, tmp[:, 0:256], ALU.mult, ALU.add)
                fw.copy(Sbd_bf, Sbd, eng="pool")
                osb = self.t32.next()
                fw.evac(osb[0:64, :], op_[0:64, :])
                sq = self.tbf.next()
                fw.act(sq[0:64, :], op_[0:64, :], AF.Square)
                ssp = self.psR.next()
                fw.mm(ssp[0:64, :], self.ones_bf[0:64, 0:64], sq[0:64, :])
                rs = self.t32.next()
                self.rstd_from_ss(rs[0:64, :], ssp[0:64, :], 64.0, EPS, rows=64)
                fw.tt(osb[0:64, :], osb[0:64, :], rs[0:64, :], ALU.mult)
                yo = yst.next()
                for h in range(4):
                    fw.stt(yo[:, h, :], osb[0:64, h * 128:(h + 1) * 128], self.vc(C_GNORM + h, rows=64),
                           r[:, h, cs], ALU.mult, ALU.mult)
                fw.dma(yv[:, :, t0 + c * 128:t0 + (c + 1) * 128].fresh(), yo, q="sp")

    def rwkv(self, l, es):
        fw, Lp = self.fw, self.Lp
        w2 = fw.sb([64, 256], BF16, es)
        a2 = fw.sb([64, 256], BF16, es)
        g2 = fw.sb([128, 256], BF16, es)
        t = self.load_w(self.rw_w2[l], 1, 256, prows=64)
        fw.copy(w2, t[:, 0, :], eng="pool")
        t = self.load_w(self.rw_a2[l], 1, 256, prows=64)
        fw.copy(a2, t[:, 0, :], eng="pool")
        t = self.load_w(self.rw_g2[l], 1, 256, prows=128)
        fw.copy(g2, t[:, 0, :], eng="pool")
        T32 = [fw.sb([64, 64], F32, es) for _ in range(4)]
        Tbf = [fw.sb([64, 64], BF16, es) for _ in range(4)]
        for h in range(4):
            fw.memset(T32[h], 0.0)
            fw.memset(Tbf[h], 0.0)
        halo = Rot([fw.sb([128, 513], F32, es) for _ in range(4)])
        f64 = Rot([fw.sb([64, 512], F32, es) for _ in range(28)])
        h64 = Rot([fw.sb([64, 512], BF16, es) for _ in range(12)])
        s128 = Rot([fw.sb([128, 128], F32, es) for _ in range(10)])
        b128 = Rot([fw.sb([128, 128], BF16, es) for _ in range(104)])
        b64 = Rot([fw.sb([128, 64], BF16, es) for _ in range(32)])
        oz = OFF_RW
        ps8 = Rot(self.banks)
        c_decay = -math.exp(-0.5)
        wlb_r = Rot([fw.sb([64, 512], BF16, es) for _ in range(2)])
        alb_r = Rot([fw.sb([64, 512], BF16, es) for _ in range(2)])
        glb_r = Rot([fw.sb([128, 512], BF16, es) for _ in range(2)])

        def load_shift(rows0, nrows, t0, w, mucol, omucol, out):
            hl = halo.next()
            if t0 == 0:
                fw.memset(hl[0:nrows, 0:1], 0.0)
                fw.dma(hl[0:nrows, 1:1 + w], self.zT[rows0:rows0 + nrows, 0:w].fresh())
            else:
                fw.dma(hl[0:nrows, 0:1 + w], self.zT[rows0:rows0 + nrows, t0 - 1:t0 + w].fresh())
            tmp = f64.next() if nrows <= 64 else self.t32.next()
            fw.ts(tmp[0:nrows, :w], hl[0:nrows, 0:w], mucol, None, op0=ALU.mult)
            fw.stt(out, hl[0:nrows, 1:1 + w], omucol, tmp[0:nrows, :w], ALU.mult, ALU.add)

        for (t0, w) in chunks(0, Lp):
            nck = w // 128
            wl = f64.next()
            load_shift(oz + 768, 64, t0, w, self.vc(C_MUWL, rows=64), self.dcols[0:64, 12:13], wl[:, :w])
            wlb = wlb_r.next()
            fw.act(wlb[:, :w], wl[:, :w], AF.Tanh)
            al = f64.next()
            load_shift(oz + 832, 64, t0, w, self.vc(C_MUAL, rows=64), self.dcols[0:64, 13:14], al[:, :w])
            alb = alb_r.next()
            fw.copy(alb[:, :w], al[:, :w])
            gl = self.t32.next()
            load_shift(oz + 896, 128, t0, w, self.vc(C_MUGL), self.dcols[:, 14:15], gl[:, :w])
            glb = glb_r.next()
            fw.act(glb[:, :w], gl[:, :w], AF.Sigmoid)
            for h in range(4):
                r32, k32, v32 = f64.next(), f64.next(), f64.next()
                load_shift(oz + h * 64, 64, t0, w, self.vc(C_MURKV + h, rows=64), self.dcols[0:64, h:h + 1], r32[:, :w])
                load_shift(oz + 256 + h * 64, 64, t0, w, self.vc(C_MURKV + 4 + h, rows=64),
                           self.dcols[0:64, 4 + h:5 + h], k32[:, :w])
                load_shift(oz + 512 + h * 64, 64, t0, w, self.vc(C_MURKV + 8 + h, rows=64),
                           self.dcols[0:64, 8 + h:9 + h], v32[:, :w])
                pw = ps8.next()
                fw.mm(pw[0:64, :w], w2[:, h * 64:(h + 1) * 64], wlb[:, :w])
                logw = f64.next()
                fw.act(logw[:, :w], pw[0:64, :w], AF.Sigmoid, bias=self.vc(C_W0 + h, rows=64))
                fw.ts(logw[:, :w], logw[:, :w], c_decay, None, op0=ALU.mult)
                pa = ps8.next()
                fw.mm(pa[0:64, :w], a2[:, h * 64:(h + 1) * 64], alb[:, :w])
                alpha = f64.next()
                fw.act(alpha[:, :w], pa[0:64, :w], AF.Sigmoid, bias=self.vc(C_A0 + h, rows=64))
                pg = ps8.next()
                fw.mm(pg[0:64, :w], g2[:, h * 64:(h + 1) * 64], glb[:, :w])
                g32 = f64.next()
                fw.evac(g32[:, :w], pg[0:64, :w])
                kkr = f64.next()
                fw.ts(kkr[:, :w], k32[:, :w], self.vc(C_KK + h, rows=64), None, op0=ALU.mult)
                sqk = h64.next()
                fw.act(sqk[:, :w], kkr[:, :w], AF.Square)
                pss = ps8.next()
                fw.mm(pss[0:64, :w], self.ones_bf[0:64, 0:64], sqk[:, :w])
                rn = f64.next()
                fw.act(rn[:, :w], pss[0:64, :w], AF.Ln, bias=self.epscol(1e-24)[0:64])
                fw.act(rn[:, :w], rn[:, :w], AF.Exp, scale=-0.5)
                kk = kkr
                fw.tt(kk[:, :w], kkr[:, :w], rn[:, :w], ALU.mult)
                kmod = f64.next()
                fw.ts(kmod[:, :w], alpha[:, :w], self.vc(C_KA + h, rows=64), self.dcols[0:64, 15 + h:16 + h],
                      op0=ALU.mult, op1=ALU.add)
                fw.tt(kmod[:, :w], kmod[:, :w], k32[:, :w], ALU.mult)
                gam = f64.next()
                fw.scan(gam[:, :w], self.rm[0:64, :w], logw[:, :w], 0.0, ALU.mult, ALU.add)
                eg = f64.next()
                fw.act(eg[:, :w], gam[:, :w], AF.Exp)
                eng = f64.next()
                fw.act(eng[:, :w], gam[:, :w], AF.Exp, scale=-1.0)
                egm = f64.next()
                fw.tt(egm[:, :w], gam[:, :w], logw[:, :w], ALU.subtract)
                fw.act(egm[:, :w], egm[:, :w], AF.Exp)
                rt = h64.next()
                fw.tt(rt[:, :w], r32[:, :w], eg[:, :w], ALU.mult)
                kt32 = f64.next()
                fw.tt(kt32[:, :w], kmod[:, :w], eng[:, :w], ALU.mult)
                ktb = h64.next()
                fw.copy(ktb[:, :w], kt32[:, :w], eng="pool")
                bt32 = f64.next()
                fw.tt(bt32[:, :w], kk[:, :w], alpha[:, :w], ALU.mult)
                fw.tt(bt32[:, :w], bt32[:, :w], eng[:, :w], ALU.mult)
                btb = h64.next()
                fw.copy(btb[:, :w], bt32[:, :w], eng="pool")
                atb = h64.next()
                fw.stt(atb[:, :w], kk[:, :w], -1.0, egm[:, :w], ALU.mult, ALU.mult)
                rk = h64.next()
                fw.stt(rk[:, :w], r32[:, :w], self.vc(C_RK + h, rows=64), kmod[:, :w], ALU.mult, ALU.mult)
                pb = ps8.next()
                fw.mm(pb[0:64, :w], self.ones_bf[0:64, 0:64], rk[:, :w])
                bon = f64.next()
                fw.tt(bon[:, :w], pb[0:64, :w], v32[:, :w], ALU.mult)
                y32 = f64.next()
                def indep(c, o):
                    cs = slice(c * 128, (c + 1) * 128)
                    gend = eg[:, c * 128 + 127:c * 128 + 128]
                    o["cs"], o["gend"] = cs, gend
                    pt = ps8.next()
                    fw.transpose(pt[:, 0:64], v32[:, cs], self.ident[0:64, 0:64])
                    kh = s128.next()
                    fw.ts(kh[0:64, :], kt32[:, cs], gend, None, op0=ALU.mult)
                    bh = s128.next()
                    fw.ts(bh[0:64, :], bt32[:, cs], gend, None, op0=ALU.mult)
                    yield
                    vt = b64.next()
                    fw.evac(vt, pt[:, 0:64])
                    pt2 = ps8.next()
                    fw.transpose(pt2[:, 0:64], kh[0:64, :], self.ident[0:64, 0:64])
                    fw.transpose(pt2[:, 64:128], bh[0:64, :], self.ident[0:64, 0:64])
                    yield
                    kht = b64.next()
                    fw.evac(kht, pt2[:, 0:64])
                    bht = b64.next()
                    fw.evac(bht, pt2[:, 64:128])
                    o["vt"], o["kht"], o["bht"] = vt, kht, bht
                    pn = ps8.next()
                    fw.mm(pn[:, 0:128], btb[:, cs], atb[:, cs])
                    fw.mm(pn[:, 128:256], atb[:, cs], btb[:, cs])
                    fw.mm(pn[:, 256:384], ktb[:, cs], atb[:, cs])
                    pr = ps8.next()
                    fw.mm(pr[:, 0:128], ktb[:, cs], rt[:, cs])
                    fw.mm(pr[:, 128:256], btb[:, cs], rt[:, cs])
                    yield
                    N = b128.next()
                    fw.tt(N, pn[:, 0:128], self.tri_strict32, ALU.mult)
                    NT = b128.next()
                    fw.tt(NT, pn[:, 128:256], self.tri_sl32, ALU.mult)
                    AakT = b128.next()
                    fw.tt(AakT, pn[:, 256:384], self.tri_strict32, ALU.mult)
                    ArkT = b128.next()
                    fw.tt(ArkT, pr[:, 0:128], self.tri_incl32, ALU.mult)
                    ArbT = b128.next()
                    fw.tt(ArbT, pr[:, 128:256], self.tri_incl32, ALU.mult)
                    o["AakT"], o["ArkT"], o["ArbT"] = AakT, ArkT, ArbT
                    P = b128.next()
                    fw.tt(P, N, self.ident_bf, ALU.add)
                    yield
                    for lev in range(6):
                        pq = ps8.next()
                        fw.mm(pq[:, 128:256], N, NT)
                        if lev < 5:
                            fw.mm(pq[:, 0:128], NT, N)
                        yield
                        if lev < 5:
                            N2 = b128.next()
                            fw.evac(N2, pq[:, 0:128])
                        NT2 = b128.next()
                        fw.evac(NT2, pq[:, 128:256])
                        yield
                        pp = ps8.next()
                        fw.mm(pp[:, 0:128], NT2, P)
                        yield
                        P2 = b128.next()
                        fw.tt(P2, P, pp[:, 0:128], ALU.add)
                        P = P2
                        NT = NT2
                        if lev < 5:
                            N = N2
                    o["MT"] = P

                outs = [dict() for _ in range(nck)]
                alive = [indep(c, outs[c]) for c in range(nck)]
                while alive:
                    for gnr in list(alive):
                        try:
                            next(gnr)
                        except StopIteration:
                            alive.remove(gnr)
                for c in range(nck):
                    o = outs[c]
                    cs, gend, vt, kht, bht = o["cs"], o["gend"], o["vt"], o["kht"], o["bht"]
                    AakT, ArkT, ArbT, MT = o["AakT"], o["ArkT"], o["ArbT"], o["MT"]
                    p0 = ps8.next()
                    fw.mm(p0[:, 0:64], atb[:, cs], Tbf[h], start=True, stop=False)
                    fw.mm(p0[:, 0:64], AakT, vt, start=False, stop=True)
                    rhs0 = b64.next()
                    fw.evac(rhs0, p0[:, 0:64])
                    pu = ps8.next()
                    fw.mm(pu[:, 0:64], MT, rhs0)
                    U = b64.next()
                    fw.evac(U, pu[:, 0:64])
                    py = ps8.next()
                    fw.mm(py[0:64, 0:128], Tbf[h], rt[:, cs], start=True, stop=False)
                    fw.mm(py[0:64, 0:128], vt, ArkT, start=False, stop=False)
                    fw.mm(py[0:64, 0:128], U, ArbT, start=False, stop=True)
                    fw.evac(y32[:, cs], py[0:64, 0:128])
                    pT = ps8.next()
                    fw.mm(pT[0:64, 0:64], kht, vt, start=True, stop=False)
                    fw.mm(pT[0:64, 0:64], bht, U, start=False, stop=True)
                    fw.stt(T32[h], T32[h], gend, pT[0:64, 0:64], ALU.mult, ALU.add)
                    fw.copy(Tbf[h], T32[h], eng="pool")
                ybf = h64.next()
                fw.copy(ybf[:, :w], y32[:, :w], eng="pool")
                pm = ps8.next()
                fw.mm(pm[0:64, :w], self.ones_bf[0:64, 0:64], ybf[:, :w])
                yc = f64.next()
                fw.stt(yc[:, :w], pm[0:64, :w], -1.0 / 64.0, y32[:, :w], ALU.mult, ALU.add)
                sq = h64.next()
                fw.act(sq[:, :w], yc[:, :w], AF.Square)
                pv = ps8.next()
                fw.mm(pv[0:64, :w], self.ones_bf[0:64, 0:64], sq[:, :w])
                rs = f64.next()
                self.rstd_from_ss(rs[:, :w], pv[0:64, :w], 64.0, 64e-5, rows=64)
                fw.tt(yc[:, :w], yc[:, :w], rs[:, :w], ALU.mult)
                fw.ts(yc[:, :w], yc[:, :w], self.vc(C_LNW + h, rows=64), self.vc(C_LNB + h, rows=64),
                      op0=ALU.mult, op1=ALU.add)
                fw.tt(yc[:, :w], yc[:, :w], bon[:, :w], ALU.add)
                yo = h64.next()
                fw.tt(yo[:, :w], yc[:, :w], g32[:, :w], ALU.mult)
                fw.dma(self.yT[256 + h * 64:256 + (h + 1) * 64, t0:t0 + w].fresh(), yo[:, :w], q="sp")

    def phase_C1(self, l, es):
        fw, Lp = self.fw, self.Lp
        TS = 2112
        n = fw.sb([128, 8, TS], BF16, es)
        y = fw.sb([128, 8, TS], BF16, es)
        mg = fw.sb([128, 8, TS], BF16, es)
        acc = fw.sb([128, TS], F32, es)
        nTv = self.nT.rr("(k p) t -> p k t", p=128)
        yTv = self.yT.rr("(k p) t -> p k t", p=128)
        wl = self.w_in[l]
        for (s0, sw) in chunks(0, Lp, TS):
            for (t0, w) in chunks(0, sw):
                fw.dma(n[:, :, t0:t0 + w], nTv[:, :, s0 + t0:s0 + t0 + w].fresh())
                fw.dma(y[:, :, t0:t0 + w], yTv[:, :, s0 + t0:s0 + t0 + w].fresh())
            jobs = [(d, m) for d in range(8) for m in range(4)]

            def loadj(j):
                d, m = j
                c0 = OFF_GATE + m * 1024 + d * 128
                return (self.load_w(wl[:, c0:c0 + 128], 8, 128),
                        self.load_w(self.w_branch[l][m][:, d * 128:(d + 1) * 128], 2, 128))
            curw = loadj(jobs[0])
            for ji, (d, m) in enumerate(jobs):
                    nxtw = loadj(jobs[ji + 1]) if ji + 1 < len(jobs) else None
                    wg, wb = curw
                    curw = nxtw
                    for (t0, w) in chunks(0, sw):
                        pg = self.psR.next()
                        for k in range(8):
                            fw.mm(pg[:, :w], wg[:, k, :], n[:, k, t0:t0 + w], start=(k == 0), stop=(k == 7))
                        pb = self.psR.next()
                        for k in range(2):
                            fw.mm(pb[:, :w], wb[:, k, :], y[:, 2 * m + k, t0:t0 + w], start=(k == 0), stop=(k == 1))
                        gt = self.t32.next()
                        fw.act(gt[:, :w], pg[:, :w], AF.Sigmoid, bias=self.vc(C_GATEB + m * 8 + d))
                        if m == 0:
                            fw.tt(acc[:, t0:t0 + w], gt[:, :w], pb[:, :w], ALU.mult)
                        else:
                            fw.tt(gt[:, :w], gt[:, :w], pb[:, :w], ALU.mult)
                            if m < 3:
                                fw.tt(acc[:, t0:t0 + w], acc[:, t0:t0 + w], gt[:, :w], ALU.add, eng="pool")
                            else:
                                fw.tt(mg[:, d, t0:t0 + w], acc[:, t0:t0 + w], gt[:, :w], ALU.add, eng="pool")
            curo = self.load_w(self.w_out[l][:, 0:128], 8, 128)
            for d in range(8):
                wo = curo
                if d + 1 < 8:
                    curo = self.load_w(self.w_out[l][:, (d + 1) * 128:(d + 2) * 128], 8, 128)
                for (t0, w) in chunks(0, sw):
                    g0 = s0 + t0
                    po = self.psR.next()
                    for k in range(8):
                        fw.mm(po[:, :w], wo[:, k, :], mg[:, k, t0:t0 + w], start=(k == 0), stop=(k == 7))
                    lo = PADC if g0 == 0 else 0
                    hc = self.t32.next()
                    cell = self.hcell(d, g0 + lo, w - lo)
                    fw.dma(hc[:, lo:w], cell)
                    fw.tt(hc[:, lo:w], hc[:, lo:w], po[:, lo:w], ALU.add)
                    fw.dma(cell, hc[:, lo:w], q="sp")

    def phase_C2(self, l, es):
        fw, Lp = self.fw, self.Lp
        TS = 1536
        n2 = fw.sb([128, 8, TS], BF16, es)
        g = fw.sb([128, 22, TS], BF16, es)
        carry = fw.sb([128, 22, 2], F32, es)
        fw.memset(carry, 0.0)
        self.h8 = Rot([fw.sb([128, 8, 512], F32, es) for _ in range(1)])
        self.sq8 = Rot([fw.sb([128, 8, 512], BF16, es) for _ in range(1)])
        asb = Rot([fw.sb([128, 514], F32, es) for _ in range(3)])
        wfi = self.w_ffn_in[l]
        for (s0, sw) in chunks(0, Lp, TS):
            for (t0, w) in chunks(0, sw):
                hc = self.h8.next()
                fw.dma(hc[:, :, :w], self.hall(s0 + t0, w))
                self.rmsnorm_chunk(hc, w, C_NFFN, self.vecs, lambda k: n2[:, k, t0:t0 + w])
            def loadf(fc):
                return (self.load_w(wfi[:, fc * 128:(fc + 1) * 128], 8, 128),
                        self.load_w(wfi[:, DFF + fc * 128:DFF + (fc + 1) * 128], 8, 128))
            curw = loadf(0)
            for fc in range(22):
                wa, wu = curw
                if fc + 1 < 22:
                    curw = loadf(fc + 1)
                for (t0, w) in chunks(0, sw):
                    pa = self.psR.next()
                    for k in range(8):
                        fw.mm(pa[:, :w], wa[:, k, :], n2[:, k, t0:t0 + w], start=(k == 0), stop=(k == 7))
                    pu = self.psR.next()
                    for k in range(8):
                        fw.mm(pu[:, :w], wu[:, k, :], n2[:, k, t0:t0 + w], start=(k == 0), stop=(k == 7))
                    a = asb.next()
                    fw.copy(a[:, 0:2], carry[:, fc, :], eng="pool")
                    fw.copy(a[:, 2:2 + w], pa[:, :w], eng="act")
                    fw.copy(carry[:, fc, :], a[:, w:w + 2], eng="pool")
                    c = self.t32.next()
                    fw.ts(c[:, :w], a[:, 0:w], self.vc(C_CONVW + fc), self.vc(C_CONVB + fc), op0=ALU.mult, op1=ALU.add)
                    fw.stt(c[:, :w], a[:, 1:1 + w], self.vc(C_CONVW + 22 + fc), c[:, :w], ALU.mult, ALU.add)
                    fw.stt(c[:, :w], a[:, 2:2 + w], self.vc(C_CONVW + 44 + fc), c[:, :w], ALU.mult, ALU.add)
                    fw.act(c[:, :w], c[:, :w], AF.Silu)
                    fw.tt(g[:, fc, t0:t0 + w], c[:, :w], pu[:, :w], ALU.mult)
            curo = self.load_w(self.w_ffn_out[l][:, 0:128], 22, 128)
            for d in range(8):
                wo = curo
                if d + 1 < 8:
                    curo = self.load_w(self.w_ffn_out[l][:, (d + 1) * 128:(d + 2) * 128], 22, 128)
                for (t0, w) in chunks(0, sw):
                    g0 = s0 + t0
                    po = self.psR.next()
                    for k in range(22):
                        fw.mm(po[:, :w], wo[:, k, :], g[:, k, t0:t0 + w], start=(k == 0), stop=(k == 21))
                    lo = PADC if g0 == 0 else 0
                    hc = self.t32.next()
                    cell = self.hcell(d, g0 + lo, w - lo)
                    fw.dma(hc[:, lo:w], cell)
                    fw.tt(hc[:, lo:w], hc[:, lo:w], po[:, lo:w], ALU.add)
                    fw.dma(cell, hc[:, lo:w], q="sp")

    def phase_F(self, es):
        fw, Lp = self.fw, self.Lp
        self.h8 = Rot([fw.sb([128, 8, 512], F32, es) for _ in range(2)])
        self.sq8 = Rot([fw.sb([128, 8, 512], BF16, es) for _ in range(2)])
        o8 = Rot([fw.sb([128, 8, 512], F32, es) for _ in range(2)])
        ov = self.outT.rr("(k p) t -> p k t", p=128)
        for (t0, w) in chunks(0, Lp):
            hc = self.h8.next()
            fw.dma(hc[:, :, :w], self.hall(t0, w))
            o = o8.next()
            self.rmsnorm_chunk(hc, w, 0, self.gvec, lambda k: o[:, k, :w])
            lo = 128 if t0 == 0 else 0
            if w - lo > 0:
                fw.dma(ov[:, :, t0 + lo - 128:t0 + w - 128].fresh(), o[:, :, lo:w], q="sp")


def _cols(v, p=128):
    a = np.asarray(v, np.float32).reshape(-1, p).T
    if p < 128:
        a = np.concatenate([a, np.zeros((128 - p, a.shape[1]), np.float32)], 0)
    return a


def pack_vecs(inp, l):
    out = np.zeros((128, NV), np.float32)

    def put(c, a):
        out[:, c:c + a.shape[1]] = a
    put(C_NMIX, _cols(inp["norm_mix"][l]))
    put(C_NFFN, _cols(inp["norm_ffn"][l]))
    put(C_GATEB, _cols(inp["gate_b"][l].reshape(-1)))
    put(C_CONVW, _cols(inp["ffn_conv_w"][l].reshape(-1)))
    put(C_CONVB, _cols(inp["ffn_conv_b"][l]))
    put(C_QN, _cols(inp["mla_q_norm"][l]))
    put(C_KVN, _cols(inp["mla_kv_norm"][l]))
    mu = inp["rw_mu"][l]
    put(C_MURKV, _cols(mu[0:768], 64))
    put(C_MUWL, _cols(mu[768:832], 64))
    put(C_MUAL, _cols(mu[832:896], 64))
    put(C_MUGL, _cols(mu[896:1024]))
    put(C_W0, _cols(inp["rw_w0"][l], 64))
    put(C_A0, _cols(inp["rw_a0"][l], 64))
    put(C_KK, _cols(inp["rw_k_k"][l], 64))
    put(C_KA, _cols(inp["rw_k_a"][l], 64))
    put(C_RK, _cols(inp["rw_r_k"][l].reshape(-1), 64))
    put(C_LNW, _cols(inp["rw_ln_w"][l], 64))
    put(C_LNB, _cols(inp["rw_ln_b"][l], 64))
    put(C_GAB, _cols(inp["gla_a_b"][l]))
    put(C_GNORM, _cols(inp["gla_norm"][l], 64))
    return out


def rope_table(Lp):
    half = 16
    freqs = (np.float32(10000.0) ** (-np.arange(half, dtype=np.float32) / np.float32(half))).astype(np.float32)
    pos = (np.arange(Lp) - PADC).astype(np.float32)
    ang = (pos[None, :] * freqs[:, None]).astype(np.float32)
    c, s = np.cos(ang).astype(np.float32), np.sin(ang).astype(np.float32)
    return np.concatenate([c, c, -s, s], 0).astype(np.float32)


_CACHE = {}


def run(inputs, NB, DEPTH, dbg=(), n_cores=8, phases=None):
    key = (NB, DEPTH, tuple(dbg), phases)
    if key not in _CACHE:
        _CACHE[key] = Builder(NB, DEPTH, dbg, phases).build()
    nc = _CACHE[key]
    Lp = NB * 128
    x = np.asarray(inputs["x"], np.float32)
    B = x.shape[0]
    meta = np.asarray(inputs["meta_tokens"], np.float32)
    shared = {
        "w_in": np.ascontiguousarray(inputs["w_in"][:DEPTH], np.float32),
        "mla_w_uq": np.ascontiguousarray(inputs["mla_w_uq"][:DEPTH], np.float32),
        "mla_w_ukv": np.ascontiguousarray(inputs["mla_w_ukv"][:DEPTH], np.float32),
        "rw_w2": np.ascontiguousarray(inputs["rw_w2"][:DEPTH], np.float32),
        "rw_a2": np.ascontiguousarray(inputs["rw_a2"][:DEPTH], np.float32),
        "rw_g2": np.ascontiguousarray(inputs["rw_g2"][:DEPTH], np.float32),
        "gla_a2": np.ascontiguousarray(inputs["gla_a2"][:DEPTH], np.float32),
        "w_branch": np.ascontiguousarray(inputs["w_branch"][:DEPTH], np.float32),
        "w_out": np.ascontiguousarray(inputs["w_out"][:DEPTH], np.float32),
        "w_ffn_in": np.ascontiguousarray(inputs["w_ffn_in"][:DEPTH], np.float32),
        "w_ffn_out": np.ascontiguousarray(inputs["w_ffn_out"][:DEPTH], np.float32),
        "vecs": np.stack([pack_vecs(inputs, l) for l in range(DEPTH)], 0),
        "gvec": _cols(inputs["norm_final"]),
        "rope": rope_table(Lp),
    }
    in_maps = []
    for c in range(n_cores):
        b = c % B
        hT0 = np.zeros((1024, Lp), np.float32)
        hT0[:, PADC:PADC + 16] = meta.T
        hT0[:, 128:] = x[b].T
        m = dict(shared)
        m["hT0"] = hT0
        in_maps.append(m)
    res = run_bass_kernel_spmd(nc, in_maps, core_ids=list(range(n_cores)))
    return res.results


def kernel(**inputs):
    x = np.asarray(inputs["x"])
    B, SEQ, D = x.shape
    NB = (SEQ + 128) // 128
    results = run(inputs, NB, 4)
    out = np.stack([np.ascontiguousarray(results[b]["outT"].T) for b in range(B)], 0)
    return out.astype(np.float32)
```
